# Optimizing a Trainium2 kernel written in Bass

```python
import jax, jax.numpy as jnp
from jax import lax
import numpy as np

D_MODEL = 1024
BATCH = 4
SEQ = 4096
DEPTH = 2

D_FF = 2816
GDN_HEADS = 8
GDN_DK = 128
GDN_DV = 128
GDN_CONV = 4
CHUNK = 64
CNV_CH = 1024
CNV_K = 31
W_Q = GDN_HEADS * GDN_DK
W_K = GDN_HEADS * GDN_DK
W_V = GDN_HEADS * GDN_DV
W_Z = GDN_HEADS * GDN_DV
W_BETA = GDN_HEADS
W_A = GDN_HEADS
W_GLU = 2 * CNV_CH
W_GATE = 2 * D_MODEL
SPLITS = [int(s) for s in np.cumsum([W_Q + W_K + W_V, W_Z, W_BETA, W_A, W_GLU])]
P_IN = W_Q + W_K + W_V + W_Z + W_BETA + W_A + W_GLU + W_GATE
RMS_EPS = 1e-6
LN_EPS = 1e-5

kernel_name = "hybrid_gdn_conformer_macaron_sandwich"


def rmsnorm(x, w):
    xf = x.astype(jnp.float32)
    y = xf * lax.rsqrt(jnp.mean(xf * xf, axis=-1, keepdims=True) + RMS_EPS)
    return (y * w.astype(jnp.float32)).astype(x.dtype)


def layernorm(x, g, b):
    xf = x.astype(jnp.float32)
    mu = jnp.mean(xf, axis=-1, keepdims=True)
    var = jnp.mean(jnp.square(xf - mu), axis=-1, keepdims=True)
    y = (xf - mu) * lax.rsqrt(var + LN_EPS)
    return (y * g.astype(jnp.float32) + b.astype(jnp.float32)).astype(x.dtype)


def l2norm(x):
    return x * lax.rsqrt(jnp.sum(x * x, axis=-1, keepdims=True) + 1e-6)


def causal_dwconv(x, w):
    k, c = w.shape
    return lax.conv_general_dilated(
        x, w[:, None, :].astype(x.dtype), window_strides=(1,), padding=[(k - 1, 0)],
        dimension_numbers=("NWC", "WIO", "NWC"), feature_group_count=c)


def swiglu_ffn(h, w_in, w_out):
    gate, up = jnp.split(h @ w_in, 2, axis=-1)
    return (jax.nn.silu(gate) * up) @ w_out


def chunk_gated_delta_rule(q, k, v, g, beta):
    b, l, h, dk = q.shape
    dv = v.shape[-1]
    n = l // CHUNK

    def to_chunks(t):
        return t.reshape(b, n, CHUNK, h, -1).transpose(0, 3, 1, 2, 4)

    q, k, v = to_chunks(q), to_chunks(k), to_chunks(v)
    g = g.reshape(b, n, CHUNK, h).transpose(0, 3, 1, 2)
    beta = beta.reshape(b, n, CHUNK, h).transpose(0, 3, 1, 2)
    G = jnp.cumsum(g, axis=-1)
    causal = jnp.tril(jnp.ones((CHUNK, CHUNK), dtype=bool))
    strict = jnp.tril(jnp.ones((CHUNK, CHUNK), dtype=bool), k=-1)
    diff = G[..., :, None] - G[..., None, :]
    decay = jnp.where(causal, jnp.exp(jnp.where(causal, diff, 0.0)), 0.0)
    kk = jnp.einsum("bhncd,bhnsd->bhncs", k, k)
    a_mat = jnp.where(strict, kk * decay * beta[..., :, None], 0.0)
    lhs = jnp.eye(CHUNK, dtype=jnp.float32) + a_mat
    rhs = jnp.concatenate([v * beta[..., None], k * (beta * jnp.exp(G))[..., None]], axis=-1)
    sol = lax.linalg.triangular_solve(lhs, rhs, left_side=True, lower=True, unit_diagonal=True)
    u, w = sol[..., :dv], sol[..., dv:]
    qk = jnp.einsum("bhncd,bhnsd->bhncs", q, k) * decay
    q_dec = q * jnp.exp(G)[..., None]
    k_dec = k * jnp.exp(G[..., -1:] - G)[..., None]
    chunk_decay = jnp.exp(G[..., -1])

    xs = tuple(jnp.moveaxis(t, 2, 0) for t in (q_dec, k_dec, u, w, qk, chunk_decay))

    def step(state, inp):
        q_c, k_c, u_c, w_c, qk_c, d_c = inp
        v_new = u_c - jnp.einsum("bhck,bhkv->bhcv", w_c, state)
        o_c = jnp.einsum("bhck,bhkv->bhcv", q_c, state) + jnp.einsum("bhcs,bhsv->bhcv", qk_c, v_new)
        state = state * d_c[..., None, None] + jnp.einsum("bhck,bhcv->bhkv", k_c, v_new)
        return state, o_c

    s0 = jnp.zeros((b, h, dk, dv), jnp.float32)
    _, o = lax.scan(step, s0, xs)
    return o.transpose(1, 0, 3, 2, 4).reshape(b, l, h, dv)


def gated_deltanet(qkv, z, beta_logit, a_logit, conv_w, a_log, dt_bias, norm_w, w_o):
    b, l, _ = qkv.shape
    qkv = jax.nn.silu(causal_dwconv(qkv, conv_w))
    q, k, v = jnp.split(qkv.astype(jnp.float32), [W_Q, W_Q + W_K], axis=-1)
    q = l2norm(q.reshape(b, l, GDN_HEADS, GDN_DK)) * (GDN_DK ** -0.5)
    k = l2norm(k.reshape(b, l, GDN_HEADS, GDN_DK))
    v = v.reshape(b, l, GDN_HEADS, GDN_DV)
    beta = jax.nn.sigmoid(beta_logit.astype(jnp.float32))
    g = -jnp.exp(a_log.astype(jnp.float32)) * jax.nn.softplus(
        a_logit.astype(jnp.float32) + dt_bias.astype(jnp.float32))
    o = chunk_gated_delta_rule(q, k, v, g, beta)
    zf = z.astype(jnp.float32).reshape(b, l, GDN_HEADS, GDN_DV)
    o = rmsnorm(o, norm_w) * jax.nn.silu(zf)
    return o.reshape(b, l, GDN_HEADS * GDN_DV).astype(qkv.dtype) @ w_o


def conformer_conv(glu_in, pw1_b, dw_w, dw_b, ln_g, ln_b, w_o, b_o):
    a, gate = jnp.split(glu_in + pw1_b, 2, axis=-1)
    h = a * jax.nn.sigmoid(gate)
    h = causal_dwconv(h, dw_w) + dw_b
    h = jax.nn.silu(layernorm(h, ln_g, ln_b))
    return h @ w_o + b_o


def setup_inputs(seed: int = 0) -> dict:
    key = jax.random.key(seed)
    ks = iter(jax.random.split(key, 40))

    def nrm(shape, scale):
        return jax.random.normal(next(ks), shape, jnp.float32) * scale

    def gain(shape):
        return 1.0 + 0.02 * jax.random.normal(next(ks), shape, jnp.float32)

    L = DEPTH
    inp = {}
    inp["x"] = nrm((BATCH, SEQ, D_MODEL), 1.0)
    inp["ffn1_norm_pre"] = gain((L, D_MODEL))
    inp["ffn1_norm_post"] = gain((L, D_MODEL))
    inp["ffn1_w_in"] = nrm((L, D_MODEL, 2 * D_FF), D_MODEL ** -0.5)
    inp["ffn1_w_out"] = nrm((L, D_FF, D_MODEL), D_FF ** -0.5)
    inp["mix_norm_pre"] = gain((L, D_MODEL))
    inp["mix_norm_post"] = gain((L, D_MODEL))
    inp["mix_w_in"] = nrm((L, D_MODEL, P_IN), D_MODEL ** -0.5)
    inp["gdn_conv_w"] = nrm((L, GDN_CONV, W_Q + W_K + W_V), GDN_CONV ** -0.5)
    inp["gdn_a_log"] = jnp.log(jax.random.uniform(next(ks), (L, GDN_HEADS), jnp.float32, 1.0, 16.0))
    dt = jnp.exp(jax.random.uniform(next(ks), (L, GDN_HEADS), jnp.float32, np.log(1e-3), np.log(1e-1)))
    inp["gdn_dt_bias"] = dt + jnp.log(-jnp.expm1(-dt))
    inp["gdn_norm_w"] = gain((L, GDN_DV))
    inp["gdn_w_o"] = nrm((L, GDN_HEADS * GDN_DV, D_MODEL), (GDN_HEADS * GDN_DV) ** -0.5)
    inp["cnv_pw1_b"] = nrm((L, W_GLU), 0.02)
    inp["cnv_dw_w"] = nrm((L, CNV_K, CNV_CH), CNV_K ** -0.5)
    inp["cnv_dw_b"] = nrm((L, CNV_CH), 0.02)
    inp["cnv_ln_g"] = gain((L, CNV_CH))
    inp["cnv_ln_b"] = nrm((L, CNV_CH), 0.02)
    inp["cnv_w_o"] = nrm((L, CNV_CH, D_MODEL), CNV_CH ** -0.5)
    inp["cnv_b_o"] = nrm((L, D_MODEL), 0.02)
    inp["mix_w_out"] = nrm((L, D_MODEL, D_MODEL), D_MODEL ** -0.5)
    inp["ffn2_norm_pre"] = gain((L, D_MODEL))
    inp["ffn2_norm_post"] = gain((L, D_MODEL))
    inp["ffn2_w_in"] = nrm((L, D_MODEL, 2 * D_FF), D_MODEL ** -0.5)
    inp["ffn2_w_out"] = nrm((L, D_FF, D_MODEL), D_FF ** -0.5)
    return inp


def reference(x, ffn1_norm_pre, ffn1_norm_post, ffn1_w_in, ffn1_w_out,
              mix_norm_pre, mix_norm_post, mix_w_in,
              gdn_conv_w, gdn_a_log, gdn_dt_bias, gdn_norm_w, gdn_w_o,
              cnv_pw1_b, cnv_dw_w, cnv_dw_b, cnv_ln_g, cnv_ln_b, cnv_w_o, cnv_b_o,
              mix_w_out,
              ffn2_norm_pre, ffn2_norm_post, ffn2_w_in, ffn2_w_out):
    for i in range(DEPTH):
        f = swiglu_ffn(rmsnorm(x, ffn1_norm_pre[i]), ffn1_w_in[i], ffn1_w_out[i])
        x = x + 0.5 * rmsnorm(f, ffn1_norm_post[i])

        h = rmsnorm(x, mix_norm_pre[i])
        p = h @ mix_w_in[i]
        qkv, z, beta_logit, a_logit, glu_in, gates = jnp.split(p, SPLITS, axis=-1)
        y_a = gated_deltanet(qkv, z, beta_logit, a_logit, gdn_conv_w[i], gdn_a_log[i],
                             gdn_dt_bias[i], gdn_norm_w[i], gdn_w_o[i])
        y_b = conformer_conv(glu_in, cnv_pw1_b[i], cnv_dw_w[i], cnv_dw_b[i],
                             cnv_ln_g[i], cnv_ln_b[i], cnv_w_o[i], cnv_b_o[i])
        g_a, g_b = jnp.split(jax.nn.sigmoid(gates), 2, axis=-1)
        y = (g_a * y_a + g_b * y_b) @ mix_w_out[i]
        x = x + rmsnorm(y, mix_norm_post[i])

        f = swiglu_ffn(rmsnorm(x, ffn2_norm_pre[i]), ffn2_w_in[i], ffn2_w_out[i])
        x = x + 0.5 * rmsnorm(f, ffn2_norm_post[i])
    return x
```

```python
import numpy as np
import concourse.bass as bass
import concourse.mybir as mybir
from concourse.bass_utils import run_bass_kernel_spmd
from contextlib import ExitStack

F32 = mybir.dt.float32
BF16 = mybir.dt.bfloat16
AF = mybir.ActivationFunctionType
ALU = mybir.AluOpType

D = 1024
DFF = 2816
NH = 8
DEPTH = 2
PIN = 8208
NDMA_SEMS = 24


class Op:
    __slots__ = ("eng", "emit", "dma", "deps", "has_dep", "token", "prev_token", "idx")

    def __init__(self, eng, emit, dma):
        self.eng = eng
        self.emit = emit
        self.dma = dma
        self.deps = ()
        self.has_dep = False
        self.token = None
        self.prev_token = None


class Sched:
    ENGS = ("pe", "act", "dve", "pool", "sp")

    def __init__(self, nc):
        self.nc = nc
        self.eng_ops = {e: [] for e in self.ENGS}
        self.last_w = {}
        self.readers = {}
        self.pending = {e: [] for e in self.ENGS}
        self.last_compute = {}
        self.dmas = []
        self.live_dmas = []
        self.nops = 0
        self.bank_rd = {}

    @staticmethod
    def _norm(keys):
        return [k[:2] if (isinstance(k, tuple) and len(k) == 3 and k[0] in ("p", "q")) else k for k in keys]

    def add(self, eng, emit, reads=(), writes=(), dma=False):
        reads = self._norm(reads)
        writes = self._norm(writes)
        op = Op(eng, emit, dma)
        op.idx = self.nops
        self.nops += 1
        deps = {}
        psum_banks = set()

        def add_dep(d):
            if d.dma:
                deps[("d", d.idx)] = d
            else:
                if eng == "pe" and d.eng == "pe":
                    return
                cur = deps.get(d.eng)
                if cur is None or cur.idx < d.idx:
                    deps[d.eng] = d

        for r in reads:
            w = self.last_w.get(r)
            if w is not None:
                add_dep(w)
            if isinstance(r, tuple) and r[0] in ("p", "q"):
                bk = (r[0], r[1])
                last = self.bank_rd.get(bk)
                if last is not None and last.eng != eng:
                    add_dep(last)
                psum_banks.add(bk)
        for w in writes:
            lw = self.last_w.get(w)
            if lw is not None:
                add_dep(lw)
            rd = self.readers.get(w)
            if rd:
                for d in rd.values():
                    add_dep(d)
        for d in self.pending[eng]:
            add_dep(d)
        self.pending[eng] = []
        op.deps = tuple(deps.values())
        for d in op.deps:
            d.has_dep = True
        for bk in psum_banks:
            self.bank_rd[bk] = op
        for r in reads:
            rd = self.readers.setdefault(r, {})
            rd[("d", op.idx) if dma else eng] = op
        for w in writes:
            self.last_w[w] = op
            self.readers[w] = {}
        self.eng_ops[eng].append(op)
        if dma:
            self.dmas.append(op)
            self.live_dmas.append(op)
        else:
            self.last_compute[eng] = op
        return op

    def barrier(self):
        toks = list(self.last_compute.values()) + list(self.live_dmas)
        for e in self.ENGS:
            self.pending[e] = list(toks)
        self.live_dmas = []
        self.last_w = {}
        self.readers = {}
        self.bank_rd = {}

    def emit_all(self, stack):
        nc = self.nc
        sems = {}
        for e in ("pe", "act", "dve", "pool"):
            sems[e] = stack.enter_context(nc.semaphore("c_" + e))
            cnt = 0
            for op in self.eng_ops[e]:
                if op.dma:
                    continue
                if op.has_dep:
                    cnt += 1
                    op.token = (sems[e], cnt)
        dsems = [stack.enter_context(nc.semaphore("d%d" % i)) for i in range(NDMA_SEMS)]
        counts = [0] * NDMA_SEMS
        for i, op in enumerate(self.dmas):
            s = i % NDMA_SEMS
            n = counts[s]
            if n > 0:
                op.prev_token = (dsems[s], 16 * n)
            counts[s] = n + 1
            op.token = (dsems[s], 16 * (n + 1))
        block = stack.enter_context(nc.Block())

        def run(eng_name):
            def body(eng):
                waited = {}
                for op in self.eng_ops[eng_name]:
                    toks = [d.token for d in op.deps]
                    if op.prev_token is not None:
                        toks.append(op.prev_token)
                    for (sem, val) in toks:
                        key = id(sem)
                        if waited.get(key, 0) < val:
                            eng.wait_ge(sem, val)
                            waited[key] = val
                    inst = op.emit(eng)
                    if op.dma:
                        inst.then_inc(op.token[0], 16)
                    elif op.has_dep:
                        inst.then_inc(op.token[0], 1)
            return body

        block.tensor(run("pe"))
        block.scalar(run("act"))
        block.vector(run("dve"))
        block.gpsimd(run("pool"))
        block.sync(run("sp"))


class Arena:
    def __init__(self, nc, limit):
        self.nc = nc
        self.limit = limit
        self.base = 20608
        self.off = 20608
        self.n = 0

    def alloc(self, shape, dtype, name="t"):
        esz = 2 if dtype == BF16 else 4
        per_part = esz
        for s in shape[1:]:
            per_part *= s
        off = (self.off + 63) // 64 * 64
        assert off + per_part <= self.limit, ("SBUF overflow", name, off, per_part, self.limit)
        self.off = off + per_part
        self.n += 1
        h = self.nc.alloc_sbuf_tensor_at("%s_%d" % (name, self.n), list(shape), dtype, offset=off)
        return h.ap()

    def mark_persistent(self):
        self.base = self.off

    def reset(self):
        self.off = self.base


MW = {"qkv": 0, "z": 3072, "ba": 4096, "glu_a": 4112, "glu_g": 5136, "ga": 6160, "gb": 7184}


def build_program(T, depth, phases=None):
    nc = bass.Bass("TRN2", target_bir_lowering=False)
    NT = T // 512

    def din(name, shape):
        return nc.dram_tensor(name, list(shape), F32, kind="ExternalInput").ap()

    x_in = din("x", [T, D])
    y_out = nc.dram_tensor("y", [T, D], F32, kind="ExternalOutput").ap()
    W = {}
    for nm, shp in WEIGHT_SHAPES.items():
        W[nm] = din(nm, (depth,) + tuple(shp))
    C = {}
    for nm, shp in CONST_SHAPES.items():
        C[nm] = din(nm, shp)
    m2s = nc.dram_tensor("m2s", [NT, 128, 8, 512], BF16, kind="Internal").ap()
    ogs = nc.dram_tensor("ogs", [NT, 128, 8, 512], BF16, kind="Internal").ap()

    S = Sched(nc)
    ar = Arena(nc, nc.SBUF_PARTITION_SIZE_BYTES)

    def A(eng, fn, r=(), w=()):
        return S.add(eng, fn, reads=r, writes=w)

    def DMA(q, out, in_, r=(), w=()):
        return S.add(q, lambda e: e.dma_start(out=out, in_=in_), reads=r, writes=w, dma=True)

    def PK(b):
        return [("p", b, s) for s in range(4)]

    def QK(b):
        return [("q", b, s) for s in range(8)]

    ident_f = ar.alloc([128, 128], F32, "identf")
    ident_b = ar.alloc([128, 128], BF16, "identb")
    ltri_f = ar.alloc([128, 128], F32, "ltri")
    ones_f = ar.alloc([128, 128], F32, "onesf")
    ones_b = ar.alloc([128, 128], BF16, "onesb")
    mask_s = ar.alloc([128, 512], F32, "masks")
    mask_i = ar.alloc([128, 512], F32, "maski")
    cm05 = ar.alloc([128, 8], F32, "cm05")
    DMA("sp", ident_f, C["c_ident"], w=["identf"])
    DMA("sp", ltri_f, C["c_ltri"], w=["ltri"])
    DMA("sp", ones_f, C["c_ones"], w=["onesf"])
    DMA("sp", mask_s, C["c_masks"], w=["masks"])
    DMA("sp", mask_i, C["c_maski"], w=["maski"])
    A("dve", lambda e: e.tensor_copy(out=ident_b, in_=ident_f), ["identf"], ["identb"])
    A("dve", lambda e: e.tensor_copy(out=ones_b, in_=ones_f), ["onesf"], ["onesb"])
    A("pool", lambda e: e.memset(cm05, -0.5), [], ["cm05"])
    ar.mark_persistent()

    ps = nc.alloc_psum_tensor("ps", [128, 6, 512], F32).ap()
    psb = nc.alloc_psum_tensor("psb", [128, 2, 1024], BF16).ap()

    def bcast_rows(dst, src_row, key):
        DMA("sp", dst, src_row.partition_broadcast(128), w=[key])

    def load_weight(dst, src2d, keyfn, ncols, chunk=1024):
        for c0 in range(0, ncols, chunk):
            c1 = min(ncols, c0 + chunk)
            for k in range(8):
                DMA("pool", dst[:, k, c0:c1], src2d[k * 128:(k + 1) * 128, c0:c1], w=[keyfn(k, c0 // chunk)])

    def load_cols(dst, dkey, entries):
        raw = ar.alloc([128, 128], F32, "raw")
        r0 = 0
        allrows = []
        for src in entries:
            R = src.shape[0]
            off = 0
            while off < R:
                n = min(R - off, 128 - (r0 % 128)) if (r0 % 128) else min(R - off, 128)
                allrows.append((src[off:off + n, :], r0, n))
                r0 += n
                off += n
        total = r0
        g = 0
        while g * 128 < total:
            rows = [(a, r, n) for (a, r, n) in allrows if r // 128 == g]
            nr = sum(n for (_, _, n) in rows)
            key = ("raw", g)
            for (a, r, n) in rows:
                DMA("sp", raw[r % 128:r % 128 + n, :], a, w=[key])
            A("pe", lambda e, nr=nr: e.transpose(out=ps[:, 5, 0:nr], in_=raw[0:nr, :], identity=ident_f[0:nr, 0:nr]),
              [key, "identf"], PK(5))
            A("dve", lambda e, nr=nr, g=g: e.tensor_copy(out=dst[:, g * 128:g * 128 + nr], in_=ps[:, 5, 0:nr]),
              PK(5), [dkey])
            g += 1
            if g * 128 < total:
                S.last_w[("raw", g)] = S.last_compute["pe"]
        return total

    def prenorm_hT(src, ti, xb, xi, nbuf, wpre, hb, hT, ss, rstd):
        slots = []
        for b in range(4):
            sl = xi[0] % nbuf
            xi[0] += 1
            slots.append(sl)
            DMA("sp", xb[sl], src[ti * 512 + b * 128:ti * 512 + (b + 1) * 128, :],
                r=[("xd", ti, b)], w=[("xb", sl)])
            hbb = hb[b % 2]
            hk = ("hb", id(hbb))
            A("act", lambda e, sl=sl, hbb=hbb, b=b: e.activation(
                out=hbb, in_=xb[sl], func=AF.Square, accum_out=ss[:, b:b + 1]),
                [("xb", sl)], [hk, ("ss", b)])
            A("dve", lambda e, b=b: e.tensor_scalar(
                out=rstd[:, b:b + 1], in0=ss[:, b:b + 1], scalar1=1.0 / D, scalar2=1e-6,
                op0=ALU.mult, op1=ALU.add), [("ss", b)], [("rstd", b)])
            A("pool", lambda e, b=b: e.tensor_tensor(
                out=rstd[:, b:b + 1], in0=rstd[:, b:b + 1], in1=cm05[:, 0:1], op=ALU.pow),
                [("rstd", b), "cm05"], [("rstd", b)])
            A("dve", lambda e, sl=sl, hbb=hbb, b=b: e.scalar_tensor_tensor(
                out=hbb, in0=xb[sl], scalar=rstd[:, b:b + 1], in1=wpre, op0=ALU.mult, op1=ALU.mult),
                [("xb", sl), ("rstd", b), "wpre"], [hk])
            q = b % 2
            for k in range(8):
                A("pe", lambda e, hbb=hbb, k=k, q=q: e.transpose(
                    out=psb[:, q, k * 128:(k + 1) * 128], in_=hbb[:, k * 128:(k + 1) * 128],
                    identity=ident_b), [hk, "identb"], [("q", q, k)])
            dstv = hT[:, :, b * 128:(b + 1) * 128]
            srcv = psb[:, q, :].rearrange("p (k t) -> p k t", k=8)
            if b % 2 == 0:
                A("act", lambda e, dstv=dstv, srcv=srcv: e.copy(out=dstv, in_=srcv), QK(q), [("hT", b)])
            else:
                A("dve", lambda e, dstv=dstv, srcv=srcv: e.tensor_copy(out=dstv, in_=srcv), QK(q), [("hT", b)])
        return slots

    HT4 = [("hT", b) for b in range(4)]

    def postnorm_residual(b, sl, xb, wpost, tmp, ss2, rstd2, scale, dst, ti):
        t0 = ti * 512
        for half in range(2):
            po = 4 + half
            A("act", lambda e, half=half, po=po: e.activation(
                out=tmp[half], in_=ps[:, po, :], func=AF.Square,
                accum_out=ss2[:, 2 * b + half:2 * b + half + 1]),
                PK(po), [("tmp", half), ("ss2", b, half)])
        A("dve", lambda e: e.tensor_tensor(
            out=rstd2[:, b:b + 1], in0=ss2[:, 2 * b:2 * b + 1], in1=ss2[:, 2 * b + 1:2 * b + 2],
            op=ALU.add), [("ss2", b, 0), ("ss2", b, 1)], [("rstd2", b)])
        inv = 1.0 / (scale * scale)
        A("dve", lambda e: e.tensor_scalar(
            out=rstd2[:, b:b + 1], in0=rstd2[:, b:b + 1], scalar1=inv / D, scalar2=inv * 1e-6,
            op0=ALU.mult, op1=ALU.add), [("rstd2", b)], [("rstd2", b)])
        A("pool", lambda e: e.tensor_tensor(
            out=rstd2[:, b:b + 1], in0=rstd2[:, b:b + 1], in1=cm05[:, 0:1], op=ALU.pow),
            [("rstd2", b), "cm05"], [("rstd2", b)])
        for half in range(2):
            po = 4 + half
            A("dve", lambda e, half=half, po=po: e.scalar_tensor_tensor(
                out=tmp[half], in0=ps[:, po, :], scalar=rstd2[:, b:b + 1],
                in1=wpost[:, half * 512:(half + 1) * 512], op0=ALU.mult, op1=ALU.mult),
                PK(po) + [("rstd2", b), "wpost"], [("tmp", half)])
            A("pool", lambda e, half=half: e.tensor_tensor(
                out=xb[sl][:, half * 512:(half + 1) * 512], in0=xb[sl][:, half * 512:(half + 1) * 512],
                in1=tmp[half], op=ALU.add), [("tmp", half), ("xb", sl)], [("xb", sl)])
        DMA("sp", dst[t0 + b * 128:t0 + (b + 1) * 128, :], xb[sl], r=[("xb", sl)], w=[("xd", ti, b)])

    def load_x(src, ti, xb, xi, nbuf):
        slots = []
        for b in range(4):
            sl = xi[0] % nbuf
            xi[0] += 1
            slots.append(sl)
            DMA("sp", xb[sl], src[ti * 512 + b * 128:ti * 512 + (b + 1) * 128, :],
                r=[("xd", ti, b)], w=[("xb", sl)])
        return slots

    def sigmoid_from(src, skeys, e_t, ekey, r_t, rkey, negbias=None, bkeys=()):
        if negbias is None:
            A("act", lambda e: e.activation(out=e_t, in_=src, func=AF.Exp, scale=-1.0), skeys, [ekey])
        else:
            A("act", lambda e: e.activation(out=e_t, in_=src, func=AF.Exp, scale=-1.0, bias=negbias),
              list(skeys) + list(bkeys), [ekey])
        A("act", lambda e: e.activation(out=e_t, in_=e_t, func=AF.Ln, bias=1.0), [ekey], [ekey])
        A("act", lambda e: e.activation(out=r_t, in_=e_t, func=AF.Exp, scale=-1.0), [ekey], [rkey])

    def ffn_phase(l, which, src, dst):
        ar.reset()
        S.barrier()
        w_in = W[which + "_w_in"]
        w_out = W[which + "_w_out"]
        win_sb = ar.alloc([128, 8, 2 * DFF], BF16, "win")
        wout_sb = ar.alloc([128, 22, D], BF16, "wout")
        wpre = ar.alloc([128, D], F32, "wpre")
        wpost = ar.alloc([128, D], F32, "wpost")
        xb = [ar.alloc([128, D], F32, "xb") for _ in range(4)]
        _hbsingle = True
        hb = [ar.alloc([128, D], BF16, "hb")] * 2
        hT = ar.alloc([128, 8, 512], BF16, "hT")
        inter = ar.alloc([128, 22, 512], BF16, "inter")
        sg = [ar.alloc([128, 512], BF16, "sg") for _ in range(2)]
        tmp = [ar.alloc([128, 512], F32, "tmp") for _ in range(2)]
        ss = ar.alloc([128, 8], F32, "ss")
        rstd = ar.alloc([128, 8], F32, "rstd")
        ss2 = ar.alloc([128, 8], F32, "ss2")
        rstd2 = ar.alloc([128, 4], F32, "rstd2")
        bcast_rows(wpre, W[which + "_norm_pre"][l], "wpre")
        bcast_rows(wpost, W[which + "_norm_post"][l], "wpost")
        for c0 in (0, 2 * 1408, 1408, 3 * 1408):
            for k in range(8):
                DMA("pool", win_sb[:, k, c0:c0 + 1408], w_in[l, k * 128:(k + 1) * 128, c0:c0 + 1408],
                    w=[("win", k, c0 // 1408)])
        for c in range(22):
            DMA("pool", wout_sb[:, c, :], w_out[l, c * 128:(c + 1) * 128, :], w=[("wout", c)])
        xi = [0]
        for ti in range(NT):
            slots = prenorm_hT(src, ti, xb, xi, 4, wpre, hb, hT, ss, rstd)
            for c in range(22):
                pa = c % 2
                pb = 2 + c % 2
                for k in range(8):
                    A("pe", lambda e, c=c, k=k, pa=pa: e.matmul(
                        ps[:, pa, :], lhsT=win_sb[:, k, c * 128:(c + 1) * 128], rhs=hT[:, k, :],
                        start=(k == 0), stop=(k == 7)),
                        [("win", k, (c * 128) // 1408), ("win", k, (c * 128 + 127) // 1408)] + HT4, PK(pa))
                for k in range(8):
                    A("pe", lambda e, c=c, k=k, pb=pb: e.matmul(
                        ps[:, pb, :], lhsT=win_sb[:, k, DFF + c * 128:DFF + (c + 1) * 128], rhs=hT[:, k, :],
                        start=(k == 0), stop=(k == 7)),
                        [("win", k, (DFF + c * 128) // 1408), ("win", k, (DFF + c * 128 + 127) // 1408)] + HT4, PK(pb))
                A("act", lambda e, c=c, pa=pa: e.activation(out=sg[c % 2], in_=ps[:, pa, :], func=AF.Silu),
                  PK(pa), [("sg", c % 2)])
                A("dve", lambda e, c=c, pb=pb: e.tensor_tensor(
                    out=inter[:, c, :], in0=ps[:, pb, :], in1=sg[c % 2], op=ALU.mult),
                    PK(pb) + [("sg", c % 2)], [("inter", c)])
            for b in range(4):
                for half in range(2):
                    po = 4 + half
                    for c in range(22):
                        A("pe", lambda e, c=c, b=b, half=half, po=po: e.matmul(
                            ps[:, po, :], lhsT=inter[:, c, b * 128:(b + 1) * 128],
                            rhs=wout_sb[:, c, half * 512:(half + 1) * 512],
                            start=(c == 0), stop=(c == 21)), [("inter", c), ("wout", c)], PK(po))
                postnorm_residual(b, slots[b], xb, wpost, tmp, ss2, rstd2, 0.5, dst, ti)

    def mixer_A(l, src):
        ar.reset()
        S.barrier()
        mw = W["mix_w_in"][l]
        wglu = ar.alloc([128, 8, 2048], BF16, "wglu")
        cwo = ar.alloc([128, 8, 1024], BF16, "cwo")
        dg = ar.alloc([128, 8, 31, 128], BF16, "dg31")
        wpre = ar.alloc([128, D], F32, "wpre")
        cols = ar.alloc([128, 512], F32, "cols")
        ncols = ar.alloc([128, 16], F32, "ncols")
        xb = [ar.alloc([128, D], F32, "xb") for _ in range(2)]
        hb = [ar.alloc([128, D], BF16, "hb")] * 2
        hT = ar.alloc([128, 8, 512], BF16, "hT")
        hg1 = ar.alloc([128, 8, 544], BF16, "hglu")
        c0f = ar.alloc([128, 8, 512], F32, "c0f")
        cbf = [ar.alloc([128, 512], BF16, "cbf")] * 2
        csq = [ar.alloc([128, 512], BF16, "csq")] * 2
        cT = ar.alloc([128, 8, 512], BF16, "cT")
        m2T = [ar.alloc([128, 8, 512], BF16, "m2T") for _ in range(1)]
        ft = [ar.alloc([128, 512], F32, "ft") for _ in range(5)]
        mu = ar.alloc([128, 512], F32, "mu")
        rsd = ar.alloc([128, 512], F32, "rsd")
        ss = ar.alloc([128, 8], F32, "ss")
        rstd = ar.alloc([128, 8], F32, "rstd")
        bcast_rows(wpre, W["mix_norm_pre"][l], "wpre")
        load_weight(wglu, mw[:, MW["glu_a"]:MW["glu_a"] + 2048], lambda k, ch: ("wglu", k, ch), 2048)
        load_weight(cwo, W["cnv_w_o"][l], lambda k, ch: ("cwo", k), 1024)
        ent = [W["cnv_pw1_b"][l].rearrange("(c p) -> c p", p=128),
               W["cnv_dw_b"][l].rearrange("(c p) -> c p", p=128),
               W["cnv_ln_g"][l].rearrange("(c p) -> c p", p=128),
               W["cnv_ln_b"][l].rearrange("(c p) -> c p", p=128),
               W["cnv_b_o"][l].rearrange("(c p) -> c p", p=128),
               W["cnv_dw_w"][l].rearrange("j (c p) -> (j c) p", p=128)]
        load_cols(cols, "cols", ent)
        PW, DWB, LNG, LNB, BO, DWW = 0, 16, 24, 32, 40, 48
        A("dve", lambda e: e.tensor_scalar(out=ncols, in0=cols[:, 0:16], scalar1=-1.0, scalar2=None, op0=ALU.mult),
          ["cols"], ["ncols"])
        for c in range(8):
            for j in range(31):
                A("dve" if (c * 31 + j) % 2 else "pool", lambda e, c=c, j=j: e.tensor_scalar(
                    out=dg[:, c, j, :], in0=ident_f, scalar1=cols[:, DWW + j * 8 + c:DWW + j * 8 + c + 1],
                    scalar2=None, op0=ALU.mult), ["cols", "identf"], [("dg", c)])
        A("pool", lambda e: e.memset(hg1[:, :, 0:32], 0.0), [], [("hgh", c) for c in range(8)])
        xi = [0]
        for ti in range(NT):
            cur = hg1
            slots = prenorm_hT(src, ti, xb, xi, 2, wpre, hb, hT, ss, rstd)
            for c in range(8):
                for k in range(8):
                    A("pe", lambda e, c=c, k=k: e.matmul(ps[:, 0, :], lhsT=wglu[:, k, c * 128:(c + 1) * 128],
                                                         rhs=hT[:, k, :], start=(k == 0), stop=(k == 7)),
                      [("wglu", k, 0)] + HT4, PK(0))
                for k in range(8):
                    A("pe", lambda e, c=c, k=k: e.matmul(ps[:, 1, :], lhsT=wglu[:, k, 1024 + c * 128:1024 + (c + 1) * 128],
                                                         rhs=hT[:, k, :], start=(k == 0), stop=(k == 7)),
                      [("wglu", k, 1)] + HT4, PK(1))
                f0 = ft[c % 2]
                f1 = ft[2 + c % 2]
                sigmoid_from(ps[:, 1, :], PK(1), f0, ("ft", c % 2), f1, ("ft", 2 + c % 2),
                             negbias=ncols[:, 8 + c:9 + c], bkeys=["ncols"])
                if ti > 0:
                    A("pool", lambda e, c=c, cur=cur: e.tensor_copy(out=cur[:, c, 0:32], in_=cur[:, c, 512:544]),
                      [("hgd", c)], [("hgh", c)])
                A("dve", lambda e, c=c, f1=f1, cur=cur: e.scalar_tensor_tensor(
                    out=cur[:, c, 32:544], in0=ps[:, 0, :], scalar=cols[:, PW + c:PW + c + 1], in1=f1,
                    op0=ALU.add, op1=ALU.mult), PK(0) + ["cols", ("ft", 2 + c % 2)], [("hgd", c)])
            for c in range(8):
                pb = 2 + c % 2
                for j in range(31):
                    A("pe", lambda e, c=c, j=j, pb=pb, cur=cur: e.matmul(
                        ps[:, pb, :], lhsT=dg[:, c, j, :], rhs=cur[:, c, 2 + j:2 + j + 512],
                        start=(j == 0), stop=(j == 30)), [("dg", c), ("hgd", c), ("hgh", c)], PK(pb))
                A("act", lambda e, c=c, pb=pb: e.activation(out=c0f[:, c, :], in_=ps[:, pb, :], func=AF.Identity,
                                                            bias=cols[:, DWB + c:DWB + c + 1]),
                  PK(pb) + ["cols"], [("c0f", c)])
                A("dve", lambda e, c=c: e.tensor_copy(out=cbf[c % 2], in_=c0f[:, c, :]), [("c0f", c)], [("cbf", 0)])
                A("act", lambda e, c=c: e.activation(out=csq[c % 2], in_=c0f[:, c, :], func=AF.Square),
                  [("c0f", c)], [("csq", 0)])
                A("pe", lambda e, c=c: e.matmul(ps[:, 4, :], lhsT=ones_b, rhs=cbf[c % 2], start=(c == 0), stop=(c == 7)),
                  ["onesb", ("cbf", 0)], PK(4))
                A("pe", lambda e, c=c: e.matmul(ps[:, 5, :], lhsT=ones_b, rhs=csq[c % 2], start=(c == 0), stop=(c == 7)),
                  ["onesb", ("csq", 0)], PK(5))
            A("dve", lambda e: e.tensor_scalar(out=mu, in0=ps[:, 4, :], scalar1=1.0 / D, scalar2=None, op0=ALU.mult),
              PK(4), ["mu"])
            A("dve", lambda e: e.tensor_tensor(out=rsd, in0=mu, in1=mu, op=ALU.mult), ["mu"], ["rsd"])
            A("dve", lambda e: e.scalar_tensor_tensor(out=rsd, in0=ps[:, 5, :], scalar=1.0 / D, in1=rsd,
                                                      op0=ALU.mult, op1=ALU.subtract), PK(5) + ["rsd"], ["rsd"])
            A("dve", lambda e: e.tensor_scalar(out=rsd, in0=rsd, scalar1=1e-5, scalar2=None, op0=ALU.add), ["rsd"], ["rsd"])
            A("act", lambda e: e.activation(out=rsd, in_=rsd, func=AF.Ln), ["rsd"], ["rsd"])
            A("act", lambda e: e.activation(out=rsd, in_=rsd, func=AF.Exp, scale=-0.5), ["rsd"], ["rsd"])
            for c in range(8):
                f0 = ft[c % 2]
                f1 = ft[2 + c % 2]
                f2 = ft[4]
                k0, k1, k2 = ("ft", c % 2), ("ft", 2 + c % 2), ("ft", 4)
                A("dve", lambda e, c=c, f0=f0: e.tensor_tensor(out=f0, in0=c0f[:, c, :], in1=mu, op=ALU.subtract),
                  [("c0f", c), "mu"], [k0])
                A("dve", lambda e, f0=f0: e.tensor_tensor(out=f0, in0=f0, in1=rsd, op=ALU.mult), [k0, "rsd"], [k0])
                A("act", lambda e, c=c, f0=f0: e.activation(out=f0, in_=f0, func=AF.Identity,
                                                            scale=cols[:, LNG + c:LNG + c + 1],
                                                            bias=cols[:, LNB + c:LNB + c + 1]), [k0, "cols"], [k0])
                sigmoid_from(f0, [k0], f1, k1, f2, k2)
                A("dve", lambda e, c=c, f0=f0, f2=f2: e.tensor_tensor(out=cT[:, c, :], in0=f0, in1=f2, op=ALU.mult),
                  [k0, k2], [("cT", c)])
            mt = m2T[0]
            for fo in range(8):
                for c in range(8):
                    A("pe", lambda e, fo=fo, c=c: e.matmul(ps[:, fo % 2, :], lhsT=cwo[:, c, fo * 128:(fo + 1) * 128],
                                                           rhs=cT[:, c, :], start=(c == 0), stop=(c == 7)),
                      [("cwo", c), ("cT", c)], PK(fo % 2))
                A("act", lambda e, fo=fo, mt=mt: e.activation(out=mt[:, fo, :], in_=ps[:, fo % 2, :], func=AF.Identity,
                                                              bias=cols[:, BO + fo:BO + fo + 1]),
                  PK(fo % 2) + ["cols"], [("m2T", 0)])
            DMA("sp", m2s[ti], mt, r=[("m2T", 0)], w=[("m2s", ti)])

    def mixer_B(l, src):
        ar.reset()
        S.barrier()
        mw = W["mix_w_in"][l]
        wqkv = ar.alloc([128, 8, 3072], BF16, "wqkv")
        wba = ar.alloc([128, 8, 16], BF16, "wba")
        dg = ar.alloc([128, 24, 4, 128], BF16, "dg4")
        wpre = ar.alloc([128, D], F32, "wpre")
        cols = ar.alloc([128, 128], F32, "cols")
        xb = [ar.alloc([128, D], F32, "xb") for _ in range(2)]
        hb = [ar.alloc([128, D], BF16, "hb")] * 2
        hT = ar.alloc([128, 8, 512], BF16, "hT")
        ss = ar.alloc([128, 8], F32, "ss")
        rstd = ar.alloc([128, 8], F32, "rstd")
        qT = ar.alloc([128, 8, 512], BF16, "qT")
        qdT = ar.alloc([128, 8, 512], BF16, "qdT")
        kT = ar.alloc([128, 8, 512], BF16, "kT")
        vT = ar.alloc([128, 8, 512], BF16, "vT")
        ogT = [ar.alloc([128, 8, 512], BF16, "ogT")] * 2
        pch = ar.alloc([128, 24, 4], BF16, "pch")
        pc = [ar.alloc([128, 516], BF16, "pc") for _ in range(2)]
        ft = [ar.alloc([128, 512], F32, "ft") for _ in range(6)]
        ysq = [ar.alloc([128, 512], BF16, "ysq")] * 2
        ebc = [ar.alloc([128, 512], F32, "ebc")] * 2
        dms = [ar.alloc([128, 512], BF16, "dms") for _ in range(2)]
        dmi = [ar.alloc([128, 512], BF16, "dmi") for _ in range(2)]
        dtb = ar.alloc([128, 32], F32, "dtb")
        nega = ar.alloc([128, 32], F32, "nega")
        sm = {nm: ar.alloc([128, 32], F32, nm) for nm in
              ("et", "beta", "nbeta", "t", "g", "Gc", "nGc", "Gt", "eG", "kds", "dch", "bEG")}
        Sst = ar.alloc([128, 8, 128], F32, "Sst")
        Sbf = ar.alloc([128, 8, 128], BF16, "Sbf")
        NG = 8
        maskPB = ar.alloc([128, 7, 256], BF16, "maskPB")
        ident2 = ar.alloc([128, 256], BF16, "ident2")
        DMA("pool", maskPB.rearrange("p a b -> p (a b)"), C["c_mpb"], w=["maskPB"])
        DMA("pool", ident2, C["c_id2"], w=["ident2"])
        kbe = [ar.alloc([128, 128], BF16, "kbe") for _ in range(NG)]
        kdc = [ar.alloc([128, 128], BF16, "kdc") for _ in range(NG)]
        vb = [ar.alloc([128, 128], BF16, "vb") for _ in range(NG)]
        qkm = [ar.alloc([128, 128], BF16, "qkm") for _ in range(NG)]
        qkmT = [ar.alloc([128, 128], BF16, "qkmT") for _ in range(NG)]
        Wn = [ar.alloc([128, 256], BF16, "Wn") for _ in range(NG)]
        TU = [ar.alloc([128, 256], BF16, "TU") for _ in range(NG)]
        YY = [ar.alloc([128, 256], BF16, "YY") for _ in range(NG)]
        uw = [ar.alloc([128, 256], BF16, "uw") for _ in range(NG)]
        vn = [ar.alloc([128, 128], BF16, "vn") for _ in range(NG)]
        normw = ar.alloc([128, 1], F32, "normw")
        gbt = [ar.alloc([128, 128], F32, "gbt")] * 2

        bcast_rows(wpre, W["mix_norm_pre"][l], "wpre")
        load_weight(wqkv, mw[:, 0:3072], lambda k, ch: ("wqkv", k, ch), 3072)
        load_weight(wba, mw[:, MW["ba"]:MW["ba"] + 16], lambda k, ch: ("wba", k), 16)
        load_cols(cols, "cols", [W["gdn_conv_w"][l].rearrange("j (c p) -> (j c) p", p=128)])
        DMA("sp", normw, W["gdn_norm_w"][l].rearrange("(p o) -> p o", o=1), w=["normw"])
        for j in range(4):
            DMA("sp", dtb[:, j * 8:(j + 1) * 8], W["gdn_dt_bias"][l].partition_broadcast(128), w=["dtb"])
            DMA("sp", nega[:, j * 8:(j + 1) * 8], W["gdn_a_log"][l].partition_broadcast(128), w=["nega"])
        A("act", lambda e: e.activation(out=nega, in_=nega, func=AF.Exp), ["nega"], ["nega"])
        A("dve", lambda e: e.tensor_scalar(out=nega, in0=nega, scalar1=-1.0, scalar2=None, op0=ALU.mult), ["nega"], ["nega"])
        for ci in range(24):
            for j in range(4):
                A("dve" if (ci + j) % 2 else "pool", lambda e, ci=ci, j=j: e.tensor_scalar(
                    out=dg[:, ci, j, :], in0=ident_f, scalar1=cols[:, j * 24 + ci:j * 24 + ci + 1],
                    scalar2=None, op0=ALU.mult), ["cols", "identf"], [("dg", ci)])
        A("pool", lambda e: e.memset(pch, 0.0), [], [("pch", ci) for ci in range(24)])
        A("pool", lambda e: e.memset(Sst, 0.0), [], [("S", h) for h in range(8)])
        A("pool", lambda e: e.memset(Sbf, 0.0), [], [("Sb", h) for h in range(8)])
        xi = [0]
        rr = [0]
        for ti in range(NT):
            slots = prenorm_hT(src, ti, xb, xi, 2, wpre, hb, hT, ss, rstd)
            og = ogT[ti % 2]
            for j in range(4):
                for k in range(8):
                    A("pe", lambda e, j=j, k=k: e.matmul(ps[:, 0, j * 16:(j + 1) * 16],
                                                         lhsT=hT[:, k, j * 128:(j + 1) * 128], rhs=wba[:, k, :],
                                                         start=(k == 0), stop=(k == 7)), [("wba", k)] + HT4, PK(0))
            lg = ps[:, 0, 0:64].rearrange("p (j t) -> p j t", j=4)
            v3 = lambda t: t.rearrange("p (j h) -> p j h", j=4)
            A("act", lambda e: e.activation(out=v3(sm["et"]), in_=lg[:, :, 0:8], func=AF.Exp, scale=-1.0), PK(0), ["et"])
            A("dve", lambda e: e.tensor_scalar(out=sm["et"], in0=sm["et"], scalar1=1.0, scalar2=None, op0=ALU.add), ["et"], ["et"])
            A("dve", lambda e: e.reciprocal(out=sm["beta"], in_=sm["et"]), ["et"], ["beta"])
            A("dve", lambda e: e.tensor_scalar(out=sm["nbeta"], in0=sm["beta"], scalar1=-1.0, scalar2=None, op0=ALU.mult),
              ["beta"], ["nbeta"])
            A("dve", lambda e: e.tensor_tensor(out=v3(sm["t"]), in0=lg[:, :, 8:16], in1=v3(dtb), op=ALU.add),
              PK(0) + ["dtb"], ["t"])
            A("act", lambda e: e.activation(out=sm["t"], in_=sm["t"], func=AF.Exp), ["t"], ["t"])
            A("act", lambda e: e.activation(out=sm["t"], in_=sm["t"], func=AF.Ln, bias=1.0), ["t"], ["t"])
            A("dve", lambda e: e.tensor_tensor(out=sm["g"], in0=sm["t"], in1=nega, op=ALU.mult), ["t", "nega"], ["g"])
            A("pe", lambda e: e.matmul(ps[:, 1, 0:32], lhsT=ltri_f, rhs=sm["g"], start=True, stop=True), ["ltri", "g"], PK(1))
            A("pe", lambda e: e.matmul(ps[:, 1, 32:64], lhsT=ones_f, rhs=sm["g"], start=True, stop=True), ["onesf", "g"], PK(1))
            A("dve", lambda e: e.tensor_copy(out=sm["Gc"], in_=ps[:, 1, 0:32]), PK(1), ["Gc"])
            A("dve", lambda e: e.tensor_scalar(out=sm["nGc"], in0=sm["Gc"], scalar1=-1.0, scalar2=None, op0=ALU.mult), ["Gc"], ["nGc"])
            A("dve", lambda e: e.tensor_tensor(out=sm["Gt"], in0=ps[:, 1, 32:64], in1=sm["Gc"], op=ALU.subtract),
              PK(1) + ["Gc"], ["Gt"])
            A("act", lambda e: e.activation(out=sm["kds"], in_=sm["Gt"], func=AF.Exp), ["Gt"], ["kds"])
            A("act", lambda e: e.activation(out=sm["dch"], in_=ps[:, 1, 32:64], func=AF.Exp), PK(1), ["dch"])
            A("act", lambda e: e.activation(out=sm["eG"], in_=sm["Gc"], func=AF.Exp), ["Gc"], ["eG"])
            A("dve", lambda e: e.tensor_tensor(out=sm["bEG"], in0=sm["eG"], in1=sm["beta"], op=ALU.mult),
              ["eG", "beta"], ["bEG"])
            def _s1_slice0(h, hp):
                for j in range(4):
                    gt = gbt[j % 2]
                    A("dve", lambda e, j=j, h=h, gt=gt: e.tensor_scalar(
                        out=gt, in0=ones_f, scalar1=sm["g"][:, j * 8 + h:j * 8 + h + 1], scalar2=None, op0=ALU.mult),
                        ["g", "onesf"], [("gbt", 0)])
                    A("pe", lambda e, j=j, gt=gt: e.matmul(
                        ps[:, 2, j * 128:(j + 1) * 128], lhsT=gt, rhs=ltri_f,
                        start=True, stop=True), [("gbt", 0), "ltri"], [("p", 2, j)])
                A("act", lambda e, hp=hp: e.activation(out=ebc[hp], in_=ps[:, 2, :], func=AF.Exp), PK(2), [("ebc", 0)])
                fd = ft[5]
                for j in range(4):
                    A("act", lambda e, j=j, h=h, fd=fd: e.activation(
                        out=fd[:, j * 128:(j + 1) * 128], in_=ps[:, 2, j * 128:(j + 1) * 128], func=AF.Relu,
                        bias=sm["nGc"][:, j * 8 + h:j * 8 + h + 1]), [("p", 2, j), "nGc"], [("ft", 5)])
                A("act", lambda e, fd=fd: e.activation(out=fd, in_=fd, func=AF.Exp, scale=-1.0), [("ft", 5)], [("ft", 5)])
                A("dve", lambda e, fd=fd, hp=hp: e.tensor_tensor(out=dms[hp], in0=fd, in1=mask_s, op=ALU.mult),
                  [("ft", 5), "masks"], [("dms", hp)])
                A("pool", lambda e, fd=fd, hp=hp: e.tensor_tensor(out=dmi[hp], in0=fd, in1=mask_i, op=ALU.mult),
                  [("ft", 5), "maski"], [("dmi", hp)])
            def _s1_qkv(h, hp, i):
                ci = i * 8 + h
                pcs = pc[(h * 3 + i) % 2]
                pk = ("pc", (h * 3 + i) % 2)
                for k in range(8):
                    A("pe", lambda e, ci=ci, k=k: e.matmul(ps[:, 0, :], lhsT=wqkv[:, k, ci * 128:(ci + 1) * 128],
                                                           rhs=hT[:, k, :], start=(k == 0), stop=(k == 7)),
                      [("wqkv", k, ci // 8)] + HT4, PK(0))
                A("act", lambda e, ci=ci, pcs=pcs: e.copy(out=pcs[:, 0:4], in_=pch[:, ci, :]), [("pch", ci)], [pk])
                A("act", lambda e, pcs=pcs: e.copy(out=pcs[:, 4:516], in_=ps[:, 0, :]), PK(0), [pk])
                A("act", lambda e, ci=ci, pcs=pcs: e.copy(out=pch[:, ci, :], in_=pcs[:, 512:516]), [pk], [("pch", ci)])
                for j in range(4):
                    A("pe", lambda e, ci=ci, j=j, pcs=pcs: e.matmul(
                        ps[:, 1, :], lhsT=dg[:, ci, j, :], rhs=pcs[:, 1 + j:1 + j + 512],
                        start=(j == 0), stop=(j == 3)), [("dg", ci), pk], PK(1))
                f0, f1, f2 = ft[0 + i % 2], ft[2 + i % 2], ft[4]
                k0, k1, k2 = ("ft", i % 2), ("ft", 2 + i % 2), ("ft", 4)
                sigmoid_from(ps[:, 1, :], PK(1), f0, k0, f1, k1)
                if i == 2:
                    A("dve", lambda e, f1=f1, h=h: e.tensor_tensor(out=vT[:, h, :], in0=ps[:, 1, :], in1=f1, op=ALU.mult),
                      PK(1) + [k1], [("vT", h)])
                    return
                A("dve", lambda e, f1=f1, f2=f2: e.tensor_tensor(out=f2, in0=ps[:, 1, :], in1=f1, op=ALU.mult),
                  PK(1) + [k1], [k2])
                yq = ysq[i % 2]
                A("act", lambda e, f2=f2, yq=yq: e.activation(out=yq, in_=f2, func=AF.Square), [k2], [("ysq", 0)])
                A("pe", lambda e, yq=yq: e.matmul(ps[:, 3, :], lhsT=ones_b, rhs=yq, start=True, stop=True),
                  ["onesb", ("ysq", 0)], PK(3))
                A("dve", lambda e, f0=f0: e.tensor_scalar(out=f0, in0=ps[:, 3, :], scalar1=1e-6, scalar2=None, op0=ALU.add),
                  PK(3), [k0])
                A("act", lambda e, f0=f0: e.activation(out=f0, in_=f0, func=AF.Ln), [k0], [k0])
                A("act", lambda e, f0=f0: e.activation(out=f0, in_=f0, func=AF.Exp, scale=-0.5), [k0], [k0])
                if i == 1:
                    A("dve", lambda e, f0=f0, f2=f2, h=h: e.tensor_tensor(out=kT[:, h, :], in0=f2, in1=f0, op=ALU.mult),
                      [k0, k2], [("kT", h)])
                else:
                    A("dve", lambda e, f0=f0, f2=f2: e.scalar_tensor_tensor(
                        out=f2, in0=f2, scalar=float(128 ** -0.5), in1=f0, op0=ALU.mult, op1=ALU.mult), [k0, k2], [k2])
                    A("act", lambda e, f2=f2, h=h: e.copy(out=qT[:, h, :], in_=f2), [k2], [("qT", h)])
                    A("dve", lambda e, f2=f2, h=h, hp=hp: e.tensor_tensor(out=qdT[:, h, :], in0=f2, in1=ebc[hp], op=ALU.mult),
                      [k2, ("ebc", 0)], [("qdT", h)])
            def _chunks_pair(hA):
                G = range(8)

                def hd(g):
                    return hA + g // 4

                def col(nm, g):
                    c = (g % 4) * 8 + hd(g)
                    return sm[nm][:, c:c + 1]

                def jb(g):
                    return slice((g % 4) * 128, (g % 4 + 1) * 128)

                def pq(g, half):
                    o = (g % 2) * 256 + half * 128
                    return ps[:, g // 2, o:o + 128]

                def pq2(g):
                    o = (g % 2) * 256
                    return ps[:, g // 2, o:o + 256]

                def PKg(g, half=None):
                    return [("p", g // 2, 0)]

                def qs(g, s_):
                    i_ = g * 2 + s_
                    return psb[:, i_ // 8, (i_ % 8) * 128:(i_ % 8) * 128 + 128]

                def QKg(g, s_):
                    return [("q", (g * 2 + s_) // 8, 0)]
                for g in G:
                    h = hd(g)
                    A("pe", lambda e, g=g, h=h: e.transpose(out=qs(g, 0), in_=kT[:, h, jb(g)], identity=ident_b),
                      [("kT", h), "identb"], QKg(g, 0))
                    A("pe", lambda e, g=g, h=h: e.transpose(out=qs(g, 1), in_=vT[:, h, jb(g)], identity=ident_b),
                      [("vT", h), "identb"], QKg(g, 1))
                    A("pe", lambda e, g=g, h=h: e.matmul(pq(g, 0), lhsT=kT[:, h, jb(g)], rhs=kT[:, h, jb(g)],
                                                         start=True, stop=True), [("kT", h)], PKg(g, 0))
                    A("pe", lambda e, g=g, h=h: e.matmul(pq(g, 1), lhsT=qT[:, h, jb(g)], rhs=kT[:, h, jb(g)],
                                                         start=True, stop=True), [("qT", h), ("kT", h)], PKg(g, 1))
                for g in G:
                    hp = hd(g) % 2
                    A("act", lambda e, g=g, c=col("bEG", g): e.activation(out=kbe[g], in_=qs(g, 0), func=AF.Identity, scale=c),
                      QKg(g, 0) + ["bEG"], [("kbe", g)])
                    A("act", lambda e, g=g, c=col("kds", g): e.activation(out=kdc[g], in_=qs(g, 0), func=AF.Identity, scale=c),
                      QKg(g, 0) + ["kds"], [("kdc", g)])
                    A("act", lambda e, g=g, c=col("beta", g): e.activation(out=vb[g], in_=qs(g, 1), func=AF.Identity, scale=c),
                      QKg(g, 1) + ["beta"], [("vb", g)])
                    A("dve", lambda e, g=g, hp=hp, c=col("nbeta", g): e.scalar_tensor_tensor(
                        out=Wn[g][:, 0:128], in0=pq(g, 0), scalar=c, in1=dms[hp][:, jb(g)], op0=ALU.mult, op1=ALU.mult),
                        PKg(g, 0) + ["nbeta", ("dms", hp)], [("Wn", g)])
                    A("dve", lambda e, g=g, hp=hp: e.tensor_tensor(out=qkm[g], in0=pq(g, 1), in1=dmi[hp][:, jb(g)], op=ALU.mult),
                      PKg(g, 1) + [("dmi", hp)], [("qkm", g)])
                for g in G:
                    A("pe", lambda e, g=g: e.transpose(out=qs(g, 0), in_=Wn[g][:, 0:128], identity=ident_b),
                      [("Wn", g), "identb"], QKg(g, 0))
                    A("pe", lambda e, g=g: e.transpose(out=qs(g, 1), in_=qkm[g], identity=ident_b),
                      [("qkm", g), "identb"], QKg(g, 1))
                for g in G:
                    A("act", lambda e, g=g: e.copy(out=Wn[g][:, 128:256], in_=qs(g, 0)), QKg(g, 0), [("Wn", g)])
                    A("act", lambda e, g=g: e.copy(out=qkmT[g], in_=qs(g, 1)), QKg(g, 1), [("qkmT", g)])
                for g in G:
                    A("dve", lambda e, g=g: e.tensor_tensor(out=YY[g], in0=Wn[g], in1=maskPB[:, 0, :], op=ALU.mult),
                      [("Wn", g), "maskPB"], [("YY", g)])
                    A("dve", lambda e, g=g: e.tensor_tensor(out=TU[g], in0=YY[g], in1=ident2, op=ALU.add),
                      [("YY", g), "ident2"], [("TU", g)])

                def _level(lvl):
                    last = lvl == 6
                    for g in G:
                        if not last:
                            A("pe", lambda e, g=g: e.matmul(pq(g, 0), lhsT=Wn[g][:, 128:256], rhs=TU[g][:, 0:128],
                                                            start=True, stop=True), [("Wn", g), ("TU", g)], PKg(g, 0))
                        A("pe", lambda e, g=g: e.matmul(pq(g, 1), lhsT=Wn[g][:, 0:128], rhs=TU[g][:, 128:256],
                                                        start=True, stop=True), [("Wn", g), ("TU", g)], PKg(g, 1))
                    for g in G:
                        if last:
                            A("dve", lambda e, g=g: e.tensor_tensor(out=YY[g][:, 128:256], in0=pq(g, 1),
                                                                    in1=maskPB[:, lvl, 128:256], op=ALU.mult),
                              PKg(g, 1) + ["maskPB"], [("YY", g)])
                        else:
                            A("dve", lambda e, g=g: e.tensor_tensor(out=YY[g], in0=pq2(g), in1=maskPB[:, lvl, :], op=ALU.mult),
                              PKg(g) + ["maskPB"], [("YY", g)])
                    for g in G:
                        if not last:
                            A("pe", lambda e, g=g: e.matmul(pq(g, 0), lhsT=ident_b, rhs=TU[g][:, 0:128], start=True, stop=False),
                              [("TU", g), "identb"], PKg(g, 0))
                            A("pe", lambda e, g=g: e.matmul(pq(g, 0), lhsT=TU[g][:, 128:256], rhs=YY[g][:, 0:128], start=False, stop=True),
                              [("TU", g), ("YY", g)], PKg(g, 0))
                        A("pe", lambda e, g=g: e.matmul(pq(g, 1), lhsT=ident_b, rhs=TU[g][:, 128:256], start=True, stop=False),
                          [("TU", g), "identb"], PKg(g, 1))
                        A("pe", lambda e, g=g: e.matmul(pq(g, 1), lhsT=TU[g][:, 0:128], rhs=YY[g][:, 128:256], start=False, stop=True),
                          [("TU", g), ("YY", g)], PKg(g, 1))
                    for g in G:
                        if last:
                            A("act", lambda e, g=g: e.copy(out=TU[g][:, 128:256], in_=pq(g, 1)), PKg(g, 1), [("TU", g)])
                        else:
                            A("act", lambda e, g=g: e.copy(out=TU[g], in_=pq2(g)), PKg(g), [("TU", g)])
                for lvl_ in range(1, 7):
                    _level(lvl_)
                for g in G:
                    A("pe", lambda e, g=g: e.matmul(pq(g, 0), lhsT=TU[g][:, 128:256], rhs=vb[g], start=True, stop=True),
                      [("TU", g), ("vb", g)], PKg(g, 0))
                    A("pe", lambda e, g=g: e.matmul(pq(g, 1), lhsT=kbe[g], rhs=TU[g][:, 128:256], start=True, stop=True),
                      [("TU", g), ("kbe", g)], PKg(g, 1))
                for g in G:
                    A("act", lambda e, g=g: e.copy(out=uw[g], in_=pq2(g)), PKg(g), [("uw", g)])

            def _rec(h, hh, j):
                g = hh * 4 + j
                jb = slice(j * 128, (j + 1) * 128)
                A("pe", lambda e: e.matmul(ps[:, hh, 0:128], lhsT=uw[g][:, 128:256], rhs=Sbf[:, h, :], start=True, stop=True),
                  [("uw", g), ("Sb", h)], [("p", hh, 0)])
                A("dve", lambda e: e.tensor_tensor(out=vn[g], in0=uw[g][:, 0:128], in1=ps[:, hh, 0:128], op=ALU.subtract),
                  [("uw", g), ("p", hh, 0)], [("vn", g)])
                A("pe", lambda e: e.matmul(ps[:, 4 + hh, jb], lhsT=Sbf[:, h, :], rhs=qdT[:, h, jb], start=True, stop=False),
                  [("Sb", h), ("qdT", h)], [("p", 4 + hh, j)])
                A("pe", lambda e: e.matmul(ps[:, 4 + hh, jb], lhsT=vn[g], rhs=qkmT[g], start=False, stop=True),
                  [("vn", g), ("qkmT", g)], [("p", 4 + hh, j)])
                A("pe", lambda e: e.matmul(ps[:, hh, 128:256], lhsT=kdc[g], rhs=vn[g], start=True, stop=True),
                  [("kdc", g), ("vn", g)], [("p", hh, 1)])
                A("dve", lambda e, c=sm["dch"][:, j * 8 + h:j * 8 + h + 1]: e.scalar_tensor_tensor(
                    out=Sst[:, h, :], in0=Sst[:, h, :], scalar=c, in1=ps[:, hh, 128:256], op0=ALU.mult, op1=ALU.add),
                    [("S", h), "dch", ("p", hh, 1)], [("S", h)])
                A("dve", lambda e: e.tensor_copy(out=Sbf[:, h, :], in_=Sst[:, h, :]), [("S", h)], [("Sb", h)])

            def _headout(h, hh):
                yq = ysq[0]
                f0 = ft[hh]
                po, pn = 4 + hh, 2 + hh
                A("act", lambda e: e.activation(out=yq, in_=ps[:, po, :], func=AF.Square), PK(po), [("ysq", 0)])
                A("pe", lambda e: e.matmul(ps[:, pn, :], lhsT=ones_b, rhs=yq, start=True, stop=True),
                  ["onesb", ("ysq", 0)], PK(pn))
                A("dve", lambda e: e.tensor_scalar(out=f0, in0=ps[:, pn, :], scalar1=1.0 / 128, scalar2=1e-6,
                                                   op0=ALU.mult, op1=ALU.add), PK(pn), [("ft", hh)])
                A("act", lambda e: e.activation(out=f0, in_=f0, func=AF.Ln), [("ft", hh)], [("ft", hh)])
                A("act", lambda e: e.activation(out=f0, in_=f0, func=AF.Exp, scale=-0.5), [("ft", hh)], [("ft", hh)])
                A("dve", lambda e: e.scalar_tensor_tensor(
                    out=og[:, h, :], in0=ps[:, po, :], scalar=normw[:, 0:1], in1=f0, op0=ALU.mult, op1=ALU.mult),
                    PK(po) + ["normw", ("ft", hh)], [("og", 0)])
            for hA in range(0, 8, 2):
                for hh in range(2):
                    _s1_slice0(hA + hh, (hA + hh) % 2)
                    for i_ in range(3):
                        _s1_qkv(hA + hh, (hA + hh) % 2, i_)
                _chunks_pair(hA)
                for j in range(4):
                    for hh in range(2):
                        _rec(hA + hh, hh, j)
                for hh in range(2):
                    _headout(hA + hh, hh)
            DMA("sp", ogs[ti], og, r=[("og", 0)], w=[("ogs", ti)])

    def mixer_C(l, src, dst):
        ar.reset()
        S.barrier()
        mw = W["mix_w_in"][l]
        wz = ar.alloc([128, 8, 1024], BF16, "wz")
        wga = ar.alloc([128, 8, 1024], BF16, "wga")
        wgb = ar.alloc([128, 8, 1024], BF16, "wgb")
        gwo = ar.alloc([128, 8, 1024], BF16, "gwo")
        wmo = ar.alloc([128, 8, 1024], BF16, "wmo")
        wpre = ar.alloc([128, D], F32, "wpre")
        wpost = ar.alloc([128, D], F32, "wpost")
        xb = [ar.alloc([128, D], F32, "xb") for _ in range(6)]
        hb = [ar.alloc([128, D], BF16, "hb") for _ in range(2)]
        hT = ar.alloc([128, 8, 512], BF16, "hT")
        ogn = [ar.alloc([128, 8, 512], BF16, "ogn") for _ in range(2)]
        m2T = [ar.alloc([128, 8, 512], BF16, "m2T") for _ in range(2)]
        ogT = ar.alloc([128, 8, 512], BF16, "ogT")
        mT = ar.alloc([128, 8, 512], BF16, "mT")
        ft = [ar.alloc([128, 512], F32, "ft") for _ in range(6)]
        tmp = [ar.alloc([128, 512], F32, "tmp") for _ in range(2)]
        ss = ar.alloc([128, 8], F32, "ss")
        rstd = ar.alloc([128, 8], F32, "rstd")
        ss2 = ar.alloc([128, 8], F32, "ss2")
        rstd2 = ar.alloc([128, 4], F32, "rstd2")
        bcast_rows(wpre, W["mix_norm_pre"][l], "wpre")
        bcast_rows(wpost, W["mix_norm_post"][l], "wpost")
        load_weight(wz, mw[:, MW["z"]:MW["z"] + 1024], lambda k, ch: ("wz", k), 1024)
        load_weight(wga, mw[:, MW["ga"]:MW["ga"] + 1024], lambda k, ch: ("wga", k), 1024)
        load_weight(wgb, mw[:, MW["gb"]:MW["gb"] + 1024], lambda k, ch: ("wgb", k), 1024)
        load_weight(gwo, W["gdn_w_o"][l], lambda k, ch: ("gwo", k), 1024)
        load_weight(wmo, W["mix_w_out"][l], lambda k, ch: ("wmo", k), 1024)
        xi = [0]
        for ti in range(NT):
            on = ogn[ti % 2]
            mt = m2T[ti % 2]
            DMA("sp", on, ogs[ti], r=[("ogs", ti)], w=[("ogn", ti % 2)])
            DMA("sp", mt, m2s[ti], r=[("m2s", ti)], w=[("m2T", ti % 2)])
            slots = prenorm_hT(src, ti, xb, xi, 6, wpre, hb, hT, ss, rstd)
            for h in range(8):
                for k in range(8):
                    A("pe", lambda e, h=h, k=k: e.matmul(ps[:, h % 2, :], lhsT=wz[:, k, h * 128:(h + 1) * 128], rhs=hT[:, k, :],
                                                         start=(k == 0), stop=(k == 7)), [("wz", k)] + HT4, PK(h % 2))
                f0, f1 = ft[h % 2], ft[2 + h % 2]
                sigmoid_from(ps[:, h % 2, :], PK(h % 2), f0, ("ft", h % 2), f1, ("ft", 2 + h % 2))
                A("dve", lambda e, h=h, f1=f1: e.tensor_tensor(out=f1, in0=ps[:, h % 2, :], in1=f1, op=ALU.mult),
                  PK(h % 2) + [("ft", 2 + h % 2)], [("ft", 2 + h % 2)])
                A("dve", lambda e, h=h, f1=f1, on=on: e.tensor_tensor(out=ogT[:, h, :], in0=on[:, h, :], in1=f1, op=ALU.mult),
                  [("ogn", ti % 2), ("ft", 2 + h % 2)], [("ogT", h)])
            for fo in range(8):
                pya, pga = (2, 3) if fo % 2 == 0 else (0, 1)
                for h in range(8):
                    A("pe", lambda e, fo=fo, h=h, pya=pya: e.matmul(ps[:, pya, :], lhsT=gwo[:, h, fo * 128:(fo + 1) * 128], rhs=ogT[:, h, :],
                                                           start=(h == 0), stop=(h == 7)), [("gwo", h), ("ogT", h)], PK(pya))
                for k in range(8):
                    A("pe", lambda e, fo=fo, k=k, pga=pga: e.matmul(ps[:, pga, :], lhsT=wga[:, k, fo * 128:(fo + 1) * 128], rhs=hT[:, k, :],
                                                           start=(k == 0), stop=(k == 7)), [("wga", k)] + HT4, PK(pga))
                f0, f1, f2 = ft[fo % 2], ft[2 + fo % 2], ft[4 + fo % 2]
                sigmoid_from(ps[:, pga, :], PK(pga), f0, ("ft", fo % 2), f1, ("ft", 2 + fo % 2))
                A("dve", lambda e, f1=f1, f2=f2, pya=pya: e.tensor_tensor(out=f2, in0=ps[:, pya, :], in1=f1, op=ALU.mult),
                  PK(pya) + [("ft", 2 + fo % 2)], [("ft", 4 + fo % 2)])
                for k in range(8):
                    A("pe", lambda e, fo=fo, k=k, pga=pga: e.matmul(ps[:, pga, :], lhsT=wgb[:, k, fo * 128:(fo + 1) * 128], rhs=hT[:, k, :],
                                                           start=(k == 0), stop=(k == 7)), [("wgb", k)] + HT4, PK(pga))
                sigmoid_from(ps[:, pga, :], PK(pga), f0, ("ft", fo % 2), f1, ("ft", 2 + fo % 2))
                A("dve", lambda e, fo=fo, f1=f1, mt=mt: e.tensor_tensor(out=f1, in0=f1, in1=mt[:, fo, :], op=ALU.mult),
                  [("ft", 2 + fo % 2), ("m2T", ti % 2)], [("ft", 2 + fo % 2)])
                A("dve", lambda e, fo=fo, f1=f1, f2=f2: e.tensor_tensor(out=mT[:, fo, :], in0=f2, in1=f1, op=ALU.add),
                  [("ft", 4 + fo % 2), ("ft", 2 + fo % 2)], [("mT", fo)])
            for b in range(4):
                for half in range(2):
                    po = 4 + half
                    for fo in range(8):
                        A("pe", lambda e, fo=fo, b=b, half=half, po=po: e.matmul(
                            ps[:, po, :], lhsT=mT[:, fo, b * 128:(b + 1) * 128], rhs=wmo[:, fo, half * 512:(half + 1) * 512],
                            start=(fo == 0), stop=(fo == 7)), [("mT", fo), ("wmo", fo)], PK(po))
                postnorm_residual(b, slots[b], xb, wpost, tmp, ss2, rstd2, 1.0, dst, ti)

    cur = x_in
    for l in range(depth):
        if phases is None or "ffn1" in phases:
            ffn_phase(l, "ffn1", cur, y_out)
            cur = y_out
        if phases is None or "mix" in phases or "mixA" in phases:
            mixer_A(l, cur)
        if phases is None or "mix" in phases or "mixB" in phases:
            mixer_B(l, cur)
        if phases is None or "mix" in phases or "mixC" in phases:
            mixer_C(l, cur, y_out)
            cur = y_out
        if phases is None or "ffn2" in phases:
            ffn_phase(l, "ffn2", cur, y_out)
            cur = y_out

    S.barrier()
    S.add("sp", lambda e: e.nop())
    stack = ExitStack()
    S.emit_all(stack)
    stack.close()
    return nc


WEIGHT_SHAPES = {
    "ffn1_norm_pre": (D,), "ffn1_norm_post": (D,), "ffn1_w_in": (D, 2 * DFF), "ffn1_w_out": (DFF, D),
    "mix_norm_pre": (D,), "mix_norm_post": (D,), "mix_w_in": (D, PIN),
    "gdn_conv_w": (4, 3072), "gdn_a_log": (8,), "gdn_dt_bias": (8,), "gdn_norm_w": (128,), "gdn_w_o": (D, D),
    "cnv_pw1_b": (2048,), "cnv_dw_w": (31, D), "cnv_dw_b": (D,), "cnv_ln_g": (D,), "cnv_ln_b": (D,),
    "cnv_w_o": (D, D), "cnv_b_o": (D,), "mix_w_out": (D, D),
    "ffn2_norm_pre": (D,), "ffn2_norm_post": (D,), "ffn2_w_in": (D, 2 * DFF), "ffn2_w_out": (DFF, D),
}
CONST_SHAPES = {"c_ident": [128, 128], "c_ltri": [128, 128], "c_ones": [128, 128],
                "c_masks": [128, 512], "c_maski": [128, 512], "c_mpb": [128, 1792], "c_id2": [128, 256]}


def consts():
    i = np.arange(128)
    ltri = (i[:, None] <= i[None, :]).astype(np.float32)
    ms = (i[None, :] < i[:, None]).astype(np.float32)
    mi = (i[None, :] <= i[:, None]).astype(np.float32)
    mp = np.zeros((128, 7, 128), np.float32)
    for lv in range(7):
        blk = i // (1 << lv)
        mp[:, lv, :] = ((blk[:, None] // 2 == blk[None, :] // 2) & (blk[:, None] % 2 == 1)
                        & (blk[None, :] % 2 == 0)).astype(np.float32)
    mb = np.ascontiguousarray(mp.transpose(2, 1, 0))
    mpb = np.concatenate([mp, mb], axis=2)
    return {"c_mpb": np.ascontiguousarray(mpb).reshape(128, 1792),
            "c_id2": np.tile(np.eye(128, dtype=np.float32), (1, 2)),
            "c_ident": np.eye(128, dtype=np.float32), "c_ltri": ltri,
            "c_ones": np.ones((128, 128), np.float32),
            "c_masks": np.tile(ms, (1, 4)), "c_maski": np.tile(mi, (1, 4))}


def kernel(**inputs):
    x = np.ascontiguousarray(inputs["x"], dtype=np.float32)
    B, T, _ = x.shape
    nc = build_program(T, DEPTH)
    shared = {k: np.ascontiguousarray(inputs[k], dtype=np.float32) for k in WEIGHT_SHAPES}
    shared.update(consts())
    in_maps = []
    for b in range(B):
        m = dict(shared)
        m["x"] = x[b]
        in_maps.append(m)
    res = run_bass_kernel_spmd(nc, in_maps, core_ids=list(range(B)))
    return np.stack([np.asarray(r["y"]).reshape(T, D) for r in res.results], axis=0).astype(np.float32)
```

```python
import numpy as np
import concourse.bass as bass
import concourse.mybir as mybir
from concourse.bass_utils import run_bass_kernel_spmd
from contextlib import ExitStack

F32 = mybir.dt.float32
BF16 = mybir.dt.bfloat16
AF = mybir.ActivationFunctionType
ALU = mybir.AluOpType

D = 1024
DFF = 2816
NH = 8
DEPTH = 2
PIN = 8208
NDMA_SEMS = 24


class Op:
    __slots__ = ("eng", "emit", "dma", "deps", "has_dep", "token", "prev_token", "idx")

    def __init__(self, eng, emit, dma):
        self.eng = eng
        self.emit = emit
        self.dma = dma
        self.deps = ()
        self.has_dep = False
        self.token = None
        self.prev_token = None


class Sched:
    ENGS = ("pe", "act", "dve", "pool", "sp")

    def __init__(self, nc):
        self.nc = nc
        self.eng_ops = {e: [] for e in self.ENGS}
        self.last_w = {}
        self.readers = {}
        self.pending = {e: [] for e in self.ENGS}
        self.last_compute = {}
        self.dmas = []
        self.live_dmas = []
        self.nops = 0
        self.bank_rd = {}

    @staticmethod
    def _norm(keys):
        return [k[:2] if (isinstance(k, tuple) and len(k) == 3 and k[0] in ("p", "q")) else k for k in keys]

    def add(self, eng, emit, reads=(), writes=(), dma=False):
        reads = self._norm(reads)
        writes = self._norm(writes)
        op = Op(eng, emit, dma)
        op.idx = self.nops
        self.nops += 1
        deps = {}
        psum_banks = set()

        def add_dep(d):
            if d.dma:
                deps[("d", d.idx)] = d
            else:
                if eng == "pe" and d.eng == "pe":
                    return
                cur = deps.get(d.eng)
                if cur is None or cur.idx < d.idx:
                    deps[d.eng] = d

        for r in reads:
            w = self.last_w.get(r)
            if w is not None:
                add_dep(w)
            if isinstance(r, tuple) and r[0] in ("p", "q"):
                bk = (r[0], r[1])
                last = self.bank_rd.get(bk)
                if last is not None and last.eng != eng:
                    add_dep(last)
                psum_banks.add(bk)
        for w in writes:
            lw = self.last_w.get(w)
            if lw is not None:
                add_dep(lw)
            rd = self.readers.get(w)
            if rd:
                for d in rd.values():
                    add_dep(d)
        for d in self.pending[eng]:
            add_dep(d)
        self.pending[eng] = []
        op.deps = tuple(deps.values())
        for d in op.deps:
            d.has_dep = True
        for bk in psum_banks:
            self.bank_rd[bk] = op
        for r in reads:
            rd = self.readers.setdefault(r, {})
            rd[("d", op.idx) if dma else eng] = op
        for w in writes:
            self.last_w[w] = op
            self.readers[w] = {}
        self.eng_ops[eng].append(op)
        if dma:
            self.dmas.append(op)
            self.live_dmas.append(op)
        else:
            self.last_compute[eng] = op
        return op

    def barrier(self):
        toks = list(self.last_compute.values()) + list(self.live_dmas)
        for e in self.ENGS:
            self.pending[e] = list(toks)
        self.live_dmas = []
        self.last_w = {}
        self.readers = {}
        self.bank_rd = {}

    def emit_all(self, stack):
        nc = self.nc
        sems = {}
        for e in ("pe", "act", "dve", "pool"):
            sems[e] = stack.enter_context(nc.semaphore("c_" + e))
            cnt = 0
            for op in self.eng_ops[e]:
                if op.dma:
                    continue
                if op.has_dep:
                    cnt += 1
                    op.token = (sems[e], cnt)
        dsems = [stack.enter_context(nc.semaphore("d%d" % i)) for i in range(NDMA_SEMS)]
        counts = [0] * NDMA_SEMS
        for i, op in enumerate(self.dmas):
            s = i % NDMA_SEMS
            n = counts[s]
            if n > 0:
                op.prev_token = (dsems[s], 16 * n)
            counts[s] = n + 1
            op.token = (dsems[s], 16 * (n + 1))
        block = stack.enter_context(nc.Block())

        def run(eng_name):
            def body(eng):
                waited = {}
                for op in self.eng_ops[eng_name]:
                    toks = [d.token for d in op.deps]
                    if op.prev_token is not None:
                        toks.append(op.prev_token)
                    for (sem, val) in toks:
                        key = id(sem)
                        if waited.get(key, 0) < val:
                            eng.wait_ge(sem, val)
                            waited[key] = val
                    inst = op.emit(eng)
                    if op.dma:
                        inst.then_inc(op.token[0], 16)
                    elif op.has_dep:
                        inst.then_inc(op.token[0], 1)
            return body

        block.tensor(run("pe"))
        block.scalar(run("act"))
        block.vector(run("dve"))
        block.gpsimd(run("pool"))
        block.sync(run("sp"))


class Arena:
    def __init__(self, nc, limit):
        self.nc = nc
        self.limit = limit
        self.base = 20608
        self.off = 20608
        self.n = 0

    def alloc(self, shape, dtype, name="t"):
        esz = 2 if dtype == BF16 else 4
        per_part = esz
        for s in shape[1:]:
            per_part *= s
        off = (self.off + 63) // 64 * 64
        assert off + per_part <= self.limit, ("SBUF overflow", name, off, per_part, self.limit)
        self.off = off + per_part
        self.n += 1
        h = self.nc.alloc_sbuf_tensor_at("%s_%d" % (name, self.n), list(shape), dtype, offset=off)
        return h.ap()

    def mark_persistent(self):
        self.base = self.off

    def reset(self):
        self.off = self.base


MW = {"qkv": 0, "z": 3072, "ba": 4096, "glu_a": 4112, "glu_g": 5136, "ga": 6160, "gb": 7184}


def build_program(T, depth, phases=None):
    nc = bass.Bass("TRN2", target_bir_lowering=False)
    NT = T // 512

    def din(name, shape):
        return nc.dram_tensor(name, list(shape), F32, kind="ExternalInput").ap()

    x_in = din("x", [T, D])
    y_out = nc.dram_tensor("y", [T, D], F32, kind="ExternalOutput").ap()
    W = {}
    for nm, shp in WEIGHT_SHAPES.items():
        W[nm] = din(nm, (depth,) + tuple(shp))
    C = {}
    for nm, shp in CONST_SHAPES.items():
        C[nm] = din(nm, shp)
    m2s = nc.dram_tensor("m2s", [NT, 128, 8, 512], BF16, kind="Internal").ap()
    ogs = nc.dram_tensor("ogs", [NT, 128, 8, 512], BF16, kind="Internal").ap()

    S = Sched(nc)
    ar = Arena(nc, nc.SBUF_PARTITION_SIZE_BYTES)

    def A(eng, fn, r=(), w=()):
        return S.add(eng, fn, reads=r, writes=w)

    def DMA(q, out, in_, r=(), w=()):
        return S.add(q, lambda e: e.dma_start(out=out, in_=in_), reads=r, writes=w, dma=True)

    def PK(b):
        return [("p", b, s) for s in range(4)]

    def QK(b):
        return [("q", b, s) for s in range(8)]

    ident_f = ar.alloc([128, 128], F32, "identf")
    ident_b = ar.alloc([128, 128], BF16, "identb")
    ltri_f = ar.alloc([128, 128], F32, "ltri")
    ones_f = ar.alloc([128, 128], F32, "onesf")
    ones_b = ar.alloc([128, 128], BF16, "onesb")
    mask_s = ar.alloc([128, 512], F32, "masks")
    mask_i = ar.alloc([128, 512], F32, "maski")
    cm05 = ar.alloc([128, 8], F32, "cm05")
    DMA("sp", ident_f, C["c_ident"], w=["identf"])
    DMA("sp", ltri_f, C["c_ltri"], w=["ltri"])
    DMA("sp", ones_f, C["c_ones"], w=["onesf"])
    DMA("sp", mask_s, C["c_masks"], w=["masks"])
    DMA("sp", mask_i, C["c_maski"], w=["maski"])
    A("dve", lambda e: e.tensor_copy(out=ident_b, in_=ident_f), ["identf"], ["identb"])
    A("dve", lambda e: e.tensor_copy(out=ones_b, in_=ones_f), ["onesf"], ["onesb"])
    A("pool", lambda e: e.memset(cm05, -0.5), [], ["cm05"])
    ar.mark_persistent()

    ps = nc.alloc_psum_tensor("ps", [128, 6, 512], F32).ap()
    psb = nc.alloc_psum_tensor("psb", [128, 2, 1024], BF16).ap()

    def bcast_rows(dst, src_row, key):
        DMA("sp", dst, src_row.partition_broadcast(128), w=[key])

    def load_weight(dst, src2d, keyfn, ncols, chunk=1024):
        for c0 in range(0, ncols, chunk):
            c1 = min(ncols, c0 + chunk)
            for k in range(8):
                DMA("pool", dst[:, k, c0:c1], src2d[k * 128:(k + 1) * 128, c0:c1], w=[keyfn(k, c0 // chunk)])

    def load_cols(dst, dkey, entries):
        raw = ar.alloc([128, 128], F32, "raw")
        r0 = 0
        allrows = []
        for src in entries:
            R = src.shape[0]
            off = 0
            while off < R:
                n = min(R - off, 128 - (r0 % 128)) if (r0 % 128) else min(R - off, 128)
                allrows.append((src[off:off + n, :], r0, n))
                r0 += n
                off += n
        total = r0
        g = 0
        while g * 128 < total:
            rows = [(a, r, n) for (a, r, n) in allrows if r // 128 == g]
            nr = sum(n for (_, _, n) in rows)
            key = ("raw", g)
            for (a, r, n) in rows:
                DMA("sp", raw[r % 128:r % 128 + n, :], a, w=[key])
            A("pe", lambda e, nr=nr: e.transpose(out=ps[:, 5, 0:nr], in_=raw[0:nr, :], identity=ident_f[0:nr, 0:nr]),
              [key, "identf"], PK(5))
            A("dve", lambda e, nr=nr, g=g: e.tensor_copy(out=dst[:, g * 128:g * 128 + nr], in_=ps[:, 5, 0:nr]),
              PK(5), [dkey])
            g += 1
            if g * 128 < total:
                S.last_w[("raw", g)] = S.last_compute["pe"]
        return total

    def prenorm_hT(src, ti, xb, xi, nbuf, wpre, hb, hT, ss, rstd, act_rsqrt=False):
        slots = []
        for b in range(4):
            sl = xi[0] % nbuf
            xi[0] += 1
            slots.append(sl)
            DMA("sp", xb[sl], src[ti * 512 + b * 128:ti * 512 + (b + 1) * 128, :],
                r=[("xd", ti, b)], w=[("xb", sl)])
            hbb = hb[b % 2]
            hk = ("hb", id(hbb))
            A("act", lambda e, sl=sl, hbb=hbb, b=b: e.activation(
                out=hbb, in_=xb[sl], func=AF.Square, accum_out=ss[:, b:b + 1]),
                [("xb", sl)], [hk, ("ss", b)])
            if act_rsqrt:
                A("act", lambda e, b=b: e.activation(out=rstd[:, b:b + 1], in_=ss[:, b:b + 1], func=AF.Ln,
                                                     scale=1.0 / D, bias=1e-6), [("ss", b)], [("rstd", b)])
                A("act", lambda e, b=b: e.activation(out=rstd[:, b:b + 1], in_=rstd[:, b:b + 1], func=AF.Exp,
                                                     scale=-0.5), [("rstd", b)], [("rstd", b)])
            else:
                A("dve", lambda e, b=b: e.tensor_scalar(
                    out=rstd[:, b:b + 1], in0=ss[:, b:b + 1], scalar1=1.0 / D, scalar2=1e-6,
                    op0=ALU.mult, op1=ALU.add), [("ss", b)], [("rstd", b)])
                A("pool", lambda e, b=b: e.tensor_tensor(
                    out=rstd[:, b:b + 1], in0=rstd[:, b:b + 1], in1=cm05[:, 0:1], op=ALU.pow),
                    [("rstd", b), "cm05"], [("rstd", b)])
            A("dve", lambda e, sl=sl, hbb=hbb, b=b: e.scalar_tensor_tensor(
                out=hbb, in0=xb[sl], scalar=rstd[:, b:b + 1], in1=wpre, op0=ALU.mult, op1=ALU.mult),
                [("xb", sl), ("rstd", b), "wpre"], [hk])
            q = b % 2
            for k in range(8):
                A("pe", lambda e, hbb=hbb, k=k, q=q: e.transpose(
                    out=psb[:, q, k * 128:(k + 1) * 128], in_=hbb[:, k * 128:(k + 1) * 128],
                    identity=ident_b), [hk, "identb"], [("q", q, k)])
            dstv = hT[:, :, b * 128:(b + 1) * 128]
            srcv = psb[:, q, :].rearrange("p (k t) -> p k t", k=8)
            if b % 2 == 0:
                A("act", lambda e, dstv=dstv, srcv=srcv: e.copy(out=dstv, in_=srcv), QK(q), [("hT", b)])
            else:
                A("dve", lambda e, dstv=dstv, srcv=srcv: e.tensor_copy(out=dstv, in_=srcv), QK(q), [("hT", b)])
        return slots

    HT4 = [("hT", b) for b in range(4)]

    def postnorm_residual(b, sl, xb, wpost, tmp, ss2, rstd2, scale, dst, ti):
        t0 = ti * 512
        for half in range(2):
            po = 4 + half
            A("act", lambda e, half=half, po=po: e.activation(
                out=tmp[half], in_=ps[:, po, :], func=AF.Square,
                accum_out=ss2[:, 2 * b + half:2 * b + half + 1]),
                PK(po), [("tmp", half), ("ss2", b, half)])
        A("dve", lambda e: e.tensor_tensor(
            out=rstd2[:, b:b + 1], in0=ss2[:, 2 * b:2 * b + 1], in1=ss2[:, 2 * b + 1:2 * b + 2],
            op=ALU.add), [("ss2", b, 0), ("ss2", b, 1)], [("rstd2", b)])
        inv = 1.0 / (scale * scale)
        A("dve", lambda e: e.tensor_scalar(
            out=rstd2[:, b:b + 1], in0=rstd2[:, b:b + 1], scalar1=inv / D, scalar2=inv * 1e-6,
            op0=ALU.mult, op1=ALU.add), [("rstd2", b)], [("rstd2", b)])
        A("pool", lambda e: e.tensor_tensor(
            out=rstd2[:, b:b + 1], in0=rstd2[:, b:b + 1], in1=cm05[:, 0:1], op=ALU.pow),
            [("rstd2", b), "cm05"], [("rstd2", b)])
        for half in range(2):
            po = 4 + half
            A("dve", lambda e, half=half, po=po: e.scalar_tensor_tensor(
                out=tmp[half], in0=ps[:, po, :], scalar=rstd2[:, b:b + 1],
                in1=wpost[:, half * 512:(half + 1) * 512], op0=ALU.mult, op1=ALU.mult),
                PK(po) + [("rstd2", b), "wpost"], [("tmp", half)])
            A("pool", lambda e, half=half: e.tensor_tensor(
                out=xb[sl][:, half * 512:(half + 1) * 512], in0=xb[sl][:, half * 512:(half + 1) * 512],
                in1=tmp[half], op=ALU.add), [("tmp", half), ("xb", sl)], [("xb", sl)])
        DMA("sp", dst[t0 + b * 128:t0 + (b + 1) * 128, :], xb[sl], r=[("xb", sl)], w=[("xd", ti, b)])

    def load_x(src, ti, xb, xi, nbuf):
        slots = []
        for b in range(4):
            sl = xi[0] % nbuf
            xi[0] += 1
            slots.append(sl)
            DMA("sp", xb[sl], src[ti * 512 + b * 128:ti * 512 + (b + 1) * 128, :],
                r=[("xd", ti, b)], w=[("xb", sl)])
        return slots

    def sigmoid_from(src, skeys, e_t, ekey, r_t, rkey, negbias=None, bkeys=()):
        if negbias is None:
            A("act", lambda e: e.activation(out=e_t, in_=src, func=AF.Exp, scale=-1.0), skeys, [ekey])
        else:
            A("act", lambda e: e.activation(out=e_t, in_=src, func=AF.Exp, scale=-1.0, bias=negbias),
              list(skeys) + list(bkeys), [ekey])
        A("act", lambda e: e.activation(out=e_t, in_=e_t, func=AF.Ln, bias=1.0), [ekey], [ekey])
        A("act", lambda e: e.activation(out=r_t, in_=e_t, func=AF.Exp, scale=-1.0), [ekey], [rkey])

    def ffn_phase(l, which, src, dst):
        ar.reset()
        S.barrier()
        w_in = W[which + "_w_in"]
        w_out = W[which + "_w_out"]
        win_sb = ar.alloc([128, 8, 2 * DFF], BF16, "win")
        wout_sb = ar.alloc([128, 22, D], BF16, "wout")
        wpre = ar.alloc([128, D], F32, "wpre")
        wpost = ar.alloc([128, D], F32, "wpost")
        xb = [ar.alloc([128, D], F32, "xb") for _ in range(4)]
        _hbsingle = True
        hb = [ar.alloc([128, D], BF16, "hb")] * 2
        hT = ar.alloc([128, 8, 512], BF16, "hT")
        inter = ar.alloc([128, 22, 512], BF16, "inter")
        sg = [ar.alloc([128, 512], BF16, "sg") for _ in range(2)]
        tmp = [ar.alloc([128, 512], F32, "tmp") for _ in range(2)]
        ss = ar.alloc([128, 8], F32, "ss")
        rstd = ar.alloc([128, 8], F32, "rstd")
        ss2 = ar.alloc([128, 8], F32, "ss2")
        rstd2 = ar.alloc([128, 4], F32, "rstd2")
        bcast_rows(wpre, W[which + "_norm_pre"][l], "wpre")
        bcast_rows(wpost, W[which + "_norm_post"][l], "wpost")
        for c0 in (0, 2 * 1408, 1408, 3 * 1408):
            for k in range(8):
                DMA("pool", win_sb[:, k, c0:c0 + 1408], w_in[l, k * 128:(k + 1) * 128, c0:c0 + 1408],
                    w=[("win", k, c0 // 1408)])
        for c in range(22):
            DMA("pool", wout_sb[:, c, :], w_out[l, c * 128:(c + 1) * 128, :], w=[("wout", c)])
        xi = [0]
        for ti in range(NT):
            slots = prenorm_hT(src, ti, xb, xi, 4, wpre, hb, hT, ss, rstd)
            for c in range(22):
                pa = c % 2
                pb = 2 + c % 2
                for k in range(8):
                    A("pe", lambda e, c=c, k=k, pa=pa: e.matmul(
                        ps[:, pa, :], lhsT=win_sb[:, k, c * 128:(c + 1) * 128], rhs=hT[:, k, :],
                        start=(k == 0), stop=(k == 7)),
                        [("win", k, (c * 128) // 1408), ("win", k, (c * 128 + 127) // 1408)] + HT4, PK(pa))
                for k in range(8):
                    A("pe", lambda e, c=c, k=k, pb=pb: e.matmul(
                        ps[:, pb, :], lhsT=win_sb[:, k, DFF + c * 128:DFF + (c + 1) * 128], rhs=hT[:, k, :],
                        start=(k == 0), stop=(k == 7)),
                        [("win", k, (DFF + c * 128) // 1408), ("win", k, (DFF + c * 128 + 127) // 1408)] + HT4, PK(pb))
                A("act", lambda e, c=c, pa=pa: e.activation(out=sg[c % 2], in_=ps[:, pa, :], func=AF.Silu),
                  PK(pa), [("sg", c % 2)])
                A("dve", lambda e, c=c, pb=pb: e.tensor_tensor(
                    out=inter[:, c, :], in0=ps[:, pb, :], in1=sg[c % 2], op=ALU.mult),
                    PK(pb) + [("sg", c % 2)], [("inter", c)])
            for b in range(4):
                for half in range(2):
                    po = 4 + half
                    for c in range(22):
                        A("pe", lambda e, c=c, b=b, half=half, po=po: e.matmul(
                            ps[:, po, :], lhsT=inter[:, c, b * 128:(b + 1) * 128],
                            rhs=wout_sb[:, c, half * 512:(half + 1) * 512],
                            start=(c == 0), stop=(c == 21)), [("inter", c), ("wout", c)], PK(po))
                postnorm_residual(b, slots[b], xb, wpost, tmp, ss2, rstd2, 0.5, dst, ti)

    def mixer_A(l, src):
        ar.reset()
        S.barrier()
        mw = W["mix_w_in"][l]
        wglu = ar.alloc([128, 8, 2048], BF16, "wglu")
        cwo = ar.alloc([128, 8, 1024], BF16, "cwo")
        dg = ar.alloc([128, 8, 31, 128], BF16, "dg31")
        wpre = ar.alloc([128, D], F32, "wpre")
        cols = ar.alloc([128, 512], F32, "cols")
        ncols = ar.alloc([128, 16], F32, "ncols")
        xb = [ar.alloc([128, D], F32, "xb") for _ in range(2)]
        hb = [ar.alloc([128, D], BF16, "hb")] * 2
        hT = ar.alloc([128, 8, 512], BF16, "hT")
        hg1 = ar.alloc([128, 8, 544], BF16, "hglu")
        c0f = ar.alloc([128, 8, 512], F32, "c0f")
        cbf = [ar.alloc([128, 512], BF16, "cbf")] * 2
        csq = [ar.alloc([128, 512], BF16, "csq")] * 2
        cT = ar.alloc([128, 8, 512], BF16, "cT")
        m2T = [ar.alloc([128, 8, 512], BF16, "m2T") for _ in range(1)]
        ft = [ar.alloc([128, 512], F32, "ft") for _ in range(5)]
        mu = ar.alloc([128, 512], F32, "mu")
        rsd = ar.alloc([128, 512], F32, "rsd")
        ss = ar.alloc([128, 8], F32, "ss")
        rstd = ar.alloc([128, 8], F32, "rstd")
        bcast_rows(wpre, W["mix_norm_pre"][l], "wpre")
        load_weight(wglu, mw[:, MW["glu_a"]:MW["glu_a"] + 2048], lambda k, ch: ("wglu", k, ch), 2048)
        load_weight(cwo, W["cnv_w_o"][l], lambda k, ch: ("cwo", k), 1024)
        ent = [W["cnv_pw1_b"][l].rearrange("(c p) -> c p", p=128),
               W["cnv_dw_b"][l].rearrange("(c p) -> c p", p=128),
               W["cnv_ln_g"][l].rearrange("(c p) -> c p", p=128),
               W["cnv_ln_b"][l].rearrange("(c p) -> c p", p=128),
               W["cnv_b_o"][l].rearrange("(c p) -> c p", p=128),
               W["cnv_dw_w"][l].rearrange("j (c p) -> (j c) p", p=128)]
        load_cols(cols, "cols", ent)
        PW, DWB, LNG, LNB, BO, DWW = 0, 16, 24, 32, 40, 48
        A("dve", lambda e: e.tensor_scalar(out=ncols, in0=cols[:, 0:16], scalar1=-1.0, scalar2=None, op0=ALU.mult),
          ["cols"], ["ncols"])
        for c in range(8):
            for j in range(31):
                A("dve" if (c * 31 + j) % 2 else "pool", lambda e, c=c, j=j: e.tensor_scalar(
                    out=dg[:, c, j, :], in0=ident_f, scalar1=cols[:, DWW + j * 8 + c:DWW + j * 8 + c + 1],
                    scalar2=None, op0=ALU.mult), ["cols", "identf"], [("dg", c)])
        A("pool", lambda e: e.memset(hg1[:, :, 0:32], 0.0), [], [("hgh", c) for c in range(8)])
        xi = [0]
        for ti in range(NT):
            cur = hg1
            slots = prenorm_hT(src, ti, xb, xi, 2, wpre, hb, hT, ss, rstd, act_rsqrt=True)
            for c in range(8):
                for k in range(8):
                    A("pe", lambda e, c=c, k=k: e.matmul(ps[:, 0, :], lhsT=wglu[:, k, c * 128:(c + 1) * 128],
                                                         rhs=hT[:, k, :], start=(k == 0), stop=(k == 7)),
                      [("wglu", k, 0)] + HT4, PK(0))
                for k in range(8):
                    A("pe", lambda e, c=c, k=k: e.matmul(ps[:, 1, :], lhsT=wglu[:, k, 1024 + c * 128:1024 + (c + 1) * 128],
                                                         rhs=hT[:, k, :], start=(k == 0), stop=(k == 7)),
                      [("wglu", k, 1)] + HT4, PK(1))
                f0 = ft[c % 2]
                f1 = ft[2 + c % 2]
                sigmoid_from(ps[:, 1, :], PK(1), f0, ("ft", c % 2), f1, ("ft", 2 + c % 2),
                             negbias=ncols[:, 8 + c:9 + c], bkeys=["ncols"])
                if ti > 0:
                    A("pool", lambda e, c=c, cur=cur: e.tensor_copy(out=cur[:, c, 0:32], in_=cur[:, c, 512:544]),
                      [("hgd", c)], [("hgh", c)])
                A("dve", lambda e, c=c, f1=f1, cur=cur: e.scalar_tensor_tensor(
                    out=cur[:, c, 32:544], in0=ps[:, 0, :], scalar=cols[:, PW + c:PW + c + 1], in1=f1,
                    op0=ALU.add, op1=ALU.mult), PK(0) + ["cols", ("ft", 2 + c % 2)], [("hgd", c)])
            for c in range(8):
                pb = 2 + c % 2
                for j in range(31):
                    A("pe", lambda e, c=c, j=j, pb=pb, cur=cur: e.matmul(
                        ps[:, pb, :], lhsT=dg[:, c, j, :], rhs=cur[:, c, 2 + j:2 + j + 512],
                        start=(j == 0), stop=(j == 30)), [("dg", c), ("hgd", c), ("hgh", c)], PK(pb))
                A("act", lambda e, c=c, pb=pb: e.activation(out=c0f[:, c, :], in_=ps[:, pb, :], func=AF.Identity,
                                                            bias=cols[:, DWB + c:DWB + c + 1]),
                  PK(pb) + ["cols"], [("c0f", c)])
                A("dve", lambda e, c=c: e.tensor_copy(out=cbf[c % 2], in_=c0f[:, c, :]), [("c0f", c)], [("cbf", 0)])
                A("act", lambda e, c=c: e.activation(out=csq[c % 2], in_=c0f[:, c, :], func=AF.Square),
                  [("c0f", c)], [("csq", 0)])
                A("pe", lambda e, c=c: e.matmul(ps[:, 4, :], lhsT=ones_b, rhs=cbf[c % 2], start=(c == 0), stop=(c == 7)),
                  ["onesb", ("cbf", 0)], PK(4))
                A("pe", lambda e, c=c: e.matmul(ps[:, 5, :], lhsT=ones_b, rhs=csq[c % 2], start=(c == 0), stop=(c == 7)),
                  ["onesb", ("csq", 0)], PK(5))
            A("dve", lambda e: e.tensor_scalar(out=mu, in0=ps[:, 4, :], scalar1=1.0 / D, scalar2=None, op0=ALU.mult),
              PK(4), ["mu"])
            A("dve", lambda e: e.tensor_tensor(out=rsd, in0=mu, in1=mu, op=ALU.mult), ["mu"], ["rsd"])
            A("dve", lambda e: e.scalar_tensor_tensor(out=rsd, in0=ps[:, 5, :], scalar=1.0 / D, in1=rsd,
                                                      op0=ALU.mult, op1=ALU.subtract), PK(5) + ["rsd"], ["rsd"])
            A("dve", lambda e: e.tensor_scalar(out=rsd, in0=rsd, scalar1=1e-5, scalar2=None, op0=ALU.add), ["rsd"], ["rsd"])
            A("act", lambda e: e.activation(out=rsd, in_=rsd, func=AF.Ln), ["rsd"], ["rsd"])
            A("act", lambda e: e.activation(out=rsd, in_=rsd, func=AF.Exp, scale=-0.5), ["rsd"], ["rsd"])
            for c in range(8):
                f0 = ft[c % 2]
                f1 = ft[2 + c % 2]
                f2 = ft[4]
                k0, k1, k2 = ("ft", c % 2), ("ft", 2 + c % 2), ("ft", 4)
                A("dve", lambda e, c=c, f0=f0: e.tensor_tensor(out=f0, in0=c0f[:, c, :], in1=mu, op=ALU.subtract),
                  [("c0f", c), "mu"], [k0])
                A("dve", lambda e, f0=f0: e.tensor_tensor(out=f0, in0=f0, in1=rsd, op=ALU.mult), [k0, "rsd"], [k0])
                A("act", lambda e, c=c, f0=f0: e.activation(out=f0, in_=f0, func=AF.Identity,
                                                            scale=cols[:, LNG + c:LNG + c + 1],
                                                            bias=cols[:, LNB + c:LNB + c + 1]), [k0, "cols"], [k0])
                sigmoid_from(f0, [k0], f1, k1, f2, k2)
                A("dve", lambda e, c=c, f0=f0, f2=f2: e.tensor_tensor(out=cT[:, c, :], in0=f0, in1=f2, op=ALU.mult),
                  [k0, k2], [("cT", c)])
            mt = m2T[0]
            for fo in range(8):
                for c in range(8):
                    A("pe", lambda e, fo=fo, c=c: e.matmul(ps[:, fo % 2, :], lhsT=cwo[:, c, fo * 128:(fo + 1) * 128],
                                                           rhs=cT[:, c, :], start=(c == 0), stop=(c == 7)),
                      [("cwo", c), ("cT", c)], PK(fo % 2))
                A("act", lambda e, fo=fo, mt=mt: e.activation(out=mt[:, fo, :], in_=ps[:, fo % 2, :], func=AF.Identity,
                                                              bias=cols[:, BO + fo:BO + fo + 1]),
                  PK(fo % 2) + ["cols"], [("m2T", 0)])
            DMA("sp", m2s[ti], mt, r=[("m2T", 0)], w=[("m2s", ti)])

    def mixer_B(l, src):
        ar.reset()
        S.barrier()
        mw = W["mix_w_in"][l]
        wqkv = ar.alloc([128, 8, 3072], BF16, "wqkv")
        wba = ar.alloc([128, 8, 16], BF16, "wba")
        dg = ar.alloc([128, 24, 4, 128], BF16, "dg4")
        wpre = ar.alloc([128, D], F32, "wpre")
        cols = ar.alloc([128, 128], F32, "cols")
        xb = [ar.alloc([128, D], F32, "xb") for _ in range(2)]
        hb = [ar.alloc([128, D], BF16, "hb")] * 2
        hT = ar.alloc([128, 8, 512], BF16, "hT")
        ss = ar.alloc([128, 8], F32, "ss")
        rstd = ar.alloc([128, 8], F32, "rstd")
        qT = ar.alloc([128, 8, 512], BF16, "qT")
        qdT = ar.alloc([128, 8, 512], BF16, "qdT")
        kT = ar.alloc([128, 8, 512], BF16, "kT")
        vT = ar.alloc([128, 8, 512], BF16, "vT")
        ogT = [ar.alloc([128, 8, 512], BF16, "ogT")] * 2
        pch = ar.alloc([128, 24, 4], BF16, "pch")
        pc = [ar.alloc([128, 516], BF16, "pc") for _ in range(2)]
        ft = [ar.alloc([128, 512], F32, "ft") for _ in range(6)]
        ysq = [ar.alloc([128, 512], BF16, "ysq")] * 2
        ebc = [ar.alloc([128, 512], F32, "ebc")] * 2
        dms = [ar.alloc([128, 512], BF16, "dms") for _ in range(2)]
        dmi = [ar.alloc([128, 512], BF16, "dmi") for _ in range(2)]
        dtb = ar.alloc([128, 32], F32, "dtb")
        nega = ar.alloc([128, 32], F32, "nega")
        sm = {nm: ar.alloc([128, 32], F32, nm) for nm in
              ("et", "beta", "nbeta", "t", "g", "Gc", "nGc", "Gt", "eG", "kds", "dch", "bEG")}
        Sst = ar.alloc([128, 8, 128], F32, "Sst")
        Sbf = ar.alloc([128, 8, 128], BF16, "Sbf")
        NG = 8
        maskPB = ar.alloc([128, 7, 256], BF16, "maskPB")
        ident2 = ar.alloc([128, 256], BF16, "ident2")
        DMA("pool", maskPB.rearrange("p a b -> p (a b)"), C["c_mpb"], w=["maskPB"])
        DMA("pool", ident2, C["c_id2"], w=["ident2"])
        kbe = [ar.alloc([128, 128], BF16, "kbe") for _ in range(NG)]
        kdc = [ar.alloc([128, 128], BF16, "kdc") for _ in range(NG)]
        vb = [ar.alloc([128, 128], BF16, "vb") for _ in range(NG)]
        qkm = [ar.alloc([128, 128], BF16, "qkm") for _ in range(NG)]
        qkmT = [ar.alloc([128, 128], BF16, "qkmT") for _ in range(NG)]
        Wn = [ar.alloc([128, 256], BF16, "Wn") for _ in range(NG)]
        TU = [ar.alloc([128, 256], BF16, "TU") for _ in range(NG)]
        YY = [ar.alloc([128, 256], BF16, "YY") for _ in range(NG)]
        uw = [ar.alloc([128, 256], BF16, "uw") for _ in range(NG)]
        vn = [ar.alloc([128, 128], BF16, "vn") for _ in range(NG)]
        normw = ar.alloc([128, 1], F32, "normw")
        gbt = [ar.alloc([128, 128], F32, "gbt")] * 2

        bcast_rows(wpre, W["mix_norm_pre"][l], "wpre")
        load_weight(wqkv, mw[:, 0:3072], lambda k, ch: ("wqkv", k, ch), 3072)
        load_weight(wba, mw[:, MW["ba"]:MW["ba"] + 16], lambda k, ch: ("wba", k), 16)
        load_cols(cols, "cols", [W["gdn_conv_w"][l].rearrange("j (c p) -> (j c) p", p=128)])
        DMA("sp", normw, W["gdn_norm_w"][l].rearrange("(p o) -> p o", o=1), w=["normw"])
        for j in range(4):
            DMA("sp", dtb[:, j * 8:(j + 1) * 8], W["gdn_dt_bias"][l].partition_broadcast(128), w=["dtb"])
            DMA("sp", nega[:, j * 8:(j + 1) * 8], W["gdn_a_log"][l].partition_broadcast(128), w=["nega"])
        A("act", lambda e: e.activation(out=nega, in_=nega, func=AF.Exp), ["nega"], ["nega"])
        A("dve", lambda e: e.tensor_scalar(out=nega, in0=nega, scalar1=-1.0, scalar2=None, op0=ALU.mult), ["nega"], ["nega"])
        for ci in range(24):
            for j in range(4):
                A("dve" if (ci + j) % 2 else "pool", lambda e, ci=ci, j=j: e.tensor_scalar(
                    out=dg[:, ci, j, :], in0=ident_f, scalar1=cols[:, j * 24 + ci:j * 24 + ci + 1],
                    scalar2=None, op0=ALU.mult), ["cols", "identf"], [("dg", ci)])
        A("pool", lambda e: e.memset(pch, 0.0), [], [("pch", ci) for ci in range(24)])
        A("pool", lambda e: e.memset(Sst, 0.0), [], [("S", h) for h in range(8)])
        A("pool", lambda e: e.memset(Sbf, 0.0), [], [("Sb", h) for h in range(8)])
        xi = [0]
        rr = [0]
        for ti in range(NT):
            slots = prenorm_hT(src, ti, xb, xi, 2, wpre, hb, hT, ss, rstd, act_rsqrt=True)
            og = ogT[ti % 2]
            for j in range(4):
                for k in range(8):
                    A("pe", lambda e, j=j, k=k: e.matmul(ps[:, 0, j * 16:(j + 1) * 16],
                                                         lhsT=hT[:, k, j * 128:(j + 1) * 128], rhs=wba[:, k, :],
                                                         start=(k == 0), stop=(k == 7)), [("wba", k)] + HT4, PK(0))
            lg = ps[:, 0, 0:64].rearrange("p (j t) -> p j t", j=4)
            v3 = lambda t: t.rearrange("p (j h) -> p j h", j=4)
            A("act", lambda e: e.activation(out=v3(sm["et"]), in_=lg[:, :, 0:8], func=AF.Exp, scale=-1.0), PK(0), ["et"])
            A("dve", lambda e: e.tensor_scalar(out=sm["et"], in0=sm["et"], scalar1=1.0, scalar2=None, op0=ALU.add), ["et"], ["et"])
            A("dve", lambda e: e.reciprocal(out=sm["beta"], in_=sm["et"]), ["et"], ["beta"])
            A("dve", lambda e: e.tensor_scalar(out=sm["nbeta"], in0=sm["beta"], scalar1=-1.0, scalar2=None, op0=ALU.mult),
              ["beta"], ["nbeta"])
            A("dve", lambda e: e.tensor_tensor(out=v3(sm["t"]), in0=lg[:, :, 8:16], in1=v3(dtb), op=ALU.add),
              PK(0) + ["dtb"], ["t"])
            A("act", lambda e: e.activation(out=sm["t"], in_=sm["t"], func=AF.Exp), ["t"], ["t"])
            A("act", lambda e: e.activation(out=sm["t"], in_=sm["t"], func=AF.Ln, bias=1.0), ["t"], ["t"])
            A("dve", lambda e: e.tensor_tensor(out=sm["g"], in0=sm["t"], in1=nega, op=ALU.mult), ["t", "nega"], ["g"])
            A("pe", lambda e: e.matmul(ps[:, 1, 0:32], lhsT=ltri_f, rhs=sm["g"], start=True, stop=True), ["ltri", "g"], PK(1))
            A("pe", lambda e: e.matmul(ps[:, 1, 32:64], lhsT=ones_f, rhs=sm["g"], start=True, stop=True), ["onesf", "g"], PK(1))
            A("dve", lambda e: e.tensor_copy(out=sm["Gc"], in_=ps[:, 1, 0:32]), PK(1), ["Gc"])
            A("dve", lambda e: e.tensor_scalar(out=sm["nGc"], in0=sm["Gc"], scalar1=-1.0, scalar2=None, op0=ALU.mult), ["Gc"], ["nGc"])
            A("dve", lambda e: e.tensor_tensor(out=sm["Gt"], in0=ps[:, 1, 32:64], in1=sm["Gc"], op=ALU.subtract),
              PK(1) + ["Gc"], ["Gt"])
            A("act", lambda e: e.activation(out=sm["kds"], in_=sm["Gt"], func=AF.Exp), ["Gt"], ["kds"])
            A("act", lambda e: e.activation(out=sm["dch"], in_=ps[:, 1, 32:64], func=AF.Exp), PK(1), ["dch"])
            A("act", lambda e: e.activation(out=sm["eG"], in_=sm["Gc"], func=AF.Exp), ["Gc"], ["eG"])
            A("dve", lambda e: e.tensor_tensor(out=sm["bEG"], in0=sm["eG"], in1=sm["beta"], op=ALU.mult),
              ["eG", "beta"], ["bEG"])
            def _s1_slice0(h, hp):
                for j in range(4):
                    gt = gbt[j % 2]
                    A("dve", lambda e, j=j, h=h, gt=gt: e.tensor_scalar(
                        out=gt, in0=ones_f, scalar1=sm["g"][:, j * 8 + h:j * 8 + h + 1], scalar2=None, op0=ALU.mult),
                        ["g", "onesf"], [("gbt", 0)])
                    A("pe", lambda e, j=j, gt=gt: e.matmul(
                        ps[:, 2, j * 128:(j + 1) * 128], lhsT=gt, rhs=ltri_f,
                        start=True, stop=True), [("gbt", 0), "ltri"], [("p", 2, j)])
                A("act", lambda e, hp=hp: e.activation(out=ebc[hp], in_=ps[:, 2, :], func=AF.Exp), PK(2), [("ebc", 0)])
                fd = ft[5]
                for j in range(4):
                    A("act", lambda e, j=j, h=h, fd=fd: e.activation(
                        out=fd[:, j * 128:(j + 1) * 128], in_=ps[:, 2, j * 128:(j + 1) * 128], func=AF.Relu,
                        bias=sm["nGc"][:, j * 8 + h:j * 8 + h + 1]), [("p", 2, j), "nGc"], [("ft", 5)])
                A("act", lambda e, fd=fd: e.activation(out=fd, in_=fd, func=AF.Exp, scale=-1.0), [("ft", 5)], [("ft", 5)])
                A("dve", lambda e, fd=fd, hp=hp: e.tensor_tensor(out=dms[hp], in0=fd, in1=mask_s, op=ALU.mult),
                  [("ft", 5), "masks"], [("dms", hp)])
                A("pool", lambda e, fd=fd, hp=hp: e.tensor_tensor(out=dmi[hp], in0=fd, in1=mask_i, op=ALU.mult),
                  [("ft", 5), "maski"], [("dmi", hp)])
            def _s1_qkv(h, hp, i):
                ci = i * 8 + h
                pcs = pc[(h * 3 + i) % 2]
                pk = ("pc", (h * 3 + i) % 2)
                for k in range(8):
                    A("pe", lambda e, ci=ci, k=k: e.matmul(ps[:, 0, :], lhsT=wqkv[:, k, ci * 128:(ci + 1) * 128],
                                                           rhs=hT[:, k, :], start=(k == 0), stop=(k == 7)),
                      [("wqkv", k, ci // 8)] + HT4, PK(0))
                A("act", lambda e, ci=ci, pcs=pcs: e.copy(out=pcs[:, 0:4], in_=pch[:, ci, :]), [("pch", ci)], [pk])
                A("act", lambda e, pcs=pcs: e.copy(out=pcs[:, 4:516], in_=ps[:, 0, :]), PK(0), [pk])
                A("act", lambda e, ci=ci, pcs=pcs: e.copy(out=pch[:, ci, :], in_=pcs[:, 512:516]), [pk], [("pch", ci)])
                for j in range(4):
                    A("pe", lambda e, ci=ci, j=j, pcs=pcs: e.matmul(
                        ps[:, 1, :], lhsT=dg[:, ci, j, :], rhs=pcs[:, 1 + j:1 + j + 512],
                        start=(j == 0), stop=(j == 3)), [("dg", ci), pk], PK(1))
                f0, f1, f2 = ft[0 + i % 2], ft[2 + i % 2], ft[4]
                k0, k1, k2 = ("ft", i % 2), ("ft", 2 + i % 2), ("ft", 4)
                sigmoid_from(ps[:, 1, :], PK(1), f0, k0, f1, k1)
                if i == 2:
                    A("dve", lambda e, f1=f1, h=h: e.tensor_tensor(out=vT[:, h, :], in0=ps[:, 1, :], in1=f1, op=ALU.mult),
                      PK(1) + [k1], [("vT", h)])
                    return
                A("dve", lambda e, f1=f1, f2=f2: e.tensor_tensor(out=f2, in0=ps[:, 1, :], in1=f1, op=ALU.mult),
                  PK(1) + [k1], [k2])
                yq = ysq[i % 2]
                A("act", lambda e, f2=f2, yq=yq: e.activation(out=yq, in_=f2, func=AF.Square), [k2], [("ysq", 0)])
                A("pe", lambda e, yq=yq: e.matmul(ps[:, 3, :], lhsT=ones_b, rhs=yq, start=True, stop=True),
                  ["onesb", ("ysq", 0)], PK(3))
                A("dve", lambda e, f0=f0: e.tensor_scalar(out=f0, in0=ps[:, 3, :], scalar1=1e-6, scalar2=None, op0=ALU.add),
                  PK(3), [k0])
                A("act", lambda e, f0=f0: e.activation(out=f0, in_=f0, func=AF.Ln), [k0], [k0])
                A("act", lambda e, f0=f0: e.activation(out=f0, in_=f0, func=AF.Exp, scale=-0.5), [k0], [k0])
                if i == 1:
                    A("dve", lambda e, f0=f0, f2=f2, h=h: e.tensor_tensor(out=kT[:, h, :], in0=f2, in1=f0, op=ALU.mult),
                      [k0, k2], [("kT", h)])
                else:
                    A("dve", lambda e, f0=f0, f2=f2: e.scalar_tensor_tensor(
                        out=f2, in0=f2, scalar=float(128 ** -0.5), in1=f0, op0=ALU.mult, op1=ALU.mult), [k0, k2], [k2])
                    A("act", lambda e, f2=f2, h=h: e.copy(out=qT[:, h, :], in_=f2), [k2], [("qT", h)])
                    A("dve", lambda e, f2=f2, h=h, hp=hp: e.tensor_tensor(out=qdT[:, h, :], in0=f2, in1=ebc[hp], op=ALU.mult),
                      [k2, ("ebc", 0)], [("qdT", h)])
            def _chunks_pair(hA):
                G = range(8)

                def hd(g):
                    return hA + g // 4

                def col(nm, g):
                    c = (g % 4) * 8 + hd(g)
                    return sm[nm][:, c:c + 1]

                def jb(g):
                    return slice((g % 4) * 128, (g % 4 + 1) * 128)

                def pq(g, half):
                    o = (g % 2) * 256 + half * 128
                    return ps[:, g // 2, o:o + 128]

                def pq2(g):
                    o = (g % 2) * 256
                    return ps[:, g // 2, o:o + 256]

                def PKg(g, half=None):
                    return [("p", g // 2, 0)]

                def qs(g, s_):
                    i_ = g * 2 + s_
                    return psb[:, i_ // 8, (i_ % 8) * 128:(i_ % 8) * 128 + 128]

                def QKg(g, s_):
                    return [("q", (g * 2 + s_) // 8, 0)]
                for g in G:
                    h = hd(g)
                    A("pe", lambda e, g=g, h=h: e.transpose(out=qs(g, 0), in_=kT[:, h, jb(g)], identity=ident_b),
                      [("kT", h), "identb"], QKg(g, 0))
                    A("pe", lambda e, g=g, h=h: e.transpose(out=qs(g, 1), in_=vT[:, h, jb(g)], identity=ident_b),
                      [("vT", h), "identb"], QKg(g, 1))
                    A("pe", lambda e, g=g, h=h: e.matmul(pq(g, 0), lhsT=kT[:, h, jb(g)], rhs=kT[:, h, jb(g)],
                                                         start=True, stop=True), [("kT", h)], PKg(g, 0))
                    A("pe", lambda e, g=g, h=h: e.matmul(pq(g, 1), lhsT=qT[:, h, jb(g)], rhs=kT[:, h, jb(g)],
                                                         start=True, stop=True), [("qT", h), ("kT", h)], PKg(g, 1))
                for g in G:
                    hp = hd(g) % 2
                    A("act", lambda e, g=g, c=col("bEG", g): e.activation(out=kbe[g], in_=qs(g, 0), func=AF.Identity, scale=c),
                      QKg(g, 0) + ["bEG"], [("kbe", g)])
                    A("act", lambda e, g=g, c=col("kds", g): e.activation(out=kdc[g], in_=qs(g, 0), func=AF.Identity, scale=c),
                      QKg(g, 0) + ["kds"], [("kdc", g)])
                    A("act", lambda e, g=g, c=col("beta", g): e.activation(out=vb[g], in_=qs(g, 1), func=AF.Identity, scale=c),
                      QKg(g, 1) + ["beta"], [("vb", g)])
                    A("dve", lambda e, g=g, hp=hp, c=col("nbeta", g): e.scalar_tensor_tensor(
                        out=Wn[g][:, 0:128], in0=pq(g, 0), scalar=c, in1=dms[hp][:, jb(g)], op0=ALU.mult, op1=ALU.mult),
                        PKg(g, 0) + ["nbeta", ("dms", hp)], [("Wn", g)])
                    A("dve", lambda e, g=g, hp=hp: e.tensor_tensor(out=qkm[g], in0=pq(g, 1), in1=dmi[hp][:, jb(g)], op=ALU.mult),
                      PKg(g, 1) + [("dmi", hp)], [("qkm", g)])
                for g in G:
                    A("pe", lambda e, g=g: e.transpose(out=qs(g, 0), in_=Wn[g][:, 0:128], identity=ident_b),
                      [("Wn", g), "identb"], QKg(g, 0))
                    A("pe", lambda e, g=g: e.transpose(out=qs(g, 1), in_=qkm[g], identity=ident_b),
                      [("qkm", g), "identb"], QKg(g, 1))
                for g in G:
                    A("act", lambda e, g=g: e.copy(out=Wn[g][:, 128:256], in_=qs(g, 0)), QKg(g, 0), [("Wn", g)])
                    A("act", lambda e, g=g: e.copy(out=qkmT[g], in_=qs(g, 1)), QKg(g, 1), [("qkmT", g)])
                for g in G:
                    A("dve", lambda e, g=g: e.tensor_tensor(out=YY[g], in0=Wn[g], in1=maskPB[:, 0, :], op=ALU.mult),
                      [("Wn", g), "maskPB"], [("YY", g)])
                    A("dve", lambda e, g=g: e.tensor_tensor(out=TU[g], in0=YY[g], in1=ident2, op=ALU.add),
                      [("YY", g), "ident2"], [("TU", g)])

                def _level(lvl):
                    last = lvl == 6
                    for g in G:
                        if not last:
                            A("pe", lambda e, g=g: e.matmul(pq(g, 0), lhsT=Wn[g][:, 128:256], rhs=TU[g][:, 0:128],
                                                            start=True, stop=True), [("Wn", g), ("TU", g)], PKg(g, 0))
                        A("pe", lambda e, g=g: e.matmul(pq(g, 1), lhsT=Wn[g][:, 0:128], rhs=TU[g][:, 128:256],
                                                        start=True, stop=True), [("Wn", g), ("TU", g)], PKg(g, 1))
                    for g in G:
                        if last:
                            A("dve", lambda e, g=g: e.tensor_tensor(out=YY[g][:, 128:256], in0=pq(g, 1),
                                                                    in1=maskPB[:, lvl, 128:256], op=ALU.mult),
                              PKg(g, 1) + ["maskPB"], [("YY", g)])
                        else:
                            A("dve", lambda e, g=g: e.tensor_tensor(out=YY[g], in0=pq2(g), in1=maskPB[:, lvl, :], op=ALU.mult),
                              PKg(g) + ["maskPB"], [("YY", g)])
                    for g in G:
                        if not last:
                            A("pe", lambda e, g=g: e.matmul(pq(g, 0), lhsT=ident_b, rhs=TU[g][:, 0:128], start=True, stop=False),
                              [("TU", g), "identb"], PKg(g, 0))
                            A("pe", lambda e, g=g: e.matmul(pq(g, 0), lhsT=TU[g][:, 128:256], rhs=YY[g][:, 0:128], start=False, stop=True),
                              [("TU", g), ("YY", g)], PKg(g, 0))
                        A("pe", lambda e, g=g: e.matmul(pq(g, 1), lhsT=ident_b, rhs=TU[g][:, 128:256], start=True, stop=False),
                          [("TU", g), "identb"], PKg(g, 1))
                        A("pe", lambda e, g=g: e.matmul(pq(g, 1), lhsT=TU[g][:, 0:128], rhs=YY[g][:, 128:256], start=False, stop=True),
                          [("TU", g), ("YY", g)], PKg(g, 1))
                    for g in G:
                        if last:
                            A("act", lambda e, g=g: e.copy(out=TU[g][:, 128:256], in_=pq(g, 1)), PKg(g, 1), [("TU", g)])
                        else:
                            A("act", lambda e, g=g: e.copy(out=TU[g], in_=pq2(g)), PKg(g), [("TU", g)])
                for lvl_ in range(1, 7):
                    _level(lvl_)
                for g in G:
                    A("pe", lambda e, g=g: e.matmul(pq(g, 0), lhsT=TU[g][:, 128:256], rhs=vb[g], start=True, stop=True),
                      [("TU", g), ("vb", g)], PKg(g, 0))
                    A("pe", lambda e, g=g: e.matmul(pq(g, 1), lhsT=kbe[g], rhs=TU[g][:, 128:256], start=True, stop=True),
                      [("TU", g), ("kbe", g)], PKg(g, 1))
                for g in G:
                    A("act", lambda e, g=g: e.copy(out=uw[g], in_=pq2(g)), PKg(g), [("uw", g)])

            def _rec(h, hh, j):
                g = hh * 4 + j
                jb = slice(j * 128, (j + 1) * 128)
                A("pe", lambda e: e.matmul(ps[:, hh, 0:128], lhsT=uw[g][:, 128:256], rhs=Sbf[:, h, :], start=True, stop=True),
                  [("uw", g), ("Sb", h)], [("p", hh, 0)])
                A("dve", lambda e: e.tensor_tensor(out=vn[g], in0=uw[g][:, 0:128], in1=ps[:, hh, 0:128], op=ALU.subtract),
                  [("uw", g), ("p", hh, 0)], [("vn", g)])
                A("pe", lambda e: e.matmul(ps[:, 4 + hh, jb], lhsT=Sbf[:, h, :], rhs=qdT[:, h, jb], start=True, stop=False),
                  [("Sb", h), ("qdT", h)], [("p", 4 + hh, j)])
                A("pe", lambda e: e.matmul(ps[:, 4 + hh, jb], lhsT=vn[g], rhs=qkmT[g], start=False, stop=True),
                  [("vn", g), ("qkmT", g)], [("p", 4 + hh, j)])
                A("pe", lambda e: e.matmul(ps[:, hh, 128:256], lhsT=kdc[g], rhs=vn[g], start=True, stop=True),
                  [("kdc", g), ("vn", g)], [("p", hh, 1)])
                A("dve", lambda e, c=sm["dch"][:, j * 8 + h:j * 8 + h + 1]: e.scalar_tensor_tensor(
                    out=Sst[:, h, :], in0=Sst[:, h, :], scalar=c, in1=ps[:, hh, 128:256], op0=ALU.mult, op1=ALU.add),
                    [("S", h), "dch", ("p", hh, 1)], [("S", h)])
                A("dve", lambda e: e.tensor_copy(out=Sbf[:, h, :], in_=Sst[:, h, :]), [("S", h)], [("Sb", h)])

            def _headout(h, hh):
                yq = ysq[0]
                f0 = ft[hh]
                po, pn = 4 + hh, 2 + hh
                A("act", lambda e: e.activation(out=yq, in_=ps[:, po, :], func=AF.Square), PK(po), [("ysq", 0)])
                A("pe", lambda e: e.matmul(ps[:, pn, :], lhsT=ones_b, rhs=yq, start=True, stop=True),
                  ["onesb", ("ysq", 0)], PK(pn))
                A("dve", lambda e: e.tensor_scalar(out=f0, in0=ps[:, pn, :], scalar1=1.0 / 128, scalar2=1e-6,
                                                   op0=ALU.mult, op1=ALU.add), PK(pn), [("ft", hh)])
                A("act", lambda e: e.activation(out=f0, in_=f0, func=AF.Ln), [("ft", hh)], [("ft", hh)])
                A("act", lambda e: e.activation(out=f0, in_=f0, func=AF.Exp, scale=-0.5), [("ft", hh)], [("ft", hh)])
                A("dve", lambda e: e.scalar_tensor_tensor(
                    out=og[:, h, :], in0=ps[:, po, :], scalar=normw[:, 0:1], in1=f0, op0=ALU.mult, op1=ALU.mult),
                    PK(po) + ["normw", ("ft", hh)], [("og", 0)])
            for hA in range(0, 8, 2):
                for hh in range(2):
                    _s1_slice0(hA + hh, (hA + hh) % 2)
                    for i_ in range(3):
                        _s1_qkv(hA + hh, (hA + hh) % 2, i_)
                _chunks_pair(hA)
                for j in range(4):
                    for hh in range(2):
                        _rec(hA + hh, hh, j)
                for hh in range(2):
                    _headout(hA + hh, hh)
            DMA("sp", ogs[ti], og, r=[("og", 0)], w=[("ogs", ti)])

    def mixer_C(l, src, dst):
        ar.reset()
        S.barrier()
        mw = W["mix_w_in"][l]
        wz = ar.alloc([128, 8, 1024], BF16, "wz")
        wga = ar.alloc([128, 8, 1024], BF16, "wga")
        wgb = ar.alloc([128, 8, 1024], BF16, "wgb")
        gwo = ar.alloc([128, 8, 1024], BF16, "gwo")
        wmo = ar.alloc([128, 8, 1024], BF16, "wmo")
        wpre = ar.alloc([128, D], F32, "wpre")
        wpost = ar.alloc([128, D], F32, "wpost")
        xb = [ar.alloc([128, D], F32, "xb") for _ in range(6)]
        hb = [ar.alloc([128, D], BF16, "hb") for _ in range(2)]
        hT = ar.alloc([128, 8, 512], BF16, "hT")
        ogn = [ar.alloc([128, 8, 512], BF16, "ogn") for _ in range(2)]
        m2T = [ar.alloc([128, 8, 512], BF16, "m2T") for _ in range(2)]
        ogT = ar.alloc([128, 8, 512], BF16, "ogT")
        mT = ar.alloc([128, 8, 512], BF16, "mT")
        ft = [ar.alloc([128, 512], F32, "ft") for _ in range(6)]
        tmp = [ar.alloc([128, 512], F32, "tmp") for _ in range(2)]
        ss = ar.alloc([128, 8], F32, "ss")
        rstd = ar.alloc([128, 8], F32, "rstd")
        ss2 = ar.alloc([128, 8], F32, "ss2")
        rstd2 = ar.alloc([128, 4], F32, "rstd2")
        bcast_rows(wpre, W["mix_norm_pre"][l], "wpre")
        bcast_rows(wpost, W["mix_norm_post"][l], "wpost")
        load_weight(wz, mw[:, MW["z"]:MW["z"] + 1024], lambda k, ch: ("wz", k), 1024)
        load_weight(wga, mw[:, MW["ga"]:MW["ga"] + 1024], lambda k, ch: ("wga", k), 1024)
        load_weight(wgb, mw[:, MW["gb"]:MW["gb"] + 1024], lambda k, ch: ("wgb", k), 1024)
        load_weight(gwo, W["gdn_w_o"][l], lambda k, ch: ("gwo", k), 1024)
        load_weight(wmo, W["mix_w_out"][l], lambda k, ch: ("wmo", k), 1024)
        xi = [0]
        for ti in range(NT):
            on = ogn[ti % 2]
            mt = m2T[ti % 2]
            DMA("sp", on, ogs[ti], r=[("ogs", ti)], w=[("ogn", ti % 2)])
            DMA("sp", mt, m2s[ti], r=[("m2s", ti)], w=[("m2T", ti % 2)])
            slots = prenorm_hT(src, ti, xb, xi, 6, wpre, hb, hT, ss, rstd, act_rsqrt=True)
            for h in range(8):
                for k in range(8):
                    A("pe", lambda e, h=h, k=k: e.matmul(ps[:, h % 2, :], lhsT=wz[:, k, h * 128:(h + 1) * 128], rhs=hT[:, k, :],
                                                         start=(k == 0), stop=(k == 7)), [("wz", k)] + HT4, PK(h % 2))
                f0, f1 = ft[h % 2], ft[2 + h % 2]
                sigmoid_from(ps[:, h % 2, :], PK(h % 2), f0, ("ft", h % 2), f1, ("ft", 2 + h % 2))
                A("dve", lambda e, h=h, f1=f1: e.tensor_tensor(out=f1, in0=ps[:, h % 2, :], in1=f1, op=ALU.mult),
                  PK(h % 2) + [("ft", 2 + h % 2)], [("ft", 2 + h % 2)])
                A("dve", lambda e, h=h, f1=f1, on=on: e.tensor_tensor(out=ogT[:, h, :], in0=on[:, h, :], in1=f1, op=ALU.mult),
                  [("ogn", ti % 2), ("ft", 2 + h % 2)], [("ogT", h)])
            for fo in range(8):
                pya, pga = (2, 3) if fo % 2 == 0 else (0, 1)
                for h in range(8):
                    A("pe", lambda e, fo=fo, h=h, pya=pya: e.matmul(ps[:, pya, :], lhsT=gwo[:, h, fo * 128:(fo + 1) * 128], rhs=ogT[:, h, :],
                                                           start=(h == 0), stop=(h == 7)), [("gwo", h), ("ogT", h)], PK(pya))
                for k in range(8):
                    A("pe", lambda e, fo=fo, k=k, pga=pga: e.matmul(ps[:, pga, :], lhsT=wga[:, k, fo * 128:(fo + 1) * 128], rhs=hT[:, k, :],
                                                           start=(k == 0), stop=(k == 7)), [("wga", k)] + HT4, PK(pga))
                f0, f1, f2 = ft[fo % 2], ft[2 + fo % 2], ft[4 + fo % 2]
                sigmoid_from(ps[:, pga, :], PK(pga), f0, ("ft", fo % 2), f1, ("ft", 2 + fo % 2))
                A("dve", lambda e, f1=f1, f2=f2, pya=pya: e.tensor_tensor(out=f2, in0=ps[:, pya, :], in1=f1, op=ALU.mult),
                  PK(pya) + [("ft", 2 + fo % 2)], [("ft", 4 + fo % 2)])
                for k in range(8):
                    A("pe", lambda e, fo=fo, k=k, pga=pga: e.matmul(ps[:, pga, :], lhsT=wgb[:, k, fo * 128:(fo + 1) * 128], rhs=hT[:, k, :],
                                                           start=(k == 0), stop=(k == 7)), [("wgb", k)] + HT4, PK(pga))
                sigmoid_from(ps[:, pga, :], PK(pga), f0, ("ft", fo % 2), f1, ("ft", 2 + fo % 2))
                A("dve", lambda e, fo=fo, f1=f1, mt=mt: e.tensor_tensor(out=f1, in0=f1, in1=mt[:, fo, :], op=ALU.mult),
                  [("ft", 2 + fo % 2), ("m2T", ti % 2)], [("ft", 2 + fo % 2)])
                A("dve", lambda e, fo=fo, f1=f1, f2=f2: e.tensor_tensor(out=mT[:, fo, :], in0=f2, in1=f1, op=ALU.add),
                  [("ft", 4 + fo % 2), ("ft", 2 + fo % 2)], [("mT", fo)])
            for b in range(4):
                for half in range(2):
                    po = 4 + half
                    for fo in range(8):
                        A("pe", lambda e, fo=fo, b=b, half=half, po=po: e.matmul(
                            ps[:, po, :], lhsT=mT[:, fo, b * 128:(b + 1) * 128], rhs=wmo[:, fo, half * 512:(half + 1) * 512],
                            start=(fo == 0), stop=(fo == 7)), [("mT", fo), ("wmo", fo)], PK(po))
                postnorm_residual(b, slots[b], xb, wpost, tmp, ss2, rstd2, 1.0, dst, ti)

    cur = x_in
    for l in range(depth):
        if phases is None or "ffn1" in phases:
            ffn_phase(l, "ffn1", cur, y_out)
            cur = y_out
        if phases is None or "mix" in phases or "mixA" in phases:
            mixer_A(l, cur)
        if phases is None or "mix" in phases or "mixB" in phases:
            mixer_B(l, cur)
        if phases is None or "mix" in phases or "mixC" in phases:
            mixer_C(l, cur, y_out)
            cur = y_out
        if phases is None or "ffn2" in phases:
            ffn_phase(l, "ffn2", cur, y_out)
            cur = y_out

    S.barrier()
    S.add("sp", lambda e: e.nop())
    stack = ExitStack()
    S.emit_all(stack)
    stack.close()
    return nc


WEIGHT_SHAPES = {
    "ffn1_norm_pre": (D,), "ffn1_norm_post": (D,), "ffn1_w_in": (D, 2 * DFF), "ffn1_w_out": (DFF, D),
    "mix_norm_pre": (D,), "mix_norm_post": (D,), "mix_w_in": (D, PIN),
    "gdn_conv_w": (4, 3072), "gdn_a_log": (8,), "gdn_dt_bias": (8,), "gdn_norm_w": (128,), "gdn_w_o": (D, D),
    "cnv_pw1_b": (2048,), "cnv_dw_w": (31, D), "cnv_dw_b": (D,), "cnv_ln_g": (D,), "cnv_ln_b": (D,),
    "cnv_w_o": (D, D), "cnv_b_o": (D,), "mix_w_out": (D, D),
    "ffn2_norm_pre": (D,), "ffn2_norm_post": (D,), "ffn2_w_in": (D, 2 * DFF), "ffn2_w_out": (DFF, D),
}
CONST_SHAPES = {"c_ident": [128, 128], "c_ltri": [128, 128], "c_ones": [128, 128],
                "c_masks": [128, 512], "c_maski": [128, 512], "c_mpb": [128, 1792], "c_id2": [128, 256]}


def consts():
    i = np.arange(128)
    ltri = (i[:, None] <= i[None, :]).astype(np.float32)
    ms = (i[None, :] < i[:, None]).astype(np.float32)
    mi = (i[None, :] <= i[:, None]).astype(np.float32)
    mp = np.zeros((128, 7, 128), np.float32)
    for lv in range(7):
        blk = i // (1 << lv)
        mp[:, lv, :] = ((blk[:, None] // 2 == blk[None, :] // 2) & (blk[:, None] % 2 == 1)
                        & (blk[None, :] % 2 == 0)).astype(np.float32)
    mb = np.ascontiguousarray(mp.transpose(2, 1, 0))
    mpb = np.concatenate([mp, mb], axis=2)
    return {"c_mpb": np.ascontiguousarray(mpb).reshape(128, 1792),
            "c_id2": np.tile(np.eye(128, dtype=np.float32), (1, 2)),
            "c_ident": np.eye(128, dtype=np.float32), "c_ltri": ltri,
            "c_ones": np.ones((128, 128), np.float32),
            "c_masks": np.tile(ms, (1, 4)), "c_maski": np.tile(mi, (1, 4))}


def kernel(**inputs):
    x = np.ascontiguousarray(inputs["x"], dtype=np.float32)
    B, T, _ = x.shape
    nc = build_program(T, DEPTH)
    shared = {k: np.ascontiguousarray(inputs[k], dtype=np.float32) for k in WEIGHT_SHAPES}
    shared.update(consts())
    in_maps = []
    for b in range(B):
        m = dict(shared)
        m["x"] = x[b]
        in_maps.append(m)
    res = run_bass_kernel_spmd(nc, in_maps, core_ids=list(range(B)))
    return np.stack([np.asarray(r["y"]).reshape(T, D) for r in res.results], axis=0).astype(np.float32)
```

```python
import numpy as np
import concourse.bass as bass
import concourse.mybir as mybir
from concourse.bass_utils import run_bass_kernel_spmd
from contextlib import ExitStack

F32 = mybir.dt.float32
BF16 = mybir.dt.bfloat16
AF = mybir.ActivationFunctionType
ALU = mybir.AluOpType

D = 1024
DFF = 2816
NH = 8
DEPTH = 2
PIN = 8208
NDMA_SEMS = 24


class Op:
    __slots__ = ("eng", "emit", "dma", "deps", "has_dep", "token", "prev_token", "idx")

    def __init__(self, eng, emit, dma):
        self.eng = eng
        self.emit = emit
        self.dma = dma
        self.deps = ()
        self.has_dep = False
        self.token = None
        self.prev_token = None


class Sched:
    ENGS = ("pe", "act", "dve", "pool", "sp")

    def __init__(self, nc):
        self.nc = nc
        self.eng_ops = {e: [] for e in self.ENGS}
        self.last_w = {}
        self.readers = {}
        self.pending = {e: [] for e in self.ENGS}
        self.last_compute = {}
        self.dmas = []
        self.live_dmas = []
        self.nops = 0
        self.bank_rd = {}

    @staticmethod
    def _norm(keys):
        return [k[:2] if (isinstance(k, tuple) and len(k) == 3 and k[0] in ("p", "q")) else k for k in keys]

    def add(self, eng, emit, reads=(), writes=(), dma=False):
        reads = self._norm(reads)
        writes = self._norm(writes)
        op = Op(eng, emit, dma)
        op.idx = self.nops
        self.nops += 1
        deps = {}
        psum_banks = set()

        def add_dep(d):
            if d.dma:
                deps[("d", d.idx)] = d
            else:
                if eng == "pe" and d.eng == "pe":
                    return
                cur = deps.get(d.eng)
                if cur is None or cur.idx < d.idx:
                    deps[d.eng] = d

        for r in reads:
            w = self.last_w.get(r)
            if w is not None:
                add_dep(w)
            if isinstance(r, tuple) and r[0] in ("p", "q"):
                bk = (r[0], r[1])
                last = self.bank_rd.get(bk)
                if last is not None and last.eng != eng:
                    add_dep(last)
                psum_banks.add(bk)
        for w in writes:
            lw = self.last_w.get(w)
            if lw is not None:
                add_dep(lw)
            rd = self.readers.get(w)
            if rd:
                for d in rd.values():
                    add_dep(d)
        for d in self.pending[eng]:
            add_dep(d)
        self.pending[eng] = []
        op.deps = tuple(deps.values())
        for d in op.deps:
            d.has_dep = True
        for bk in psum_banks:
            self.bank_rd[bk] = op
        for r in reads:
            rd = self.readers.setdefault(r, {})
            rd[("d", op.idx) if dma else eng] = op
        for w in writes:
            self.last_w[w] = op
            self.readers[w] = {}
        self.eng_ops[eng].append(op)
        if dma:
            self.dmas.append(op)
            self.live_dmas.append(op)
        else:
            self.last_compute[eng] = op
        return op

    def barrier(self):
        toks = list(self.last_compute.values()) + list(self.live_dmas)
        for e in self.ENGS:
            self.pending[e] = list(toks)
        self.live_dmas = []
        self.last_w = {}
        self.readers = {}
        self.bank_rd = {}

    def emit_all(self, stack):
        nc = self.nc
        sems = {}
        for e in ("pe", "act", "dve", "pool"):
            sems[e] = stack.enter_context(nc.semaphore("c_" + e))
            cnt = 0
            for op in self.eng_ops[e]:
                if op.dma:
                    continue
                if op.has_dep:
                    cnt += 1
                    op.token = (sems[e], cnt)
        dsems = [stack.enter_context(nc.semaphore("d%d" % i)) for i in range(NDMA_SEMS)]
        counts = [0] * NDMA_SEMS
        for i, op in enumerate(self.dmas):
            s = i % NDMA_SEMS
            n = counts[s]
            if n > 0:
                op.prev_token = (dsems[s], 16 * n)
            counts[s] = n + 1
            op.token = (dsems[s], 16 * (n + 1))
        block = stack.enter_context(nc.Block())

        def run(eng_name):
            def body(eng):
                waited = {}
                for op in self.eng_ops[eng_name]:
                    toks = [d.token for d in op.deps]
                    if op.prev_token is not None:
                        toks.append(op.prev_token)
                    for (sem, val) in toks:
                        key = id(sem)
                        if waited.get(key, 0) < val:
                            eng.wait_ge(sem, val)
                            waited[key] = val
                    inst = op.emit(eng)
                    if op.dma:
                        inst.then_inc(op.token[0], 16)
                    elif op.has_dep:
                        inst.then_inc(op.token[0], 1)
            return body

        block.tensor(run("pe"))
        block.scalar(run("act"))
        block.vector(run("dve"))
        block.gpsimd(run("pool"))
        block.sync(run("sp"))


class Arena:
    def __init__(self, nc, limit):
        self.nc = nc
        self.limit = limit
        self.base = 20608
        self.off = 20608
        self.n = 0

    def alloc(self, shape, dtype, name="t"):
        esz = 2 if dtype == BF16 else 4
        per_part = esz
        for s in shape[1:]:
            per_part *= s
        off = (self.off + 63) // 64 * 64
        assert off + per_part <= self.limit, ("SBUF overflow", name, off, per_part, self.limit)
        self.off = off + per_part
        self.n += 1
        h = self.nc.alloc_sbuf_tensor_at("%s_%d" % (name, self.n), list(shape), dtype, offset=off)
        return h.ap()

    def mark_persistent(self):
        self.base = self.off

    def reset(self):
        self.off = self.base


MW = {"qkv": 0, "z": 3072, "ba": 4096, "glu_a": 4112, "glu_g": 5136, "ga": 6160, "gb": 7184}


def build_program(T, depth, phases=None):
    nc = bass.Bass("TRN2", target_bir_lowering=False)
    NT = T // 512

    def din(name, shape):
        return nc.dram_tensor(name, list(shape), F32, kind="ExternalInput").ap()

    x_in = din("x", [T, D])
    y_out = nc.dram_tensor("y", [T, D], F32, kind="ExternalOutput").ap()
    W = {}
    for nm, shp in WEIGHT_SHAPES.items():
        W[nm] = din(nm, (depth,) + tuple(shp))
    C = {}
    for nm, shp in CONST_SHAPES.items():
        C[nm] = din(nm, shp)
    m2s = nc.dram_tensor("m2s", [NT, 128, 8, 512], BF16, kind="Internal").ap()
    ogs = nc.dram_tensor("ogs", [NT, 128, 8, 512], BF16, kind="Internal").ap()

    S = Sched(nc)
    ar = Arena(nc, nc.SBUF_PARTITION_SIZE_BYTES)

    def A(eng, fn, r=(), w=()):
        return S.add(eng, fn, reads=r, writes=w)

    def DMA(q, out, in_, r=(), w=()):
        return S.add(q, lambda e: e.dma_start(out=out, in_=in_), reads=r, writes=w, dma=True)

    def PK(b):
        return [("p", b, s) for s in range(4)]

    def QK(b):
        return [("q", b, s) for s in range(8)]

    ident_f = ar.alloc([128, 128], F32, "identf")
    ident_b = ar.alloc([128, 128], BF16, "identb")
    ltri_f = ar.alloc([128, 128], F32, "ltri")
    ones_f = ar.alloc([128, 128], F32, "onesf")
    ones_b = ar.alloc([128, 128], BF16, "onesb")
    mask_s = ar.alloc([128, 512], F32, "masks")
    mask_i = ar.alloc([128, 512], F32, "maski")
    cm05 = ar.alloc([128, 8], F32, "cm05")
    DMA("sp", ident_f, C["c_ident"], w=["identf"])
    DMA("sp", ltri_f, C["c_ltri"], w=["ltri"])
    DMA("sp", ones_f, C["c_ones"], w=["onesf"])
    DMA("sp", mask_s, C["c_masks"], w=["masks"])
    DMA("sp", mask_i, C["c_maski"], w=["maski"])
    A("dve", lambda e: e.tensor_copy(out=ident_b, in_=ident_f), ["identf"], ["identb"])
    A("dve", lambda e: e.tensor_copy(out=ones_b, in_=ones_f), ["onesf"], ["onesb"])
    A("pool", lambda e: e.memset(cm05, -0.5), [], ["cm05"])
    ar.mark_persistent()

    ps = nc.alloc_psum_tensor("ps", [128, 6, 512], F32).ap()
    psb = nc.alloc_psum_tensor("psb", [128, 2, 1024], BF16).ap()

    def bcast_rows(dst, src_row, key):
        DMA("sp", dst, src_row.partition_broadcast(128), w=[key])

    def load_weight(dst, src2d, keyfn, ncols, chunk=1024):
        for c0 in range(0, ncols, chunk):
            c1 = min(ncols, c0 + chunk)
            for k in range(8):
                DMA("pool", dst[:, k, c0:c1], src2d[k * 128:(k + 1) * 128, c0:c1], w=[keyfn(k, c0 // chunk)])

    def load_cols(dst, dkey, entries):
        raw = ar.alloc([128, 128], F32, "raw")
        r0 = 0
        allrows = []
        for src in entries:
            R = src.shape[0]
            off = 0
            while off < R:
                n = min(R - off, 128 - (r0 % 128)) if (r0 % 128) else min(R - off, 128)
                allrows.append((src[off:off + n, :], r0, n))
                r0 += n
                off += n
        total = r0
        g = 0
        while g * 128 < total:
            rows = [(a, r, n) for (a, r, n) in allrows if r // 128 == g]
            nr = sum(n for (_, _, n) in rows)
            key = ("raw", g)
            for (a, r, n) in rows:
                DMA("sp", raw[r % 128:r % 128 + n, :], a, w=[key])
            A("pe", lambda e, nr=nr: e.transpose(out=ps[:, 5, 0:nr], in_=raw[0:nr, :], identity=ident_f[0:nr, 0:nr]),
              [key, "identf"], PK(5))
            A("dve", lambda e, nr=nr, g=g: e.tensor_copy(out=dst[:, g * 128:g * 128 + nr], in_=ps[:, 5, 0:nr]),
              PK(5), [dkey])
            g += 1
            if g * 128 < total:
                S.last_w[("raw", g)] = S.last_compute["pe"]
        return total

    def prenorm_hT(src, ti, xb, xi, nbuf, wpre, hb, hT, ss, rstd):
        slots = []
        for b in range(4):
            sl = xi[0] % nbuf
            xi[0] += 1
            slots.append(sl)
            DMA("sp", xb[sl], src[ti * 512 + b * 128:ti * 512 + (b + 1) * 128, :],
                r=[("xd", ti, b)], w=[("xb", sl)])
            hbb = hb[b % 2]
            hk = ("hb", id(hbb))
            A("act", lambda e, sl=sl, hbb=hbb, b=b: e.activation(
                out=hbb, in_=xb[sl], func=AF.Square, accum_out=ss[:, b:b + 1]),
                [("xb", sl)], [hk, ("ss", b)])
            A("dve", lambda e, b=b: e.tensor_scalar(
                out=rstd[:, b:b + 1], in0=ss[:, b:b + 1], scalar1=1.0 / D, scalar2=1e-6,
                op0=ALU.mult, op1=ALU.add), [("ss", b)], [("rstd", b)])
            A("pool", lambda e, b=b: e.tensor_tensor(
                out=rstd[:, b:b + 1], in0=rstd[:, b:b + 1], in1=cm05[:, 0:1], op=ALU.pow),
                [("rstd", b), "cm05"], [("rstd", b)])
            A("dve", lambda e, sl=sl, hbb=hbb, b=b: e.scalar_tensor_tensor(
                out=hbb, in0=xb[sl], scalar=rstd[:, b:b + 1], in1=wpre, op0=ALU.mult, op1=ALU.mult),
                [("xb", sl), ("rstd", b), "wpre"], [hk])
            q = b % 2
            for k in range(8):
                A("pe", lambda e, hbb=hbb, k=k, q=q: e.transpose(
                    out=psb[:, q, k * 128:(k + 1) * 128], in_=hbb[:, k * 128:(k + 1) * 128],
                    identity=ident_b), [hk, "identb"], [("q", q, k)])
            dstv = hT[:, :, b * 128:(b + 1) * 128]
            srcv = psb[:, q, :].rearrange("p (k t) -> p k t", k=8)
            if b % 2 == 0:
                A("act", lambda e, dstv=dstv, srcv=srcv: e.copy(out=dstv, in_=srcv), QK(q), [("hT", b)])
            else:
                A("dve", lambda e, dstv=dstv, srcv=srcv: e.tensor_copy(out=dstv, in_=srcv), QK(q), [("hT", b)])
        return slots

    HT4 = [("hT", b) for b in range(4)]

    def postnorm_residual(b, sl, xb, wpost, tmp, ss2, rstd2, scale, dst, ti, pbase=4):
        t0 = ti * 512
        for half in range(2):
            po = pbase + half
            A("act", lambda e, half=half, po=po: e.activation(
                out=tmp[half], in_=ps[:, po, :], func=AF.Square,
                accum_out=ss2[:, 2 * b + half:2 * b + half + 1]),
                PK(po), [("tmp", half), ("ss2", b, half)])
        A("dve", lambda e: e.tensor_tensor(
            out=rstd2[:, b:b + 1], in0=ss2[:, 2 * b:2 * b + 1], in1=ss2[:, 2 * b + 1:2 * b + 2],
            op=ALU.add), [("ss2", b, 0), ("ss2", b, 1)], [("rstd2", b)])
        inv = 1.0 / (scale * scale)
        A("dve", lambda e: e.tensor_scalar(
            out=rstd2[:, b:b + 1], in0=rstd2[:, b:b + 1], scalar1=inv / D, scalar2=inv * 1e-6,
            op0=ALU.mult, op1=ALU.add), [("rstd2", b)], [("rstd2", b)])
        A("pool", lambda e: e.tensor_tensor(
            out=rstd2[:, b:b + 1], in0=rstd2[:, b:b + 1], in1=cm05[:, 0:1], op=ALU.pow),
            [("rstd2", b), "cm05"], [("rstd2", b)])
        for half in range(2):
            po = pbase + half
            A("dve", lambda e, half=half, po=po: e.scalar_tensor_tensor(
                out=tmp[half], in0=ps[:, po, :], scalar=rstd2[:, b:b + 1],
                in1=wpost[:, half * 512:(half + 1) * 512], op0=ALU.mult, op1=ALU.mult),
                PK(po) + [("rstd2", b), "wpost"], [("tmp", half)])
            A("pool", lambda e, half=half: e.tensor_tensor(
                out=xb[sl][:, half * 512:(half + 1) * 512], in0=xb[sl][:, half * 512:(half + 1) * 512],
                in1=tmp[half], op=ALU.add), [("tmp", half), ("xb", sl)], [("xb", sl)])
        DMA("sp", dst[t0 + b * 128:t0 + (b + 1) * 128, :], xb[sl], r=[("xb", sl)], w=[("xd", ti, b)])

    def load_x(src, ti, xb, xi, nbuf):
        slots = []
        for b in range(4):
            sl = xi[0] % nbuf
            xi[0] += 1
            slots.append(sl)
            DMA("sp", xb[sl], src[ti * 512 + b * 128:ti * 512 + (b + 1) * 128, :],
                r=[("xd", ti, b)], w=[("xb", sl)])
        return slots

    def sigmoid_from(src, skeys, e_t, ekey, r_t, rkey, negbias=None, bkeys=()):
        if negbias is None:
            A("act", lambda e: e.activation(out=e_t, in_=src, func=AF.Exp, scale=-1.0), skeys, [ekey])
        else:
            A("act", lambda e: e.activation(out=e_t, in_=src, func=AF.Exp, scale=-1.0, bias=negbias),
              list(skeys) + list(bkeys), [ekey])
        A("act", lambda e: e.activation(out=e_t, in_=e_t, func=AF.Ln, bias=1.0), [ekey], [ekey])
        A("act", lambda e: e.activation(out=r_t, in_=e_t, func=AF.Exp, scale=-1.0), [ekey], [rkey])

    def ffn_phase(l, which, src, dst):
        ar.reset()
        S.barrier()
        w_in = W[which + "_w_in"]
        w_out = W[which + "_w_out"]
        win_sb = ar.alloc([128, 8, 2 * DFF], BF16, "win")
        wout_sb = ar.alloc([128, 22, D], BF16, "wout")
        wpre = ar.alloc([128, D], F32, "wpre")
        wpost = ar.alloc([128, D], F32, "wpost")
        xb = [ar.alloc([128, D], F32, "xb") for _ in range(4)]
        _hbsingle = True
        hb = [ar.alloc([128, D], BF16, "hb")] * 2
        hT = ar.alloc([128, 8, 512], BF16, "hT")
        inter = ar.alloc([128, 22, 512], BF16, "inter")
        sg = [ar.alloc([128, 512], BF16, "sg") for _ in range(2)]
        tmp = [ar.alloc([128, 512], F32, "tmp") for _ in range(2)]
        ss = ar.alloc([128, 8], F32, "ss")
        rstd = ar.alloc([128, 8], F32, "rstd")
        ss2 = ar.alloc([128, 8], F32, "ss2")
        rstd2 = ar.alloc([128, 4], F32, "rstd2")
        bcast_rows(wpre, W[which + "_norm_pre"][l], "wpre")
        bcast_rows(wpost, W[which + "_norm_post"][l], "wpost")
        for c0 in (0, 2 * 1408, 1408, 3 * 1408):
            for k in range(8):
                DMA("pool", win_sb[:, k, c0:c0 + 1408], w_in[l, k * 128:(k + 1) * 128, c0:c0 + 1408],
                    w=[("win", k, c0 // 1408)])
        for c in range(22):
            DMA("pool", wout_sb[:, c, :], w_out[l, c * 128:(c + 1) * 128, :], w=[("wout", c)])
        xi = [0]
        for ti in range(NT):
            slots = prenorm_hT(src, ti, xb, xi, 4, wpre, hb, hT, ss, rstd)
            for c in range(22):
                pa = c % 2
                pb = 2 + c % 2
                for k in range(8):
                    A("pe", lambda e, c=c, k=k, pa=pa: e.matmul(
                        ps[:, pa, :], lhsT=win_sb[:, k, c * 128:(c + 1) * 128], rhs=hT[:, k, :],
                        start=(k == 0), stop=(k == 7)),
                        [("win", k, (c * 128) // 1408), ("win", k, (c * 128 + 127) // 1408)] + HT4, PK(pa))
                for k in range(8):
                    A("pe", lambda e, c=c, k=k, pb=pb: e.matmul(
                        ps[:, pb, :], lhsT=win_sb[:, k, DFF + c * 128:DFF + (c + 1) * 128], rhs=hT[:, k, :],
                        start=(k == 0), stop=(k == 7)),
                        [("win", k, (DFF + c * 128) // 1408), ("win", k, (DFF + c * 128 + 127) // 1408)] + HT4, PK(pb))
                A("act", lambda e, c=c, pa=pa: e.activation(out=sg[c % 2], in_=ps[:, pa, :], func=AF.Silu),
                  PK(pa), [("sg", c % 2)])
                A("dve", lambda e, c=c, pb=pb: e.tensor_tensor(
                    out=inter[:, c, :], in0=ps[:, pb, :], in1=sg[c % 2], op=ALU.mult),
                    PK(pb) + [("sg", c % 2)], [("inter", c)])
            for b in range(4):
                pbase = 4 if b % 2 == 0 else 0
                for half in range(2):
                    po = pbase + half
                    for c in range(22):
                        A("pe", lambda e, c=c, b=b, half=half, po=po: e.matmul(
                            ps[:, po, :], lhsT=inter[:, c, b * 128:(b + 1) * 128],
                            rhs=wout_sb[:, c, half * 512:(half + 1) * 512],
                            start=(c == 0), stop=(c == 21)), [("inter", c), ("wout", c)], PK(po))
                postnorm_residual(b, slots[b], xb, wpost, tmp, ss2, rstd2, 0.5, dst, ti, pbase)

    def mixer_A(l, src):
        ar.reset()
        S.barrier()
        mw = W["mix_w_in"][l]
        wglu = ar.alloc([128, 8, 2048], BF16, "wglu")
        cwo = ar.alloc([128, 8, 1024], BF16, "cwo")
        dg = ar.alloc([128, 8, 31, 128], BF16, "dg31")
        wpre = ar.alloc([128, D], F32, "wpre")
        cols = ar.alloc([128, 512], F32, "cols")
        ncols = ar.alloc([128, 16], F32, "ncols")
        xb = [ar.alloc([128, D], F32, "xb") for _ in range(2)]
        hb = [ar.alloc([128, D], BF16, "hb")] * 2
        hT = ar.alloc([128, 8, 512], BF16, "hT")
        hg1 = ar.alloc([128, 8, 544], BF16, "hglu")
        c0f = ar.alloc([128, 8, 512], F32, "c0f")
        cbf = [ar.alloc([128, 512], BF16, "cbf")] * 2
        csq = [ar.alloc([128, 512], BF16, "csq")] * 2
        cT = ar.alloc([128, 8, 512], BF16, "cT")
        m2T = [ar.alloc([128, 8, 512], BF16, "m2T") for _ in range(1)]
        ft = [ar.alloc([128, 512], F32, "ft") for _ in range(5)]
        mu = ar.alloc([128, 512], F32, "mu")
        rsd = ar.alloc([128, 512], F32, "rsd")
        ss = ar.alloc([128, 8], F32, "ss")
        rstd = ar.alloc([128, 8], F32, "rstd")
        bcast_rows(wpre, W["mix_norm_pre"][l], "wpre")
        load_weight(wglu, mw[:, MW["glu_a"]:MW["glu_a"] + 2048], lambda k, ch: ("wglu", k, ch), 2048)
        load_weight(cwo, W["cnv_w_o"][l], lambda k, ch: ("cwo", k), 1024)
        ent = [W["cnv_pw1_b"][l].rearrange("(c p) -> c p", p=128),
               W["cnv_dw_b"][l].rearrange("(c p) -> c p", p=128),
               W["cnv_ln_g"][l].rearrange("(c p) -> c p", p=128),
               W["cnv_ln_b"][l].rearrange("(c p) -> c p", p=128),
               W["cnv_b_o"][l].rearrange("(c p) -> c p", p=128),
               W["cnv_dw_w"][l].rearrange("j (c p) -> (j c) p", p=128)]
        load_cols(cols, "cols", ent)
        PW, DWB, LNG, LNB, BO, DWW = 0, 16, 24, 32, 40, 48
        A("dve", lambda e: e.tensor_scalar(out=ncols, in0=cols[:, 0:16], scalar1=-1.0, scalar2=None, op0=ALU.mult),
          ["cols"], ["ncols"])
        for c in range(8):
            for j in range(31):
                A("dve" if (c * 31 + j) % 2 else "pool", lambda e, c=c, j=j: e.tensor_scalar(
                    out=dg[:, c, j, :], in0=ident_f, scalar1=cols[:, DWW + j * 8 + c:DWW + j * 8 + c + 1],
                    scalar2=None, op0=ALU.mult), ["cols", "identf"], [("dg", c)])
        A("pool", lambda e: e.memset(hg1[:, :, 0:32], 0.0), [], [("hgh", c) for c in range(8)])
        xi = [0]
        for ti in range(NT):
            cur = hg1
            slots = prenorm_hT(src, ti, xb, xi, 2, wpre, hb, hT, ss, rstd)
            for c in range(8):
                for k in range(8):
                    A("pe", lambda e, c=c, k=k: e.matmul(ps[:, 0, :], lhsT=wglu[:, k, c * 128:(c + 1) * 128],
                                                         rhs=hT[:, k, :], start=(k == 0), stop=(k == 7)),
                      [("wglu", k, 0)] + HT4, PK(0))
                for k in range(8):
                    A("pe", lambda e, c=c, k=k: e.matmul(ps[:, 1, :], lhsT=wglu[:, k, 1024 + c * 128:1024 + (c + 1) * 128],
                                                         rhs=hT[:, k, :], start=(k == 0), stop=(k == 7)),
                      [("wglu", k, 1)] + HT4, PK(1))
                f0 = ft[c % 2]
                f1 = ft[2 + c % 2]
                sigmoid_from(ps[:, 1, :], PK(1), f0, ("ft", c % 2), f1, ("ft", 2 + c % 2),
                             negbias=ncols[:, 8 + c:9 + c], bkeys=["ncols"])
                if ti > 0:
                    A("pool", lambda e, c=c, cur=cur: e.tensor_copy(out=cur[:, c, 0:32], in_=cur[:, c, 512:544]),
                      [("hgd", c)], [("hgh", c)])
                A("dve", lambda e, c=c, f1=f1, cur=cur: e.scalar_tensor_tensor(
                    out=cur[:, c, 32:544], in0=ps[:, 0, :], scalar=cols[:, PW + c:PW + c + 1], in1=f1,
                    op0=ALU.add, op1=ALU.mult), PK(0) + ["cols", ("ft", 2 + c % 2)], [("hgd", c)])
            for c in range(8):
                pb = 2 + c % 2
                for j in range(31):
                    A("pe", lambda e, c=c, j=j, pb=pb, cur=cur: e.matmul(
                        ps[:, pb, :], lhsT=dg[:, c, j, :], rhs=cur[:, c, 2 + j:2 + j + 512],
                        start=(j == 0), stop=(j == 30)), [("dg", c), ("hgd", c), ("hgh", c)], PK(pb))
                A("act", lambda e, c=c, pb=pb: e.activation(out=c0f[:, c, :], in_=ps[:, pb, :], func=AF.Identity,
                                                            bias=cols[:, DWB + c:DWB + c + 1]),
                  PK(pb) + ["cols"], [("c0f", c)])
                A("dve", lambda e, c=c: e.tensor_copy(out=cbf[c % 2], in_=c0f[:, c, :]), [("c0f", c)], [("cbf", 0)])
                A("act", lambda e, c=c: e.activation(out=csq[c % 2], in_=c0f[:, c, :], func=AF.Square),
                  [("c0f", c)], [("csq", 0)])
                A("pe", lambda e, c=c: e.matmul(ps[:, 4, :], lhsT=ones_b, rhs=cbf[c % 2], start=(c == 0), stop=(c == 7)),
                  ["onesb", ("cbf", 0)], PK(4))
                A("pe", lambda e, c=c: e.matmul(ps[:, 5, :], lhsT=ones_b, rhs=csq[c % 2], start=(c == 0), stop=(c == 7)),
                  ["onesb", ("csq", 0)], PK(5))
            A("dve", lambda e: e.tensor_scalar(out=mu, in0=ps[:, 4, :], scalar1=1.0 / D, scalar2=None, op0=ALU.mult),
              PK(4), ["mu"])
            A("dve", lambda e: e.tensor_tensor(out=rsd, in0=mu, in1=mu, op=ALU.mult), ["mu"], ["rsd"])
            A("dve", lambda e: e.scalar_tensor_tensor(out=rsd, in0=ps[:, 5, :], scalar=1.0 / D, in1=rsd,
                                                      op0=ALU.mult, op1=ALU.subtract), PK(5) + ["rsd"], ["rsd"])
            A("dve", lambda e: e.tensor_scalar(out=rsd, in0=rsd, scalar1=1e-5, scalar2=None, op0=ALU.add), ["rsd"], ["rsd"])
            A("act", lambda e: e.activation(out=rsd, in_=rsd, func=AF.Ln), ["rsd"], ["rsd"])
            A("act", lambda e: e.activation(out=rsd, in_=rsd, func=AF.Exp, scale=-0.5), ["rsd"], ["rsd"])
            for c in range(8):
                f0 = ft[c % 2]
                f1 = ft[2 + c % 2]
                f2 = ft[4]
                k0, k1, k2 = ("ft", c % 2), ("ft", 2 + c % 2), ("ft", 4)
                A("dve", lambda e, c=c, f0=f0: e.tensor_tensor(out=f0, in0=c0f[:, c, :], in1=mu, op=ALU.subtract),
                  [("c0f", c), "mu"], [k0])
                A("dve", lambda e, f0=f0: e.tensor_tensor(out=f0, in0=f0, in1=rsd, op=ALU.mult), [k0, "rsd"], [k0])
                A("act", lambda e, c=c, f0=f0: e.activation(out=f0, in_=f0, func=AF.Identity,
                                                            scale=cols[:, LNG + c:LNG + c + 1],
                                                            bias=cols[:, LNB + c:LNB + c + 1]), [k0, "cols"], [k0])
                sigmoid_from(f0, [k0], f1, k1, f2, k2)
                A("dve", lambda e, c=c, f0=f0, f2=f2: e.tensor_tensor(out=cT[:, c, :], in0=f0, in1=f2, op=ALU.mult),
                  [k0, k2], [("cT", c)])
            mt = m2T[0]
            for fo in range(8):
                for c in range(8):
                    A("pe", lambda e, fo=fo, c=c: e.matmul(ps[:, fo % 2, :], lhsT=cwo[:, c, fo * 128:(fo + 1) * 128],
                                                           rhs=cT[:, c, :], start=(c == 0), stop=(c == 7)),
                      [("cwo", c), ("cT", c)], PK(fo % 2))
                A("act", lambda e, fo=fo, mt=mt: e.activation(out=mt[:, fo, :], in_=ps[:, fo % 2, :], func=AF.Identity,
                                                              bias=cols[:, BO + fo:BO + fo + 1]),
                  PK(fo % 2) + ["cols"], [("m2T", 0)])
            DMA("sp", m2s[ti], mt, r=[("m2T", 0)], w=[("m2s", ti)])

    def mixer_B(l, src):
        ar.reset()
        S.barrier()
        mw = W["mix_w_in"][l]
        wqkv = ar.alloc([128, 8, 3072], BF16, "wqkv")
        wba = ar.alloc([128, 8, 16], BF16, "wba")
        dg = ar.alloc([128, 24, 4, 128], BF16, "dg4")
        wpre = ar.alloc([128, D], F32, "wpre")
        cols = ar.alloc([128, 128], F32, "cols")
        xb = [ar.alloc([128, D], F32, "xb") for _ in range(2)]
        hb = [ar.alloc([128, D], BF16, "hb")] * 2
        hT = ar.alloc([128, 8, 512], BF16, "hT")
        ss = ar.alloc([128, 8], F32, "ss")
        rstd = ar.alloc([128, 8], F32, "rstd")
        qT = ar.alloc([128, 8, 512], BF16, "qT")
        qdT = ar.alloc([128, 8, 512], BF16, "qdT")
        kT = ar.alloc([128, 8, 512], BF16, "kT")
        vT = ar.alloc([128, 8, 512], BF16, "vT")
        ogT = [ar.alloc([128, 8, 512], BF16, "ogT")] * 2
        pch = ar.alloc([128, 24, 4], BF16, "pch")
        pc = [ar.alloc([128, 516], BF16, "pc") for _ in range(2)]
        ft = [ar.alloc([128, 512], F32, "ft") for _ in range(6)]
        ysq = [ar.alloc([128, 512], BF16, "ysq")] * 2
        ebc = [ar.alloc([128, 512], F32, "ebc")] * 2
        dms = [ar.alloc([128, 512], BF16, "dms") for _ in range(2)]
        dmi = [ar.alloc([128, 512], BF16, "dmi") for _ in range(2)]
        dtb = ar.alloc([128, 32], F32, "dtb")
        nega = ar.alloc([128, 32], F32, "nega")
        sm = {nm: ar.alloc([128, 32], F32, nm) for nm in
              ("et", "beta", "nbeta", "t", "g", "Gc", "nGc", "Gt", "eG", "kds", "dch", "bEG")}
        Sst = ar.alloc([128, 8, 128], F32, "Sst")
        Sbf = ar.alloc([128, 8, 128], BF16, "Sbf")
        NG = 8
        maskPB = ar.alloc([128, 7, 256], BF16, "maskPB")
        ident2 = ar.alloc([128, 256], BF16, "ident2")
        DMA("pool", maskPB.rearrange("p a b -> p (a b)"), C["c_mpb"], w=["maskPB"])
        DMA("pool", ident2, C["c_id2"], w=["ident2"])
        kbe = [ar.alloc([128, 128], BF16, "kbe") for _ in range(NG)]
        kdc = [ar.alloc([128, 128], BF16, "kdc") for _ in range(NG)]
        vb = [ar.alloc([128, 128], BF16, "vb") for _ in range(NG)]
        qkm = [ar.alloc([128, 128], BF16, "qkm") for _ in range(NG)]
        qkmT = [ar.alloc([128, 128], BF16, "qkmT") for _ in range(NG)]
        Wn = [ar.alloc([128, 256], BF16, "Wn") for _ in range(NG)]
        TU = [ar.alloc([128, 256], BF16, "TU") for _ in range(NG)]
        YY = [ar.alloc([128, 256], BF16, "YY") for _ in range(NG)]
        uw = [ar.alloc([128, 256], BF16, "uw") for _ in range(NG)]
        vn = [ar.alloc([128, 128], BF16, "vn") for _ in range(NG)]
        normw = ar.alloc([128, 1], F32, "normw")
        gbt = [ar.alloc([128, 128], F32, "gbt")] * 2

        bcast_rows(wpre, W["mix_norm_pre"][l], "wpre")
        load_weight(wqkv, mw[:, 0:3072], lambda k, ch: ("wqkv", k, ch), 3072)
        load_weight(wba, mw[:, MW["ba"]:MW["ba"] + 16], lambda k, ch: ("wba", k), 16)
        load_cols(cols, "cols", [W["gdn_conv_w"][l].rearrange("j (c p) -> (j c) p", p=128)])
        DMA("sp", normw, W["gdn_norm_w"][l].rearrange("(p o) -> p o", o=1), w=["normw"])
        for j in range(4):
            DMA("sp", dtb[:, j * 8:(j + 1) * 8], W["gdn_dt_bias"][l].partition_broadcast(128), w=["dtb"])
            DMA("sp", nega[:, j * 8:(j + 1) * 8], W["gdn_a_log"][l].partition_broadcast(128), w=["nega"])
        A("act", lambda e: e.activation(out=nega, in_=nega, func=AF.Exp), ["nega"], ["nega"])
        A("dve", lambda e: e.tensor_scalar(out=nega, in0=nega, scalar1=-1.0, scalar2=None, op0=ALU.mult), ["nega"], ["nega"])
        for ci in range(24):
            for j in range(4):
                A("dve" if (ci + j) % 2 else "pool", lambda e, ci=ci, j=j: e.tensor_scalar(
                    out=dg[:, ci, j, :], in0=ident_f, scalar1=cols[:, j * 24 + ci:j * 24 + ci + 1],
                    scalar2=None, op0=ALU.mult), ["cols", "identf"], [("dg", ci)])
        A("pool", lambda e: e.memset(pch, 0.0), [], [("pch", ci) for ci in range(24)])
        A("pool", lambda e: e.memset(Sst, 0.0), [], [("S", h) for h in range(8)])
        A("pool", lambda e: e.memset(Sbf, 0.0), [], [("Sb", h) for h in range(8)])
        xi = [0]
        rr = [0]
        for ti in range(NT):
            slots = prenorm_hT(src, ti, xb, xi, 2, wpre, hb, hT, ss, rstd)
            og = ogT[ti % 2]
            for j in range(4):
                for k in range(8):
                    A("pe", lambda e, j=j, k=k: e.matmul(ps[:, 0, j * 16:(j + 1) * 16],
                                                         lhsT=hT[:, k, j * 128:(j + 1) * 128], rhs=wba[:, k, :],
                                                         start=(k == 0), stop=(k == 7)), [("wba", k)] + HT4, PK(0))
            lg = ps[:, 0, 0:64].rearrange("p (j t) -> p j t", j=4)
            v3 = lambda t: t.rearrange("p (j h) -> p j h", j=4)
            A("act", lambda e: e.activation(out=v3(sm["et"]), in_=lg[:, :, 0:8], func=AF.Exp, scale=-1.0), PK(0), ["et"])
            A("dve", lambda e: e.tensor_scalar(out=sm["et"], in0=sm["et"], scalar1=1.0, scalar2=None, op0=ALU.add), ["et"], ["et"])
            A("dve", lambda e: e.reciprocal(out=sm["beta"], in_=sm["et"]), ["et"], ["beta"])
            A("dve", lambda e: e.tensor_scalar(out=sm["nbeta"], in0=sm["beta"], scalar1=-1.0, scalar2=None, op0=ALU.mult),
              ["beta"], ["nbeta"])
            A("dve", lambda e: e.tensor_tensor(out=v3(sm["t"]), in0=lg[:, :, 8:16], in1=v3(dtb), op=ALU.add),
              PK(0) + ["dtb"], ["t"])
            A("act", lambda e: e.activation(out=sm["t"], in_=sm["t"], func=AF.Exp), ["t"], ["t"])
            A("act", lambda e: e.activation(out=sm["t"], in_=sm["t"], func=AF.Ln, bias=1.0), ["t"], ["t"])
            A("dve", lambda e: e.tensor_tensor(out=sm["g"], in0=sm["t"], in1=nega, op=ALU.mult), ["t", "nega"], ["g"])
            A("pe", lambda e: e.matmul(ps[:, 1, 0:32], lhsT=ltri_f, rhs=sm["g"], start=True, stop=True), ["ltri", "g"], PK(1))
            A("pe", lambda e: e.matmul(ps[:, 1, 32:64], lhsT=ones_f, rhs=sm["g"], start=True, stop=True), ["onesf", "g"], PK(1))
            A("dve", lambda e: e.tensor_copy(out=sm["Gc"], in_=ps[:, 1, 0:32]), PK(1), ["Gc"])
            A("dve", lambda e: e.tensor_scalar(out=sm["nGc"], in0=sm["Gc"], scalar1=-1.0, scalar2=None, op0=ALU.mult), ["Gc"], ["nGc"])
            A("dve", lambda e: e.tensor_tensor(out=sm["Gt"], in0=ps[:, 1, 32:64], in1=sm["Gc"], op=ALU.subtract),
              PK(1) + ["Gc"], ["Gt"])
            A("act", lambda e: e.activation(out=sm["kds"], in_=sm["Gt"], func=AF.Exp), ["Gt"], ["kds"])
            A("act", lambda e: e.activation(out=sm["dch"], in_=ps[:, 1, 32:64], func=AF.Exp), PK(1), ["dch"])
            A("act", lambda e: e.activation(out=sm["eG"], in_=sm["Gc"], func=AF.Exp), ["Gc"], ["eG"])
            A("dve", lambda e: e.tensor_tensor(out=sm["bEG"], in0=sm["eG"], in1=sm["beta"], op=ALU.mult),
              ["eG", "beta"], ["bEG"])
            def _s1_slice0(h, hp):
                for j in range(4):
                    gt = gbt[j % 2]
                    A("dve", lambda e, j=j, h=h, gt=gt: e.tensor_scalar(
                        out=gt, in0=ones_f, scalar1=sm["g"][:, j * 8 + h:j * 8 + h + 1], scalar2=None, op0=ALU.mult),
                        ["g", "onesf"], [("gbt", 0)])
                    A("pe", lambda e, j=j, gt=gt: e.matmul(
                        ps[:, 2, j * 128:(j + 1) * 128], lhsT=gt, rhs=ltri_f,
                        start=True, stop=True), [("gbt", 0), "ltri"], [("p", 2, j)])
                A("act", lambda e, hp=hp: e.activation(out=ebc[hp], in_=ps[:, 2, :], func=AF.Exp), PK(2), [("ebc", 0)])
                fd = ft[5]
                for j in range(4):
                    A("act", lambda e, j=j, h=h, fd=fd: e.activation(
                        out=fd[:, j * 128:(j + 1) * 128], in_=ps[:, 2, j * 128:(j + 1) * 128], func=AF.Relu,
                        bias=sm["nGc"][:, j * 8 + h:j * 8 + h + 1]), [("p", 2, j), "nGc"], [("ft", 5)])
                A("act", lambda e, fd=fd: e.activation(out=fd, in_=fd, func=AF.Exp, scale=-1.0), [("ft", 5)], [("ft", 5)])
                A("dve", lambda e, fd=fd, hp=hp: e.tensor_tensor(out=dms[hp], in0=fd, in1=mask_s, op=ALU.mult),
                  [("ft", 5), "masks"], [("dms", hp)])
                A("pool", lambda e, fd=fd, hp=hp: e.tensor_tensor(out=dmi[hp], in0=fd, in1=mask_i, op=ALU.mult),
                  [("ft", 5), "maski"], [("dmi", hp)])
            def _s1_qkv(h, hp, i):
                ci = i * 8 + h
                pcs = pc[(h * 3 + i) % 2]
                pk = ("pc", (h * 3 + i) % 2)
                for k in range(8):
                    A("pe", lambda e, ci=ci, k=k: e.matmul(ps[:, 0, :], lhsT=wqkv[:, k, ci * 128:(ci + 1) * 128],
                                                           rhs=hT[:, k, :], start=(k == 0), stop=(k == 7)),
                      [("wqkv", k, ci // 8)] + HT4, PK(0))
                A("act", lambda e, ci=ci, pcs=pcs: e.copy(out=pcs[:, 0:4], in_=pch[:, ci, :]), [("pch", ci)], [pk])
                A("act", lambda e, pcs=pcs: e.copy(out=pcs[:, 4:516], in_=ps[:, 0, :]), PK(0), [pk])
                A("act", lambda e, ci=ci, pcs=pcs: e.copy(out=pch[:, ci, :], in_=pcs[:, 512:516]), [pk], [("pch", ci)])
                for j in range(4):
                    A("pe", lambda e, ci=ci, j=j, pcs=pcs: e.matmul(
                        ps[:, 1, :], lhsT=dg[:, ci, j, :], rhs=pcs[:, 1 + j:1 + j + 512],
                        start=(j == 0), stop=(j == 3)), [("dg", ci), pk], PK(1))
                f0, f1, f2 = ft[0 + i % 2], ft[2 + i % 2], ft[4]
                k0, k1, k2 = ("ft", i % 2), ("ft", 2 + i % 2), ("ft", 4)
                sigmoid_from(ps[:, 1, :], PK(1), f0, k0, f1, k1)
                if i == 2:
                    A("dve", lambda e, f1=f1, h=h: e.tensor_tensor(out=vT[:, h, :], in0=ps[:, 1, :], in1=f1, op=ALU.mult),
                      PK(1) + [k1], [("vT", h)])
                    return
                A("dve", lambda e, f1=f1, f2=f2: e.tensor_tensor(out=f2, in0=ps[:, 1, :], in1=f1, op=ALU.mult),
                  PK(1) + [k1], [k2])
                yq = ysq[i % 2]
                A("act", lambda e, f2=f2, yq=yq: e.activation(out=yq, in_=f2, func=AF.Square), [k2], [("ysq", 0)])
                A("pe", lambda e, yq=yq: e.matmul(ps[:, 3, :], lhsT=ones_b, rhs=yq, start=True, stop=True),
                  ["onesb", ("ysq", 0)], PK(3))
                A("dve", lambda e, f0=f0: e.tensor_scalar(out=f0, in0=ps[:, 3, :], scalar1=1e-6, scalar2=None, op0=ALU.add),
                  PK(3), [k0])
                A("act", lambda e, f0=f0: e.activation(out=f0, in_=f0, func=AF.Ln), [k0], [k0])
                A("act", lambda e, f0=f0: e.activation(out=f0, in_=f0, func=AF.Exp, scale=-0.5), [k0], [k0])
                if i == 1:
                    A("dve", lambda e, f0=f0, f2=f2, h=h: e.tensor_tensor(out=kT[:, h, :], in0=f2, in1=f0, op=ALU.mult),
                      [k0, k2], [("kT", h)])
                else:
                    A("dve", lambda e, f0=f0, f2=f2: e.scalar_tensor_tensor(
                        out=f2, in0=f2, scalar=float(128 ** -0.5), in1=f0, op0=ALU.mult, op1=ALU.mult), [k0, k2], [k2])
                    A("act", lambda e, f2=f2, h=h: e.copy(out=qT[:, h, :], in_=f2), [k2], [("qT", h)])
                    A("dve", lambda e, f2=f2, h=h, hp=hp: e.tensor_tensor(out=qdT[:, h, :], in0=f2, in1=ebc[hp], op=ALU.mult),
                      [k2, ("ebc", 0)], [("qdT", h)])
            def _chunks_pair(hA):
                G = range(8)

                def hd(g):
                    return hA + g // 4

                def col(nm, g):
                    c = (g % 4) * 8 + hd(g)
                    return sm[nm][:, c:c + 1]

                def jb(g):
                    return slice((g % 4) * 128, (g % 4 + 1) * 128)

                def pq(g, half):
                    o = (g % 2) * 256 + half * 128
                    return ps[:, g // 2, o:o + 128]

                def pq2(g):
                    o = (g % 2) * 256
                    return ps[:, g // 2, o:o + 256]

                def PKg(g, half=None):
                    return [("p", g // 2, 0)]

                def qs(g, s_):
                    i_ = g * 2 + s_
                    return psb[:, i_ // 8, (i_ % 8) * 128:(i_ % 8) * 128 + 128]

                def QKg(g, s_):
                    return [("q", (g * 2 + s_) // 8, 0)]
                for g in G:
                    h = hd(g)
                    A("pe", lambda e, g=g, h=h: e.transpose(out=qs(g, 0), in_=kT[:, h, jb(g)], identity=ident_b),
                      [("kT", h), "identb"], QKg(g, 0))
                    A("pe", lambda e, g=g, h=h: e.transpose(out=qs(g, 1), in_=vT[:, h, jb(g)], identity=ident_b),
                      [("vT", h), "identb"], QKg(g, 1))
                    A("pe", lambda e, g=g, h=h: e.matmul(pq(g, 0), lhsT=kT[:, h, jb(g)], rhs=kT[:, h, jb(g)],
                                                         start=True, stop=True), [("kT", h)], PKg(g, 0))
                    A("pe", lambda e, g=g, h=h: e.matmul(pq(g, 1), lhsT=qT[:, h, jb(g)], rhs=kT[:, h, jb(g)],
                                                         start=True, stop=True), [("qT", h), ("kT", h)], PKg(g, 1))
                for g in G:
                    hp = hd(g) % 2
                    A("act", lambda e, g=g, c=col("bEG", g): e.activation(out=kbe[g], in_=qs(g, 0), func=AF.Identity, scale=c),
                      QKg(g, 0) + ["bEG"], [("kbe", g)])
                    A("act", lambda e, g=g, c=col("kds", g): e.activation(out=kdc[g], in_=qs(g, 0), func=AF.Identity, scale=c),
                      QKg(g, 0) + ["kds"], [("kdc", g)])
                    A("act", lambda e, g=g, c=col("beta", g): e.activation(out=vb[g], in_=qs(g, 1), func=AF.Identity, scale=c),
                      QKg(g, 1) + ["beta"], [("vb", g)])
                    A("dve", lambda e, g=g, hp=hp, c=col("nbeta", g): e.scalar_tensor_tensor(
                        out=Wn[g][:, 0:128], in0=pq(g, 0), scalar=c, in1=dms[hp][:, jb(g)], op0=ALU.mult, op1=ALU.mult),
                        PKg(g, 0) + ["nbeta", ("dms", hp)], [("Wn", g)])
                    A("dve", lambda e, g=g, hp=hp: e.tensor_tensor(out=qkm[g], in0=pq(g, 1), in1=dmi[hp][:, jb(g)], op=ALU.mult),
                      PKg(g, 1) + [("dmi", hp)], [("qkm", g)])
                for g in G:
                    A("pe", lambda e, g=g: e.transpose(out=qs(g, 0), in_=Wn[g][:, 0:128], identity=ident_b),
                      [("Wn", g), "identb"], QKg(g, 0))
                    A("pe", lambda e, g=g: e.transpose(out=qs(g, 1), in_=qkm[g], identity=ident_b),
                      [("qkm", g), "identb"], QKg(g, 1))
                for g in G:
                    A("act", lambda e, g=g: e.copy(out=Wn[g][:, 128:256], in_=qs(g, 0)), QKg(g, 0), [("Wn", g)])
                    A("act", lambda e, g=g: e.copy(out=qkmT[g], in_=qs(g, 1)), QKg(g, 1), [("qkmT", g)])
                for g in G:
                    A("dve", lambda e, g=g: e.tensor_tensor(out=YY[g], in0=Wn[g], in1=maskPB[:, 0, :], op=ALU.mult),
                      [("Wn", g), "maskPB"], [("YY", g)])
                    A("dve", lambda e, g=g: e.tensor_tensor(out=TU[g], in0=YY[g], in1=ident2, op=ALU.add),
                      [("YY", g), "ident2"], [("TU", g)])

                def _level(lvl):
                    last = lvl == 6
                    for g in G:
                        if not last:
                            A("pe", lambda e, g=g: e.matmul(pq(g, 0), lhsT=Wn[g][:, 128:256], rhs=TU[g][:, 0:128],
                                                            start=True, stop=True), [("Wn", g), ("TU", g)], PKg(g, 0))
                        A("pe", lambda e, g=g: e.matmul(pq(g, 1), lhsT=Wn[g][:, 0:128], rhs=TU[g][:, 128:256],
                                                        start=True, stop=True), [("Wn", g), ("TU", g)], PKg(g, 1))
                    for g in G:
                        if last:
                            A("dve", lambda e, g=g: e.tensor_tensor(out=YY[g][:, 128:256], in0=pq(g, 1),
                                                                    in1=maskPB[:, lvl, 128:256], op=ALU.mult),
                              PKg(g, 1) + ["maskPB"], [("YY", g)])
                        else:
                            A("dve", lambda e, g=g: e.tensor_tensor(out=YY[g], in0=pq2(g), in1=maskPB[:, lvl, :], op=ALU.mult),
                              PKg(g) + ["maskPB"], [("YY", g)])
                    for g in G:
                        if not last:
                            A("pe", lambda e, g=g: e.matmul(pq(g, 0), lhsT=ident_b, rhs=TU[g][:, 0:128], start=True, stop=False),
                              [("TU", g), "identb"], PKg(g, 0))
                            A("pe", lambda e, g=g: e.matmul(pq(g, 0), lhsT=TU[g][:, 128:256], rhs=YY[g][:, 0:128], start=False, stop=True),
                              [("TU", g), ("YY", g)], PKg(g, 0))
                        A("pe", lambda e, g=g: e.matmul(pq(g, 1), lhsT=ident_b, rhs=TU[g][:, 128:256], start=True, stop=False),
                          [("TU", g), "identb"], PKg(g, 1))
                        A("pe", lambda e, g=g: e.matmul(pq(g, 1), lhsT=TU[g][:, 0:128], rhs=YY[g][:, 128:256], start=False, stop=True),
                          [("TU", g), ("YY", g)], PKg(g, 1))
                    for g in G:
                        if last:
                            A("act", lambda e, g=g: e.copy(out=TU[g][:, 128:256], in_=pq(g, 1)), PKg(g, 1), [("TU", g)])
                        else:
                            A("act", lambda e, g=g: e.copy(out=TU[g], in_=pq2(g)), PKg(g), [("TU", g)])
                for lvl_ in range(1, 7):
                    _level(lvl_)
                for g in G:
                    A("pe", lambda e, g=g: e.matmul(pq(g, 0), lhsT=TU[g][:, 128:256], rhs=vb[g], start=True, stop=True),
                      [("TU", g), ("vb", g)], PKg(g, 0))
                    A("pe", lambda e, g=g: e.matmul(pq(g, 1), lhsT=kbe[g], rhs=TU[g][:, 128:256], start=True, stop=True),
                      [("TU", g), ("kbe", g)], PKg(g, 1))
                for g in G:
                    A("act", lambda e, g=g: e.copy(out=uw[g], in_=pq2(g)), PKg(g), [("uw", g)])

            def _rec(h, hh, j):
                g = hh * 4 + j
                jb = slice(j * 128, (j + 1) * 128)
                A("pe", lambda e: e.matmul(ps[:, hh, 0:128], lhsT=uw[g][:, 128:256], rhs=Sbf[:, h, :], start=True, stop=True),
                  [("uw", g), ("Sb", h)], [("p", hh, 0)])
                A("dve", lambda e: e.tensor_tensor(out=vn[g], in0=uw[g][:, 0:128], in1=ps[:, hh, 0:128], op=ALU.subtract),
                  [("uw", g), ("p", hh, 0)], [("vn", g)])
                A("pe", lambda e: e.matmul(ps[:, 4 + hh, jb], lhsT=Sbf[:, h, :], rhs=qdT[:, h, jb], start=True, stop=False),
                  [("Sb", h), ("qdT", h)], [("p", 4 + hh, j)])
                A("pe", lambda e: e.matmul(ps[:, 4 + hh, jb], lhsT=vn[g], rhs=qkmT[g], start=False, stop=True),
                  [("vn", g), ("qkmT", g)], [("p", 4 + hh, j)])
                A("pe", lambda e: e.matmul(ps[:, hh, 128:256], lhsT=kdc[g], rhs=vn[g], start=True, stop=True),
                  [("kdc", g), ("vn", g)], [("p", hh, 1)])
                A("dve", lambda e, c=sm["dch"][:, j * 8 + h:j * 8 + h + 1]: e.scalar_tensor_tensor(
                    out=Sst[:, h, :], in0=Sst[:, h, :], scalar=c, in1=ps[:, hh, 128:256], op0=ALU.mult, op1=ALU.add),
                    [("S", h), "dch", ("p", hh, 1)], [("S", h)])
                A("dve", lambda e: e.tensor_copy(out=Sbf[:, h, :], in_=Sst[:, h, :]), [("S", h)], [("Sb", h)])

            def _headout(h, hh):
                yq = ysq[0]
                f0 = ft[hh]
                po, pn = 4 + hh, 2 + hh
                A("act", lambda e: e.activation(out=yq, in_=ps[:, po, :], func=AF.Square), PK(po), [("ysq", 0)])
                A("pe", lambda e: e.matmul(ps[:, pn, :], lhsT=ones_b, rhs=yq, start=True, stop=True),
                  ["onesb", ("ysq", 0)], PK(pn))
                A("dve", lambda e: e.tensor_scalar(out=f0, in0=ps[:, pn, :], scalar1=1.0 / 128, scalar2=1e-6,
                                                   op0=ALU.mult, op1=ALU.add), PK(pn), [("ft", hh)])
                A("act", lambda e: e.activation(out=f0, in_=f0, func=AF.Ln), [("ft", hh)], [("ft", hh)])
                A("act", lambda e: e.activation(out=f0, in_=f0, func=AF.Exp, scale=-0.5), [("ft", hh)], [("ft", hh)])
                A("dve", lambda e: e.scalar_tensor_tensor(
                    out=og[:, h, :], in0=ps[:, po, :], scalar=normw[:, 0:1], in1=f0, op0=ALU.mult, op1=ALU.mult),
                    PK(po) + ["normw", ("ft", hh)], [("og", 0)])
            for hA in range(0, 8, 2):
                for hh in range(2):
                    _s1_slice0(hA + hh, (hA + hh) % 2)
                    for i_ in range(3):
                        _s1_qkv(hA + hh, (hA + hh) % 2, i_)
                _chunks_pair(hA)
                for j in range(4):
                    for hh in range(2):
                        _rec(hA + hh, hh, j)
                for hh in range(2):
                    _headout(hA + hh, hh)
            DMA("sp", ogs[ti], og, r=[("og", 0)], w=[("ogs", ti)])

    def mixer_C(l, src, dst):
        ar.reset()
        S.barrier()
        mw = W["mix_w_in"][l]
        wz = ar.alloc([128, 8, 1024], BF16, "wz")
        wga = ar.alloc([128, 8, 1024], BF16, "wga")
        wgb = ar.alloc([128, 8, 1024], BF16, "wgb")
        gwo = ar.alloc([128, 8, 1024], BF16, "gwo")
        wmo = ar.alloc([128, 8, 1024], BF16, "wmo")
        wpre = ar.alloc([128, D], F32, "wpre")
        wpost = ar.alloc([128, D], F32, "wpost")
        xb = [ar.alloc([128, D], F32, "xb") for _ in range(6)]
        hb = [ar.alloc([128, D], BF16, "hb") for _ in range(2)]
        hT = ar.alloc([128, 8, 512], BF16, "hT")
        ogn = [ar.alloc([128, 8, 512], BF16, "ogn") for _ in range(2)]
        m2T = [ar.alloc([128, 8, 512], BF16, "m2T") for _ in range(2)]
        ogT = ar.alloc([128, 8, 512], BF16, "ogT")
        mT = ar.alloc([128, 8, 512], BF16, "mT")
        ft = [ar.alloc([128, 512], F32, "ft") for _ in range(6)]
        tmp = [ar.alloc([128, 512], F32, "tmp") for _ in range(2)]
        ss = ar.alloc([128, 8], F32, "ss")
        rstd = ar.alloc([128, 8], F32, "rstd")
        ss2 = ar.alloc([128, 8], F32, "ss2")
        rstd2 = ar.alloc([128, 4], F32, "rstd2")
        bcast_rows(wpre, W["mix_norm_pre"][l], "wpre")
        bcast_rows(wpost, W["mix_norm_post"][l], "wpost")
        load_weight(wz, mw[:, MW["z"]:MW["z"] + 1024], lambda k, ch: ("wz", k), 1024)
        load_weight(wga, mw[:, MW["ga"]:MW["ga"] + 1024], lambda k, ch: ("wga", k), 1024)
        load_weight(wgb, mw[:, MW["gb"]:MW["gb"] + 1024], lambda k, ch: ("wgb", k), 1024)
        load_weight(gwo, W["gdn_w_o"][l], lambda k, ch: ("gwo", k), 1024)
        load_weight(wmo, W["mix_w_out"][l], lambda k, ch: ("wmo", k), 1024)
        xi = [0]
        for ti in range(NT):
            on = ogn[ti % 2]
            mt = m2T[ti % 2]
            DMA("sp", on, ogs[ti], r=[("ogs", ti)], w=[("ogn", ti % 2)])
            DMA("sp", mt, m2s[ti], r=[("m2s", ti)], w=[("m2T", ti % 2)])
            slots = prenorm_hT(src, ti, xb, xi, 6, wpre, hb, hT, ss, rstd)
            for h in range(8):
                for k in range(8):
                    A("pe", lambda e, h=h, k=k: e.matmul(ps[:, h % 2, :], lhsT=wz[:, k, h * 128:(h + 1) * 128], rhs=hT[:, k, :],
                                                         start=(k == 0), stop=(k == 7)), [("wz", k)] + HT4, PK(h % 2))
                f0, f1 = ft[h % 2], ft[2 + h % 2]
                sigmoid_from(ps[:, h % 2, :], PK(h % 2), f0, ("ft", h % 2), f1, ("ft", 2 + h % 2))
                A("dve", lambda e, h=h, f1=f1: e.tensor_tensor(out=f1, in0=ps[:, h % 2, :], in1=f1, op=ALU.mult),
                  PK(h % 2) + [("ft", 2 + h % 2)], [("ft", 2 + h % 2)])
                A("dve", lambda e, h=h, f1=f1, on=on: e.tensor_tensor(out=ogT[:, h, :], in0=on[:, h, :], in1=f1, op=ALU.mult),
                  [("ogn", ti % 2), ("ft", 2 + h % 2)], [("ogT", h)])
            for fo in range(8):
                pya, pga = (2, 3) if fo % 2 == 0 else (0, 1)
                for h in range(8):
                    A("pe", lambda e, fo=fo, h=h, pya=pya: e.matmul(ps[:, pya, :], lhsT=gwo[:, h, fo * 128:(fo + 1) * 128], rhs=ogT[:, h, :],
                                                           start=(h == 0), stop=(h == 7)), [("gwo", h), ("ogT", h)], PK(pya))
                for k in range(8):
                    A("pe", lambda e, fo=fo, k=k, pga=pga: e.matmul(ps[:, pga, :], lhsT=wga[:, k, fo * 128:(fo + 1) * 128], rhs=hT[:, k, :],
                                                           start=(k == 0), stop=(k == 7)), [("wga", k)] + HT4, PK(pga))
                f0, f1, f2 = ft[fo % 2], ft[2 + fo % 2], ft[4 + fo % 2]
                sigmoid_from(ps[:, pga, :], PK(pga), f0, ("ft", fo % 2), f1, ("ft", 2 + fo % 2))
                A("dve", lambda e, f1=f1, f2=f2, pya=pya: e.tensor_tensor(out=f2, in0=ps[:, pya, :], in1=f1, op=ALU.mult),
                  PK(pya) + [("ft", 2 + fo % 2)], [("ft", 4 + fo % 2)])
                for k in range(8):
                    A("pe", lambda e, fo=fo, k=k, pga=pga: e.matmul(ps[:, pga, :], lhsT=wgb[:, k, fo * 128:(fo + 1) * 128], rhs=hT[:, k, :],
                                                           start=(k == 0), stop=(k == 7)), [("wgb", k)] + HT4, PK(pga))
                sigmoid_from(ps[:, pga, :], PK(pga), f0, ("ft", fo % 2), f1, ("ft", 2 + fo % 2))
                A("dve", lambda e, fo=fo, f1=f1, mt=mt: e.tensor_tensor(out=f1, in0=f1, in1=mt[:, fo, :], op=ALU.mult),
                  [("ft", 2 + fo % 2), ("m2T", ti % 2)], [("ft", 2 + fo % 2)])
                A("dve", lambda e, fo=fo, f1=f1, f2=f2: e.tensor_tensor(out=mT[:, fo, :], in0=f2, in1=f1, op=ALU.add),
                  [("ft", 4 + fo % 2), ("ft", 2 + fo % 2)], [("mT", fo)])
            for b in range(4):
                pbase = 4 if b % 2 == 0 else 0
                for half in range(2):
                    po = pbase + half
                    for fo in range(8):
                        A("pe", lambda e, fo=fo, b=b, half=half, po=po: e.matmul(
                            ps[:, po, :], lhsT=mT[:, fo, b * 128:(b + 1) * 128], rhs=wmo[:, fo, half * 512:(half + 1) * 512],
                            start=(fo == 0), stop=(fo == 7)), [("mT", fo), ("wmo", fo)], PK(po))
                postnorm_residual(b, slots[b], xb, wpost, tmp, ss2, rstd2, 1.0, dst, ti, pbase)

    cur = x_in
    for l in range(depth):
        if phases is None or "ffn1" in phases:
            ffn_phase(l, "ffn1", cur, y_out)
            cur = y_out
        if phases is None or "mix" in phases or "mixA" in phases:
            mixer_A(l, cur)
        if phases is None or "mix" in phases or "mixB" in phases:
            mixer_B(l, cur)
        if phases is None or "mix" in phases or "mixC" in phases:
            mixer_C(l, cur, y_out)
            cur = y_out
        if phases is None or "ffn2" in phases:
            ffn_phase(l, "ffn2", cur, y_out)
            cur = y_out

    S.barrier()
    S.add("sp", lambda e: e.nop())
    stack = ExitStack()
    S.emit_all(stack)
    stack.close()
    return nc


WEIGHT_SHAPES = {
    "ffn1_norm_pre": (D,), "ffn1_norm_post": (D,), "ffn1_w_in": (D, 2 * DFF), "ffn1_w_out": (DFF, D),
    "mix_norm_pre": (D,), "mix_norm_post": (D,), "mix_w_in": (D, PIN),
    "gdn_conv_w": (4, 3072), "gdn_a_log": (8,), "gdn_dt_bias": (8,), "gdn_norm_w": (128,), "gdn_w_o": (D, D),
    "cnv_pw1_b": (2048,), "cnv_dw_w": (31, D), "cnv_dw_b": (D,), "cnv_ln_g": (D,), "cnv_ln_b": (D,),
    "cnv_w_o": (D, D), "cnv_b_o": (D,), "mix_w_out": (D, D),
    "ffn2_norm_pre": (D,), "ffn2_norm_post": (D,), "ffn2_w_in": (D, 2 * DFF), "ffn2_w_out": (DFF, D),
}
CONST_SHAPES = {"c_ident": [128, 128], "c_ltri": [128, 128], "c_ones": [128, 128],
                "c_masks": [128, 512], "c_maski": [128, 512], "c_mpb": [128, 1792], "c_id2": [128, 256]}


def consts():
    i = np.arange(128)
    ltri = (i[:, None] <= i[None, :]).astype(np.float32)
    ms = (i[None, :] < i[:, None]).astype(np.float32)
    mi = (i[None, :] <= i[:, None]).astype(np.float32)
    mp = np.zeros((128, 7, 128), np.float32)
    for lv in range(7):
        blk = i // (1 << lv)
        mp[:, lv, :] = ((blk[:, None] // 2 == blk[None, :] // 2) & (blk[:, None] % 2 == 1)
                        & (blk[None, :] % 2 == 0)).astype(np.float32)
    mb = np.ascontiguousarray(mp.transpose(2, 1, 0))
    mpb = np.concatenate([mp, mb], axis=2)
    return {"c_mpb": np.ascontiguousarray(mpb).reshape(128, 1792),
            "c_id2": np.tile(np.eye(128, dtype=np.float32), (1, 2)),
            "c_ident": np.eye(128, dtype=np.float32), "c_ltri": ltri,
            "c_ones": np.ones((128, 128), np.float32),
            "c_masks": np.tile(ms, (1, 4)), "c_maski": np.tile(mi, (1, 4))}


def kernel(**inputs):
    x = np.ascontiguousarray(inputs["x"], dtype=np.float32)
    B, T, _ = x.shape
    nc = build_program(T, DEPTH)
    shared = {k: np.ascontiguousarray(inputs[k], dtype=np.float32) for k in WEIGHT_SHAPES}
    shared.update(consts())
    in_maps = []
    for b in range(B):
        m = dict(shared)
        m["x"] = x[b]
        in_maps.append(m)
    res = run_bass_kernel_spmd(nc, in_maps, core_ids=list(range(B)))
    return np.stack([np.asarray(r["y"]).reshape(T, D) for r in res.results], axis=0).astype(np.float32)
```

```python
import numpy as np
import concourse.bass as bass
import concourse.mybir as mybir
from concourse.bass_utils import run_bass_kernel_spmd
from contextlib import ExitStack

F32 = mybir.dt.float32
BF16 = mybir.dt.bfloat16
AF = mybir.ActivationFunctionType
ALU = mybir.AluOpType

D = 1024
DFF = 2816
NH = 8
DEPTH = 2
PIN = 8208
NDMA_SEMS = 24


class Op:
    __slots__ = ("eng", "emit", "dma", "deps", "has_dep", "token", "prev_token", "idx")

    def __init__(self, eng, emit, dma):
        self.eng = eng
        self.emit = emit
        self.dma = dma
        self.deps = ()
        self.has_dep = False
        self.token = None
        self.prev_token = None


class Sched:
    ENGS = ("pe", "act", "dve", "pool", "sp")

    def __init__(self, nc):
        self.nc = nc
        self.eng_ops = {e: [] for e in self.ENGS}
        self.last_w = {}
        self.readers = {}
        self.pending = {e: [] for e in self.ENGS}
        self.last_compute = {}
        self.dmas = []
        self.live_dmas = []
        self.nops = 0
        self.bank_rd = {}

    @staticmethod
    def _norm(keys):
        return [k[:2] if (isinstance(k, tuple) and len(k) == 3 and k[0] in ("p", "q")) else k for k in keys]

    def add(self, eng, emit, reads=(), writes=(), dma=False):
        reads = self._norm(reads)
        writes = self._norm(writes)
        op = Op(eng, emit, dma)
        op.idx = self.nops
        self.nops += 1
        deps = {}
        psum_banks = set()

        def add_dep(d):
            if d.dma:
                deps[("d", d.idx)] = d
            else:
                if eng == "pe" and d.eng == "pe":
                    return
                cur = deps.get(d.eng)
                if cur is None or cur.idx < d.idx:
                    deps[d.eng] = d

        for r in reads:
            w = self.last_w.get(r)
            if w is not None:
                add_dep(w)
            if isinstance(r, tuple) and r[0] in ("p", "q"):
                bk = (r[0], r[1])
                last = self.bank_rd.get(bk)
                if last is not None and last.eng != eng:
                    add_dep(last)
                psum_banks.add(bk)
        for w in writes:
            lw = self.last_w.get(w)
            if lw is not None:
                add_dep(lw)
            rd = self.readers.get(w)
            if rd:
                for d in rd.values():
                    add_dep(d)
        for d in self.pending[eng]:
            add_dep(d)
        self.pending[eng] = []
        op.deps = tuple(deps.values())
        for d in op.deps:
            d.has_dep = True
        for bk in psum_banks:
            self.bank_rd[bk] = op
        for r in reads:
            rd = self.readers.setdefault(r, {})
            rd[("d", op.idx) if dma else eng] = op
        for w in writes:
            self.last_w[w] = op
            self.readers[w] = {}
        self.eng_ops[eng].append(op)
        if dma:
            self.dmas.append(op)
            self.live_dmas.append(op)
        else:
            self.last_compute[eng] = op
        return op

    def barrier(self):
        toks = list(self.last_compute.values()) + list(self.live_dmas)
        for e in self.ENGS:
            self.pending[e] = list(toks)
        self.live_dmas = []
        self.last_w = {}
        self.readers = {}
        self.bank_rd = {}

    def emit_all(self, stack):
        nc = self.nc
        sems = {}
        for e in ("pe", "act", "dve", "pool"):
            sems[e] = stack.enter_context(nc.semaphore("c_" + e))
            cnt = 0
            for op in self.eng_ops[e]:
                if op.dma:
                    continue
                if op.has_dep:
                    cnt += 1
                    op.token = (sems[e], cnt)
        dsems = [stack.enter_context(nc.semaphore("d%d" % i)) for i in range(NDMA_SEMS)]
        counts = [0] * NDMA_SEMS
        for i, op in enumerate(self.dmas):
            s = i % NDMA_SEMS
            n = counts[s]
            if n > 0:
                op.prev_token = (dsems[s], 16 * n)
            counts[s] = n + 1
            op.token = (dsems[s], 16 * (n + 1))
        block = stack.enter_context(nc.Block())

        def run(eng_name):
            def body(eng):
                waited = {}
                for op in self.eng_ops[eng_name]:
                    toks = [d.token for d in op.deps]
                    if op.prev_token is not None:
                        toks.append(op.prev_token)
                    for (sem, val) in toks:
                        key = id(sem)
                        if waited.get(key, 0) < val:
                            eng.wait_ge(sem, val)
                            waited[key] = val
                    inst = op.emit(eng)
                    if op.dma:
                        inst.then_inc(op.token[0], 16)
                    elif op.has_dep:
                        inst.then_inc(op.token[0], 1)
            return body

        block.tensor(run("pe"))
        block.scalar(run("act"))
        block.vector(run("dve"))
        block.gpsimd(run("pool"))
        block.sync(run("sp"))


class Arena:
    def __init__(self, nc, limit):
        self.nc = nc
        self.limit = limit
        self.base = 20608
        self.off = 20608
        self.n = 0

    def alloc(self, shape, dtype, name="t"):
        esz = 2 if dtype == BF16 else 4
        per_part = esz
        for s in shape[1:]:
            per_part *= s
        off = (self.off + 63) // 64 * 64
        assert off + per_part <= self.limit, ("SBUF overflow", name, off, per_part, self.limit)
        self.off = off + per_part
        self.n += 1
        h = self.nc.alloc_sbuf_tensor_at("%s_%d" % (name, self.n), list(shape), dtype, offset=off)
        return h.ap()

    def mark_persistent(self):
        self.base = self.off

    def reset(self):
        self.off = self.base


MW = {"qkv": 0, "z": 3072, "ba": 4096, "glu_a": 4112, "glu_g": 5136, "ga": 6160, "gb": 7184}


def build_program(T, depth, phases=None):
    nc = bass.Bass("TRN2", target_bir_lowering=False)
    NT = T // 512

    def din(name, shape):
        return nc.dram_tensor(name, list(shape), F32, kind="ExternalInput").ap()

    x_in = din("x", [T, D])
    y_out = nc.dram_tensor("y", [T, D], F32, kind="ExternalOutput").ap()
    W = {}
    for nm, shp in WEIGHT_SHAPES.items():
        W[nm] = din(nm, (depth,) + tuple(shp))
    C = {}
    for nm, shp in CONST_SHAPES.items():
        C[nm] = din(nm, shp)
    m2s = nc.dram_tensor("m2s", [NT, 128, 8, 512], BF16, kind="Internal").ap()
    ogs = nc.dram_tensor("ogs", [NT, 128, 8, 512], BF16, kind="Internal").ap()

    S = Sched(nc)
    ar = Arena(nc, nc.SBUF_PARTITION_SIZE_BYTES)

    def A(eng, fn, r=(), w=()):
        return S.add(eng, fn, reads=r, writes=w)

    def DMA(q, out, in_, r=(), w=()):
        return S.add(q, lambda e: e.dma_start(out=out, in_=in_), reads=r, writes=w, dma=True)

    def PK(b):
        return [("p", b, s) for s in range(4)]

    def QK(b):
        return [("q", b, s) for s in range(8)]

    ident_f = ar.alloc([128, 128], F32, "identf")
    ident_b = ar.alloc([128, 128], BF16, "identb")
    ltri_f = ar.alloc([128, 128], F32, "ltri")
    ones_f = ar.alloc([128, 128], F32, "onesf")
    ones_b = ar.alloc([128, 128], BF16, "onesb")
    mask_s = ar.alloc([128, 512], F32, "masks")
    mask_i = ar.alloc([128, 512], F32, "maski")
    cm05 = ar.alloc([128, 8], F32, "cm05")
    DMA("sp", ident_f, C["c_ident"], w=["identf"])
    DMA("sp", ltri_f, C["c_ltri"], w=["ltri"])
    DMA("sp", ones_f, C["c_ones"], w=["onesf"])
    DMA("sp", mask_s, C["c_masks"], w=["masks"])
    DMA("sp", mask_i, C["c_maski"], w=["maski"])
    A("dve", lambda e: e.tensor_copy(out=ident_b, in_=ident_f), ["identf"], ["identb"])
    A("dve", lambda e: e.tensor_copy(out=ones_b, in_=ones_f), ["onesf"], ["onesb"])
    A("pool", lambda e: e.memset(cm05, -0.5), [], ["cm05"])
    ar.mark_persistent()

    ps = nc.alloc_psum_tensor("ps", [128, 6, 512], F32).ap()
    psb = nc.alloc_psum_tensor("psb", [128, 2, 1024], BF16).ap()

    def bcast_rows(dst, src_row, key):
        DMA("sp", dst, src_row.partition_broadcast(128), w=[key])

    def load_weight(dst, src2d, keyfn, ncols, chunk=1024):
        for c0 in range(0, ncols, chunk):
            c1 = min(ncols, c0 + chunk)
            for k in range(8):
                DMA("pool", dst[:, k, c0:c1], src2d[k * 128:(k + 1) * 128, c0:c1], w=[keyfn(k, c0 // chunk)])

    def load_cols(dst, dkey, entries):
        raw = ar.alloc([128, 128], F32, "raw")
        r0 = 0
        allrows = []
        for src in entries:
            R = src.shape[0]
            off = 0
            while off < R:
                n = min(R - off, 128 - (r0 % 128)) if (r0 % 128) else min(R - off, 128)
                allrows.append((src[off:off + n, :], r0, n))
                r0 += n
                off += n
        total = r0
        g = 0
        while g * 128 < total:
            rows = [(a, r, n) for (a, r, n) in allrows if r // 128 == g]
            nr = sum(n for (_, _, n) in rows)
            key = ("raw", g)
            for (a, r, n) in rows:
                DMA("sp", raw[r % 128:r % 128 + n, :], a, w=[key])
            A("pe", lambda e, nr=nr: e.transpose(out=ps[:, 5, 0:nr], in_=raw[0:nr, :], identity=ident_f[0:nr, 0:nr]),
              [key, "identf"], PK(5))
            A("dve", lambda e, nr=nr, g=g: e.tensor_copy(out=dst[:, g * 128:g * 128 + nr], in_=ps[:, 5, 0:nr]),
              PK(5), [dkey])
            g += 1
            if g * 128 < total:
                S.last_w[("raw", g)] = S.last_compute["pe"]
        return total

    def prenorm_hT(src, ti, xb, xi, nbuf, wpre, hb, hT, ss, rstd):
        slots = []
        for b in range(4):
            sl = xi[0] % nbuf
            xi[0] += 1
            slots.append(sl)
            DMA("sp", xb[sl], src[ti * 512 + b * 128:ti * 512 + (b + 1) * 128, :],
                r=[("xd", ti, b)], w=[("xb", sl)])
            hbb = hb[b % 2]
            hk = ("hb", id(hbb))
            A("act", lambda e, sl=sl, hbb=hbb, b=b: e.activation(
                out=hbb, in_=xb[sl], func=AF.Square, accum_out=ss[:, b:b + 1]),
                [("xb", sl)], [hk, ("ss", b)])
            A("dve", lambda e, b=b: e.tensor_scalar(
                out=rstd[:, b:b + 1], in0=ss[:, b:b + 1], scalar1=1.0 / D, scalar2=1e-6,
                op0=ALU.mult, op1=ALU.add), [("ss", b)], [("rstd", b)])
            A("pool", lambda e, b=b: e.tensor_tensor(
                out=rstd[:, b:b + 1], in0=rstd[:, b:b + 1], in1=cm05[:, 0:1], op=ALU.pow),
                [("rstd", b), "cm05"], [("rstd", b)])
            A("dve", lambda e, sl=sl, hbb=hbb, b=b: e.scalar_tensor_tensor(
                out=hbb, in0=xb[sl], scalar=rstd[:, b:b + 1], in1=wpre, op0=ALU.mult, op1=ALU.mult),
                [("xb", sl), ("rstd", b), "wpre"], [hk])
            q = b % 2
            for k in range(8):
                A("pe", lambda e, hbb=hbb, k=k, q=q: e.transpose(
                    out=psb[:, q, k * 128:(k + 1) * 128], in_=hbb[:, k * 128:(k + 1) * 128],
                    identity=ident_b), [hk, "identb"], [("q", q, k)])
            dstv = hT[:, :, b * 128:(b + 1) * 128]
            srcv = psb[:, q, :].rearrange("p (k t) -> p k t", k=8)
            if b % 2 == 0:
                A("act", lambda e, dstv=dstv, srcv=srcv: e.copy(out=dstv, in_=srcv), QK(q), [("hT", b)])
            else:
                A("dve", lambda e, dstv=dstv, srcv=srcv: e.tensor_copy(out=dstv, in_=srcv), QK(q), [("hT", b)])
        return slots

    HT4 = [("hT", b) for b in range(4)]

    def postnorm_residual(b, sl, xb, wpost, tmp, ss2, rstd2, scale, dst, ti, pbase=4):
        t0 = ti * 512
        for half in range(2):
            po = pbase + half
            A("act", lambda e, half=half, po=po: e.activation(
                out=tmp[half], in_=ps[:, po, :], func=AF.Square,
                accum_out=ss2[:, 2 * b + half:2 * b + half + 1]),
                PK(po), [("tmp", half), ("ss2", b, half)])
        A("dve", lambda e: e.tensor_tensor(
            out=rstd2[:, b:b + 1], in0=ss2[:, 2 * b:2 * b + 1], in1=ss2[:, 2 * b + 1:2 * b + 2],
            op=ALU.add), [("ss2", b, 0), ("ss2", b, 1)], [("rstd2", b)])
        inv = 1.0 / (scale * scale)
        A("dve", lambda e: e.tensor_scalar(
            out=rstd2[:, b:b + 1], in0=rstd2[:, b:b + 1], scalar1=inv / D, scalar2=inv * 1e-6,
            op0=ALU.mult, op1=ALU.add), [("rstd2", b)], [("rstd2", b)])
        A("pool", lambda e: e.tensor_tensor(
            out=rstd2[:, b:b + 1], in0=rstd2[:, b:b + 1], in1=cm05[:, 0:1], op=ALU.pow),
            [("rstd2", b), "cm05"], [("rstd2", b)])
        for half in range(2):
            po = pbase + half
            A("dve", lambda e, half=half, po=po: e.scalar_tensor_tensor(
                out=tmp[half], in0=ps[:, po, :], scalar=rstd2[:, b:b + 1],
                in1=wpost[:, half * 512:(half + 1) * 512], op0=ALU.mult, op1=ALU.mult),
                PK(po) + [("rstd2", b), "wpost"], [("tmp", half)])
            A("pool", lambda e, half=half: e.tensor_tensor(
                out=xb[sl][:, half * 512:(half + 1) * 512], in0=xb[sl][:, half * 512:(half + 1) * 512],
                in1=tmp[half], op=ALU.add), [("tmp", half), ("xb", sl)], [("xb", sl)])
        DMA("sp", dst[t0 + b * 128:t0 + (b + 1) * 128, :], xb[sl], r=[("xb", sl)], w=[("xd", ti, b)])

    def load_x(src, ti, xb, xi, nbuf):
        slots = []
        for b in range(4):
            sl = xi[0] % nbuf
            xi[0] += 1
            slots.append(sl)
            DMA("sp", xb[sl], src[ti * 512 + b * 128:ti * 512 + (b + 1) * 128, :],
                r=[("xd", ti, b)], w=[("xb", sl)])
        return slots

    def sigmoid_from(src, skeys, e_t, ekey, r_t, rkey, negbias=None, bkeys=()):
        if negbias is None:
            A("act", lambda e: e.activation(out=e_t, in_=src, func=AF.Exp, scale=-1.0), skeys, [ekey])
        else:
            A("act", lambda e: e.activation(out=e_t, in_=src, func=AF.Exp, scale=-1.0, bias=negbias),
              list(skeys) + list(bkeys), [ekey])
        A("act", lambda e: e.activation(out=e_t, in_=e_t, func=AF.Ln, bias=1.0), [ekey], [ekey])
        A("act", lambda e: e.activation(out=r_t, in_=e_t, func=AF.Exp, scale=-1.0), [ekey], [rkey])

    def ffn_phase(l, which, src, dst):
        ar.reset()
        S.barrier()
        w_in = W[which + "_w_in"]
        w_out = W[which + "_w_out"]
        win_sb = ar.alloc([128, 8, 2 * DFF], BF16, "win")
        wout_sb = ar.alloc([128, 22, D], BF16, "wout")
        wpre = ar.alloc([128, D], F32, "wpre")
        wpost = ar.alloc([128, D], F32, "wpost")
        xb = [ar.alloc([128, D], F32, "xb") for _ in range(4)]
        _hbsingle = True
        hb = [ar.alloc([128, D], BF16, "hb")] * 2
        hT = ar.alloc([128, 8, 512], BF16, "hT")
        inter = ar.alloc([128, 22, 512], BF16, "inter")
        sg = [ar.alloc([128, 512], BF16, "sg") for _ in range(2)]
        tmp = [ar.alloc([128, 512], F32, "tmp") for _ in range(2)]
        ss = ar.alloc([128, 8], F32, "ss")
        rstd = ar.alloc([128, 8], F32, "rstd")
        ss2 = ar.alloc([128, 8], F32, "ss2")
        rstd2 = ar.alloc([128, 4], F32, "rstd2")
        bcast_rows(wpre, W[which + "_norm_pre"][l], "wpre")
        bcast_rows(wpost, W[which + "_norm_post"][l], "wpost")
        for c0 in (0, 2 * 1408, 1408, 3 * 1408):
            for k in range(8):
                DMA("pool", win_sb[:, k, c0:c0 + 1408], w_in[l, k * 128:(k + 1) * 128, c0:c0 + 1408],
                    w=[("win", k, c0 // 1408)])
        for c in range(22):
            DMA("pool", wout_sb[:, c, :], w_out[l, c * 128:(c + 1) * 128, :], w=[("wout", c)])
        xi = [0]
        for ti in range(NT):
            slots = prenorm_hT(src, ti, xb, xi, 4, wpre, hb, hT, ss, rstd)
            for c in range(22):
                pa = c % 2
                pb = 2 + c % 2
                for k in range(8):
                    A("pe", lambda e, c=c, k=k, pa=pa: e.matmul(
                        ps[:, pa, :], lhsT=win_sb[:, k, c * 128:(c + 1) * 128], rhs=hT[:, k, :],
                        start=(k == 0), stop=(k == 7)),
                        [("win", k, (c * 128) // 1408), ("win", k, (c * 128 + 127) // 1408)] + HT4, PK(pa))
                for k in range(8):
                    A("pe", lambda e, c=c, k=k, pb=pb: e.matmul(
                        ps[:, pb, :], lhsT=win_sb[:, k, DFF + c * 128:DFF + (c + 1) * 128], rhs=hT[:, k, :],
                        start=(k == 0), stop=(k == 7)),
                        [("win", k, (DFF + c * 128) // 1408), ("win", k, (DFF + c * 128 + 127) // 1408)] + HT4, PK(pb))
                A("act", lambda e, c=c, pa=pa: e.activation(out=sg[c % 2], in_=ps[:, pa, :], func=AF.Silu),
                  PK(pa), [("sg", c % 2)])
                A("dve", lambda e, c=c, pb=pb: e.tensor_tensor(
                    out=inter[:, c, :], in0=ps[:, pb, :], in1=sg[c % 2], op=ALU.mult),
                    PK(pb) + [("sg", c % 2)], [("inter", c)])
            for b in range(4):
                pbase = 4 if b % 2 == 0 else 0
                for half in range(2):
                    po = pbase + half
                    for c in range(22):
                        A("pe", lambda e, c=c, b=b, half=half, po=po: e.matmul(
                            ps[:, po, :], lhsT=inter[:, c, b * 128:(b + 1) * 128],
                            rhs=wout_sb[:, c, half * 512:(half + 1) * 512],
                            start=(c == 0), stop=(c == 21)), [("inter", c), ("wout", c)], PK(po))
                postnorm_residual(b, slots[b], xb, wpost, tmp, ss2, rstd2, 0.5, dst, ti, pbase)

    def mixer_A(l, src):
        ar.reset()
        S.barrier()
        mw = W["mix_w_in"][l]
        wglu = ar.alloc([128, 8, 2048], BF16, "wglu")
        cwo = ar.alloc([128, 8, 1024], BF16, "cwo")
        dg = ar.alloc([128, 8, 31, 128], BF16, "dg31")
        wpre = ar.alloc([128, D], F32, "wpre")
        cols = ar.alloc([128, 512], F32, "cols")
        ncols = ar.alloc([128, 16], F32, "ncols")
        xb = [ar.alloc([128, D], F32, "xb") for _ in range(2)]
        hb = [ar.alloc([128, D], BF16, "hb")] * 2
        hT = ar.alloc([128, 8, 512], BF16, "hT")
        hg1 = ar.alloc([128, 8, 544], BF16, "hglu")
        c0f = ar.alloc([128, 8, 512], F32, "c0f")
        cbf = [ar.alloc([128, 512], BF16, "cbf")] * 2
        csq = [ar.alloc([128, 512], BF16, "csq")] * 2
        cT = ar.alloc([128, 8, 512], BF16, "cT")
        m2T = [ar.alloc([128, 8, 512], BF16, "m2T") for _ in range(1)]
        ft = [ar.alloc([128, 512], F32, "ft") for _ in range(5)]
        mu = ar.alloc([128, 512], F32, "mu")
        rsd = ar.alloc([128, 512], F32, "rsd")
        ss = ar.alloc([128, 8], F32, "ss")
        rstd = ar.alloc([128, 8], F32, "rstd")
        bcast_rows(wpre, W["mix_norm_pre"][l], "wpre")
        load_weight(wglu, mw[:, MW["glu_a"]:MW["glu_a"] + 2048], lambda k, ch: ("wglu", k, ch), 2048)
        load_weight(cwo, W["cnv_w_o"][l], lambda k, ch: ("cwo", k), 1024)
        ent = [W["cnv_pw1_b"][l].rearrange("(c p) -> c p", p=128),
               W["cnv_dw_b"][l].rearrange("(c p) -> c p", p=128),
               W["cnv_ln_g"][l].rearrange("(c p) -> c p", p=128),
               W["cnv_ln_b"][l].rearrange("(c p) -> c p", p=128),
               W["cnv_b_o"][l].rearrange("(c p) -> c p", p=128),
               W["cnv_dw_w"][l].rearrange("j (c p) -> (j c) p", p=128)]
        load_cols(cols, "cols", ent)
        PW, DWB, LNG, LNB, BO, DWW = 0, 16, 24, 32, 40, 48
        A("dve", lambda e: e.tensor_scalar(out=ncols, in0=cols[:, 0:16], scalar1=-1.0, scalar2=None, op0=ALU.mult),
          ["cols"], ["ncols"])
        for c in range(8):
            for j in range(31):
                A("dve" if (c * 31 + j) % 2 else "pool", lambda e, c=c, j=j: e.tensor_scalar(
                    out=dg[:, c, j, :], in0=ident_f, scalar1=cols[:, DWW + j * 8 + c:DWW + j * 8 + c + 1],
                    scalar2=None, op0=ALU.mult), ["cols", "identf"], [("dg", c)])
        A("pool", lambda e: e.memset(hg1[:, :, 0:32], 0.0), [], [("hgh", c) for c in range(8)])
        xi = [0]
        for ti in range(NT):
            cur = hg1
            slots = prenorm_hT(src, ti, xb, xi, 2, wpre, hb, hT, ss, rstd)
            for c in range(8):
                pa, pg = (0, 1) if c % 2 == 0 else (2, 3)
                for k in range(8):
                    A("pe", lambda e, c=c, k=k, pa=pa: e.matmul(ps[:, pa, :], lhsT=wglu[:, k, c * 128:(c + 1) * 128],
                                                         rhs=hT[:, k, :], start=(k == 0), stop=(k == 7)),
                      [("wglu", k, 0)] + HT4, PK(pa))
                for k in range(8):
                    A("pe", lambda e, c=c, k=k, pg=pg: e.matmul(ps[:, pg, :], lhsT=wglu[:, k, 1024 + c * 128:1024 + (c + 1) * 128],
                                                         rhs=hT[:, k, :], start=(k == 0), stop=(k == 7)),
                      [("wglu", k, 1)] + HT4, PK(pg))
                f0 = ft[c % 2]
                f1 = ft[2 + c % 2]
                sigmoid_from(ps[:, pg, :], PK(pg), f0, ("ft", c % 2), f1, ("ft", 2 + c % 2),
                             negbias=ncols[:, 8 + c:9 + c], bkeys=["ncols"])
                if ti > 0:
                    A("pool", lambda e, c=c, cur=cur: e.tensor_copy(out=cur[:, c, 0:32], in_=cur[:, c, 512:544]),
                      [("hgd", c)], [("hgh", c)])
                A("dve", lambda e, c=c, f1=f1, cur=cur, pa=pa: e.scalar_tensor_tensor(
                    out=cur[:, c, 32:544], in0=ps[:, pa, :], scalar=cols[:, PW + c:PW + c + 1], in1=f1,
                    op0=ALU.add, op1=ALU.mult), PK(pa) + ["cols", ("ft", 2 + c % 2)], [("hgd", c)])
            for c in range(8):
                pb = 2 + c % 2
                for j in range(31):
                    A("pe", lambda e, c=c, j=j, pb=pb, cur=cur: e.matmul(
                        ps[:, pb, :], lhsT=dg[:, c, j, :], rhs=cur[:, c, 2 + j:2 + j + 512],
                        start=(j == 0), stop=(j == 30)), [("dg", c), ("hgd", c), ("hgh", c)], PK(pb))
                A("act", lambda e, c=c, pb=pb: e.activation(out=c0f[:, c, :], in_=ps[:, pb, :], func=AF.Identity,
                                                            bias=cols[:, DWB + c:DWB + c + 1]),
                  PK(pb) + ["cols"], [("c0f", c)])
                A("dve", lambda e, c=c: e.tensor_copy(out=cbf[c % 2], in_=c0f[:, c, :]), [("c0f", c)], [("cbf", 0)])
                A("act", lambda e, c=c: e.activation(out=csq[c % 2], in_=c0f[:, c, :], func=AF.Square),
                  [("c0f", c)], [("csq", 0)])
                A("pe", lambda e, c=c: e.matmul(ps[:, 4, :], lhsT=ones_b, rhs=cbf[c % 2], start=(c == 0), stop=(c == 7)),
                  ["onesb", ("cbf", 0)], PK(4))
                A("pe", lambda e, c=c: e.matmul(ps[:, 5, :], lhsT=ones_b, rhs=csq[c % 2], start=(c == 0), stop=(c == 7)),
                  ["onesb", ("csq", 0)], PK(5))
            A("dve", lambda e: e.tensor_scalar(out=mu, in0=ps[:, 4, :], scalar1=1.0 / D, scalar2=None, op0=ALU.mult),
              PK(4), ["mu"])
            A("dve", lambda e: e.tensor_tensor(out=rsd, in0=mu, in1=mu, op=ALU.mult), ["mu"], ["rsd"])
            A("dve", lambda e: e.scalar_tensor_tensor(out=rsd, in0=ps[:, 5, :], scalar=1.0 / D, in1=rsd,
                                                      op0=ALU.mult, op1=ALU.subtract), PK(5) + ["rsd"], ["rsd"])
            A("dve", lambda e: e.tensor_scalar(out=rsd, in0=rsd, scalar1=1e-5, scalar2=None, op0=ALU.add), ["rsd"], ["rsd"])
            A("act", lambda e: e.activation(out=rsd, in_=rsd, func=AF.Ln), ["rsd"], ["rsd"])
            A("act", lambda e: e.activation(out=rsd, in_=rsd, func=AF.Exp, scale=-0.5), ["rsd"], ["rsd"])
            for c in range(8):
                f0 = ft[c % 2]
                f1 = ft[2 + c % 2]
                f2 = ft[4]
                k0, k1, k2 = ("ft", c % 2), ("ft", 2 + c % 2), ("ft", 4)
                A("dve", lambda e, c=c, f0=f0: e.tensor_tensor(out=f0, in0=c0f[:, c, :], in1=mu, op=ALU.subtract),
                  [("c0f", c), "mu"], [k0])
                A("dve", lambda e, f0=f0: e.tensor_tensor(out=f0, in0=f0, in1=rsd, op=ALU.mult), [k0, "rsd"], [k0])
                A("act", lambda e, c=c, f0=f0: e.activation(out=f0, in_=f0, func=AF.Identity,
                                                            scale=cols[:, LNG + c:LNG + c + 1],
                                                            bias=cols[:, LNB + c:LNB + c + 1]), [k0, "cols"], [k0])
                sigmoid_from(f0, [k0], f1, k1, f2, k2)
                A("dve", lambda e, c=c, f0=f0, f2=f2: e.tensor_tensor(out=cT[:, c, :], in0=f0, in1=f2, op=ALU.mult),
                  [k0, k2], [("cT", c)])
            mt = m2T[0]
            for fo in range(8):
                for c in range(8):
                    A("pe", lambda e, fo=fo, c=c: e.matmul(ps[:, fo % 2, :], lhsT=cwo[:, c, fo * 128:(fo + 1) * 128],
                                                           rhs=cT[:, c, :], start=(c == 0), stop=(c == 7)),
                      [("cwo", c), ("cT", c)], PK(fo % 2))
                A("act", lambda e, fo=fo, mt=mt: e.activation(out=mt[:, fo, :], in_=ps[:, fo % 2, :], func=AF.Identity,
                                                              bias=cols[:, BO + fo:BO + fo + 1]),
                  PK(fo % 2) + ["cols"], [("m2T", 0)])
            DMA("sp", m2s[ti], mt, r=[("m2T", 0)], w=[("m2s", ti)])

    def mixer_B(l, src):
        ar.reset()
        S.barrier()
        mw = W["mix_w_in"][l]
        wqkv = ar.alloc([128, 8, 3072], BF16, "wqkv")
        wba = ar.alloc([128, 8, 16], BF16, "wba")
        dg = ar.alloc([128, 24, 4, 128], BF16, "dg4")
        wpre = ar.alloc([128, D], F32, "wpre")
        cols = ar.alloc([128, 128], F32, "cols")
        xb = [ar.alloc([128, D], F32, "xb") for _ in range(2)]
        hb = [ar.alloc([128, D], BF16, "hb")] * 2
        hT = ar.alloc([128, 8, 512], BF16, "hT")
        ss = ar.alloc([128, 8], F32, "ss")
        rstd = ar.alloc([128, 8], F32, "rstd")
        qT = ar.alloc([128, 8, 512], BF16, "qT")
        qdT = ar.alloc([128, 8, 512], BF16, "qdT")
        kT = ar.alloc([128, 8, 512], BF16, "kT")
        vT = ar.alloc([128, 8, 512], BF16, "vT")
        ogT = [ar.alloc([128, 8, 512], BF16, "ogT")] * 2
        pch = ar.alloc([128, 24, 4], BF16, "pch")
        pc = [ar.alloc([128, 516], BF16, "pc") for _ in range(2)]
        ft = [ar.alloc([128, 512], F32, "ft") for _ in range(6)]
        ysq = [ar.alloc([128, 512], BF16, "ysq")] * 2
        ebc = [ar.alloc([128, 512], F32, "ebc")] * 2
        dms = [ar.alloc([128, 512], BF16, "dms") for _ in range(2)]
        dmi = [ar.alloc([128, 512], BF16, "dmi") for _ in range(2)]
        dtb = ar.alloc([128, 32], F32, "dtb")
        nega = ar.alloc([128, 32], F32, "nega")
        sm = {nm: ar.alloc([128, 32], F32, nm) for nm in
              ("et", "beta", "nbeta", "t", "g", "Gc", "nGc", "Gt", "eG", "kds", "dch", "bEG")}
        Sst = ar.alloc([128, 8, 128], F32, "Sst")
        Sbf = ar.alloc([128, 8, 128], BF16, "Sbf")
        NG = 8
        maskPB = ar.alloc([128, 7, 256], BF16, "maskPB")
        ident2 = ar.alloc([128, 256], BF16, "ident2")
        DMA("pool", maskPB.rearrange("p a b -> p (a b)"), C["c_mpb"], w=["maskPB"])
        DMA("pool", ident2, C["c_id2"], w=["ident2"])
        kbe = [ar.alloc([128, 128], BF16, "kbe") for _ in range(NG)]
        kdc = [ar.alloc([128, 128], BF16, "kdc") for _ in range(NG)]
        vb = [ar.alloc([128, 128], BF16, "vb") for _ in range(NG)]
        qkm = [ar.alloc([128, 128], BF16, "qkm") for _ in range(NG)]
        qkmT = [ar.alloc([128, 128], BF16, "qkmT") for _ in range(NG)]
        Wn = [ar.alloc([128, 256], BF16, "Wn") for _ in range(NG)]
        TU = [ar.alloc([128, 256], BF16, "TU") for _ in range(NG)]
        YY = [ar.alloc([128, 256], BF16, "YY") for _ in range(NG)]
        uw = [ar.alloc([128, 256], BF16, "uw") for _ in range(NG)]
        vn = [ar.alloc([128, 128], BF16, "vn") for _ in range(NG)]
        normw = ar.alloc([128, 1], F32, "normw")
        gbt = [ar.alloc([128, 128], F32, "gbt")] * 2

        bcast_rows(wpre, W["mix_norm_pre"][l], "wpre")
        load_weight(wqkv, mw[:, 0:3072], lambda k, ch: ("wqkv", k, ch), 3072)
        load_weight(wba, mw[:, MW["ba"]:MW["ba"] + 16], lambda k, ch: ("wba", k), 16)
        load_cols(cols, "cols", [W["gdn_conv_w"][l].rearrange("j (c p) -> (j c) p", p=128)])
        DMA("sp", normw, W["gdn_norm_w"][l].rearrange("(p o) -> p o", o=1), w=["normw"])
        for j in range(4):
            DMA("sp", dtb[:, j * 8:(j + 1) * 8], W["gdn_dt_bias"][l].partition_broadcast(128), w=["dtb"])
            DMA("sp", nega[:, j * 8:(j + 1) * 8], W["gdn_a_log"][l].partition_broadcast(128), w=["nega"])
        A("act", lambda e: e.activation(out=nega, in_=nega, func=AF.Exp), ["nega"], ["nega"])
        A("dve", lambda e: e.tensor_scalar(out=nega, in0=nega, scalar1=-1.0, scalar2=None, op0=ALU.mult), ["nega"], ["nega"])
        for ci in range(24):
            for j in range(4):
                A("dve" if (ci + j) % 2 else "pool", lambda e, ci=ci, j=j: e.tensor_scalar(
                    out=dg[:, ci, j, :], in0=ident_f, scalar1=cols[:, j * 24 + ci:j * 24 + ci + 1],
                    scalar2=None, op0=ALU.mult), ["cols", "identf"], [("dg", ci)])
        A("pool", lambda e: e.memset(pch, 0.0), [], [("pch", ci) for ci in range(24)])
        A("pool", lambda e: e.memset(Sst, 0.0), [], [("S", h) for h in range(8)])
        A("pool", lambda e: e.memset(Sbf, 0.0), [], [("Sb", h) for h in range(8)])
        xi = [0]
        rr = [0]
        for ti in range(NT):
            slots = prenorm_hT(src, ti, xb, xi, 2, wpre, hb, hT, ss, rstd)
            og = ogT[ti % 2]
            for j in range(4):
                for k in range(8):
                    A("pe", lambda e, j=j, k=k: e.matmul(ps[:, 0, j * 16:(j + 1) * 16],
                                                         lhsT=hT[:, k, j * 128:(j + 1) * 128], rhs=wba[:, k, :],
                                                         start=(k == 0), stop=(k == 7)), [("wba", k)] + HT4, PK(0))
            lg = ps[:, 0, 0:64].rearrange("p (j t) -> p j t", j=4)
            v3 = lambda t: t.rearrange("p (j h) -> p j h", j=4)
            A("act", lambda e: e.activation(out=v3(sm["et"]), in_=lg[:, :, 0:8], func=AF.Exp, scale=-1.0), PK(0), ["et"])
            A("dve", lambda e: e.tensor_scalar(out=sm["et"], in0=sm["et"], scalar1=1.0, scalar2=None, op0=ALU.add), ["et"], ["et"])
            A("dve", lambda e: e.reciprocal(out=sm["beta"], in_=sm["et"]), ["et"], ["beta"])
            A("dve", lambda e: e.tensor_scalar(out=sm["nbeta"], in0=sm["beta"], scalar1=-1.0, scalar2=None, op0=ALU.mult),
              ["beta"], ["nbeta"])
            A("dve", lambda e: e.tensor_tensor(out=v3(sm["t"]), in0=lg[:, :, 8:16], in1=v3(dtb), op=ALU.add),
              PK(0) + ["dtb"], ["t"])
            A("act", lambda e: e.activation(out=sm["t"], in_=sm["t"], func=AF.Exp), ["t"], ["t"])
            A("act", lambda e: e.activation(out=sm["t"], in_=sm["t"], func=AF.Ln, bias=1.0), ["t"], ["t"])
            A("dve", lambda e: e.tensor_tensor(out=sm["g"], in0=sm["t"], in1=nega, op=ALU.mult), ["t", "nega"], ["g"])
            A("pe", lambda e: e.matmul(ps[:, 1, 0:32], lhsT=ltri_f, rhs=sm["g"], start=True, stop=True), ["ltri", "g"], PK(1))
            A("pe", lambda e: e.matmul(ps[:, 1, 32:64], lhsT=ones_f, rhs=sm["g"], start=True, stop=True), ["onesf", "g"], PK(1))
            A("dve", lambda e: e.tensor_copy(out=sm["Gc"], in_=ps[:, 1, 0:32]), PK(1), ["Gc"])
            A("dve", lambda e: e.tensor_scalar(out=sm["nGc"], in0=sm["Gc"], scalar1=-1.0, scalar2=None, op0=ALU.mult), ["Gc"], ["nGc"])
            A("dve", lambda e: e.tensor_tensor(out=sm["Gt"], in0=ps[:, 1, 32:64], in1=sm["Gc"], op=ALU.subtract),
              PK(1) + ["Gc"], ["Gt"])
            A("act", lambda e: e.activation(out=sm["kds"], in_=sm["Gt"], func=AF.Exp), ["Gt"], ["kds"])
            A("act", lambda e: e.activation(out=sm["dch"], in_=ps[:, 1, 32:64], func=AF.Exp), PK(1), ["dch"])
            A("act", lambda e: e.activation(out=sm["eG"], in_=sm["Gc"], func=AF.Exp), ["Gc"], ["eG"])
            A("dve", lambda e: e.tensor_tensor(out=sm["bEG"], in0=sm["eG"], in1=sm["beta"], op=ALU.mult),
              ["eG", "beta"], ["bEG"])
            def _s1_slice0(h, hp):
                for j in range(4):
                    gt = gbt[j % 2]
                    A("dve", lambda e, j=j, h=h, gt=gt: e.tensor_scalar(
                        out=gt, in0=ones_f, scalar1=sm["g"][:, j * 8 + h:j * 8 + h + 1], scalar2=None, op0=ALU.mult),
                        ["g", "onesf"], [("gbt", 0)])
                    A("pe", lambda e, j=j, gt=gt: e.matmul(
                        ps[:, 2, j * 128:(j + 1) * 128], lhsT=gt, rhs=ltri_f,
                        start=True, stop=True), [("gbt", 0), "ltri"], [("p", 2, j)])
                A("act", lambda e, hp=hp: e.activation(out=ebc[hp], in_=ps[:, 2, :], func=AF.Exp), PK(2), [("ebc", 0)])
                fd = ft[5]
                for j in range(4):
                    A("act", lambda e, j=j, h=h, fd=fd: e.activation(
                        out=fd[:, j * 128:(j + 1) * 128], in_=ps[:, 2, j * 128:(j + 1) * 128], func=AF.Relu,
                        bias=sm["nGc"][:, j * 8 + h:j * 8 + h + 1]), [("p", 2, j), "nGc"], [("ft", 5)])
                A("act", lambda e, fd=fd: e.activation(out=fd, in_=fd, func=AF.Exp, scale=-1.0), [("ft", 5)], [("ft", 5)])
                A("dve", lambda e, fd=fd, hp=hp: e.tensor_tensor(out=dms[hp], in0=fd, in1=mask_s, op=ALU.mult),
                  [("ft", 5), "masks"], [("dms", hp)])
                A("pool", lambda e, fd=fd, hp=hp: e.tensor_tensor(out=dmi[hp], in0=fd, in1=mask_i, op=ALU.mult),
                  [("ft", 5), "maski"], [("dmi", hp)])
            def _s1_qkv(h, hp, i):
                ci = i * 8 + h
                pcs = pc[(h * 3 + i) % 2]
                pk = ("pc", (h * 3 + i) % 2)
                for k in range(8):
                    A("pe", lambda e, ci=ci, k=k: e.matmul(ps[:, 0, :], lhsT=wqkv[:, k, ci * 128:(ci + 1) * 128],
                                                           rhs=hT[:, k, :], start=(k == 0), stop=(k == 7)),
                      [("wqkv", k, ci // 8)] + HT4, PK(0))
                A("act", lambda e, ci=ci, pcs=pcs: e.copy(out=pcs[:, 0:4], in_=pch[:, ci, :]), [("pch", ci)], [pk])
                A("act", lambda e, pcs=pcs: e.copy(out=pcs[:, 4:516], in_=ps[:, 0, :]), PK(0), [pk])
                A("act", lambda e, ci=ci, pcs=pcs: e.copy(out=pch[:, ci, :], in_=pcs[:, 512:516]), [pk], [("pch", ci)])
                for j in range(4):
                    A("pe", lambda e, ci=ci, j=j, pcs=pcs: e.matmul(
                        ps[:, 1, :], lhsT=dg[:, ci, j, :], rhs=pcs[:, 1 + j:1 + j + 512],
                        start=(j == 0), stop=(j == 3)), [("dg", ci), pk], PK(1))
                f0, f1, f2 = ft[0 + i % 2], ft[2 + i % 2], ft[4]
                k0, k1, k2 = ("ft", i % 2), ("ft", 2 + i % 2), ("ft", 4)
                sigmoid_from(ps[:, 1, :], PK(1), f0, k0, f1, k1)
                if i == 2:
                    A("dve", lambda e, f1=f1, h=h: e.tensor_tensor(out=vT[:, h, :], in0=ps[:, 1, :], in1=f1, op=ALU.mult),
                      PK(1) + [k1], [("vT", h)])
                    return
                A("dve", lambda e, f1=f1, f2=f2: e.tensor_tensor(out=f2, in0=ps[:, 1, :], in1=f1, op=ALU.mult),
                  PK(1) + [k1], [k2])
                yq = ysq[i % 2]
                A("act", lambda e, f2=f2, yq=yq: e.activation(out=yq, in_=f2, func=AF.Square), [k2], [("ysq", 0)])
                A("pe", lambda e, yq=yq: e.matmul(ps[:, 3, :], lhsT=ones_b, rhs=yq, start=True, stop=True),
                  ["onesb", ("ysq", 0)], PK(3))
                A("dve", lambda e, f0=f0: e.tensor_scalar(out=f0, in0=ps[:, 3, :], scalar1=1e-6, scalar2=None, op0=ALU.add),
                  PK(3), [k0])
                A("act", lambda e, f0=f0: e.activation(out=f0, in_=f0, func=AF.Ln), [k0], [k0])
                A("act", lambda e, f0=f0: e.activation(out=f0, in_=f0, func=AF.Exp, scale=-0.5), [k0], [k0])
                if i == 1:
                    A("dve", lambda e, f0=f0, f2=f2, h=h: e.tensor_tensor(out=kT[:, h, :], in0=f2, in1=f0, op=ALU.mult),
                      [k0, k2], [("kT", h)])
                else:
                    A("dve", lambda e, f0=f0, f2=f2: e.scalar_tensor_tensor(
                        out=f2, in0=f2, scalar=float(128 ** -0.5), in1=f0, op0=ALU.mult, op1=ALU.mult), [k0, k2], [k2])
                    A("act", lambda e, f2=f2, h=h: e.copy(out=qT[:, h, :], in_=f2), [k2], [("qT", h)])
                    A("dve", lambda e, f2=f2, h=h, hp=hp: e.tensor_tensor(out=qdT[:, h, :], in0=f2, in1=ebc[hp], op=ALU.mult),
                      [k2, ("ebc", 0)], [("qdT", h)])
            def _chunks_pair(hA):
                G = range(8)

                def hd(g):
                    return hA + g // 4

                def col(nm, g):
                    c = (g % 4) * 8 + hd(g)
                    return sm[nm][:, c:c + 1]

                def jb(g):
                    return slice((g % 4) * 128, (g % 4 + 1) * 128)

                def pq(g, half):
                    o = (g % 2) * 256 + half * 128
                    return ps[:, g // 2, o:o + 128]

                def pq2(g):
                    o = (g % 2) * 256
                    return ps[:, g // 2, o:o + 256]

                def PKg(g, half=None):
                    return [("p", g // 2, 0)]

                def qs(g, s_):
                    i_ = g * 2 + s_
                    return psb[:, i_ // 8, (i_ % 8) * 128:(i_ % 8) * 128 + 128]

                def QKg(g, s_):
                    return [("q", (g * 2 + s_) // 8, 0)]
                for g in G:
                    h = hd(g)
                    A("pe", lambda e, g=g, h=h: e.transpose(out=qs(g, 0), in_=kT[:, h, jb(g)], identity=ident_b),
                      [("kT", h), "identb"], QKg(g, 0))
                    A("pe", lambda e, g=g, h=h: e.transpose(out=qs(g, 1), in_=vT[:, h, jb(g)], identity=ident_b),
                      [("vT", h), "identb"], QKg(g, 1))
                    A("pe", lambda e, g=g, h=h: e.matmul(pq(g, 0), lhsT=kT[:, h, jb(g)], rhs=kT[:, h, jb(g)],
                                                         start=True, stop=True), [("kT", h)], PKg(g, 0))
                    A("pe", lambda e, g=g, h=h: e.matmul(pq(g, 1), lhsT=qT[:, h, jb(g)], rhs=kT[:, h, jb(g)],
                                                         start=True, stop=True), [("qT", h), ("kT", h)], PKg(g, 1))
                for g in G:
                    hp = hd(g) % 2
                    A("act", lambda e, g=g, c=col("bEG", g): e.activation(out=kbe[g], in_=qs(g, 0), func=AF.Identity, scale=c),
                      QKg(g, 0) + ["bEG"], [("kbe", g)])
                    A("act", lambda e, g=g, c=col("kds", g): e.activation(out=kdc[g], in_=qs(g, 0), func=AF.Identity, scale=c),
                      QKg(g, 0) + ["kds"], [("kdc", g)])
                    A("act", lambda e, g=g, c=col("beta", g): e.activation(out=vb[g], in_=qs(g, 1), func=AF.Identity, scale=c),
                      QKg(g, 1) + ["beta"], [("vb", g)])
                    A("dve", lambda e, g=g, hp=hp, c=col("nbeta", g): e.scalar_tensor_tensor(
                        out=Wn[g][:, 0:128], in0=pq(g, 0), scalar=c, in1=dms[hp][:, jb(g)], op0=ALU.mult, op1=ALU.mult),
                        PKg(g, 0) + ["nbeta", ("dms", hp)], [("Wn", g)])
                    A("dve", lambda e, g=g, hp=hp: e.tensor_tensor(out=qkm[g], in0=pq(g, 1), in1=dmi[hp][:, jb(g)], op=ALU.mult),
                      PKg(g, 1) + [("dmi", hp)], [("qkm", g)])
                for g in G:
                    A("pe", lambda e, g=g: e.transpose(out=qs(g, 0), in_=Wn[g][:, 0:128], identity=ident_b),
                      [("Wn", g), "identb"], QKg(g, 0))
                    A("pe", lambda e, g=g: e.transpose(out=qs(g, 1), in_=qkm[g], identity=ident_b),
                      [("qkm", g), "identb"], QKg(g, 1))
                for g in G:
                    A("act", lambda e, g=g: e.copy(out=Wn[g][:, 128:256], in_=qs(g, 0)), QKg(g, 0), [("Wn", g)])
                    A("act", lambda e, g=g: e.copy(out=qkmT[g], in_=qs(g, 1)), QKg(g, 1), [("qkmT", g)])
                for g in G:
                    A("dve", lambda e, g=g: e.tensor_tensor(out=YY[g], in0=Wn[g], in1=maskPB[:, 0, :], op=ALU.mult),
                      [("Wn", g), "maskPB"], [("YY", g)])
                    A("dve", lambda e, g=g: e.tensor_tensor(out=TU[g], in0=YY[g], in1=ident2, op=ALU.add),
                      [("YY", g), "ident2"], [("TU", g)])

                def _level(lvl):
                    last = lvl == 6
                    for g in G:
                        if not last:
                            A("pe", lambda e, g=g: e.matmul(pq(g, 0), lhsT=Wn[g][:, 128:256], rhs=TU[g][:, 0:128],
                                                            start=True, stop=True), [("Wn", g), ("TU", g)], PKg(g, 0))
                        A("pe", lambda e, g=g: e.matmul(pq(g, 1), lhsT=Wn[g][:, 0:128], rhs=TU[g][:, 128:256],
                                                        start=True, stop=True), [("Wn", g), ("TU", g)], PKg(g, 1))
                    for g in G:
                        if last:
                            A("dve", lambda e, g=g: e.tensor_tensor(out=YY[g][:, 128:256], in0=pq(g, 1),
                                                                    in1=maskPB[:, lvl, 128:256], op=ALU.mult),
                              PKg(g, 1) + ["maskPB"], [("YY", g)])
                        else:
                            A("dve", lambda e, g=g: e.tensor_tensor(out=YY[g], in0=pq2(g), in1=maskPB[:, lvl, :], op=ALU.mult),
                              PKg(g) + ["maskPB"], [("YY", g)])
                    for g in G:
                        if not last:
                            A("pe", lambda e, g=g: e.matmul(pq(g, 0), lhsT=ident_b, rhs=TU[g][:, 0:128], start=True, stop=False),
                              [("TU", g), "identb"], PKg(g, 0))
                            A("pe", lambda e, g=g: e.matmul(pq(g, 0), lhsT=TU[g][:, 128:256], rhs=YY[g][:, 0:128], start=False, stop=True),
                              [("TU", g), ("YY", g)], PKg(g, 0))
                        A("pe", lambda e, g=g: e.matmul(pq(g, 1), lhsT=ident_b, rhs=TU[g][:, 128:256], start=True, stop=False),
                          [("TU", g), "identb"], PKg(g, 1))
                        A("pe", lambda e, g=g: e.matmul(pq(g, 1), lhsT=TU[g][:, 0:128], rhs=YY[g][:, 128:256], start=False, stop=True),
                          [("TU", g), ("YY", g)], PKg(g, 1))
                    for g in G:
                        if last:
                            A("act", lambda e, g=g: e.copy(out=TU[g][:, 128:256], in_=pq(g, 1)), PKg(g, 1), [("TU", g)])
                        else:
                            A("act", lambda e, g=g: e.copy(out=TU[g], in_=pq2(g)), PKg(g), [("TU", g)])
                for lvl_ in range(1, 7):
                    _level(lvl_)
                for g in G:
                    A("pe", lambda e, g=g: e.matmul(pq(g, 0), lhsT=TU[g][:, 128:256], rhs=vb[g], start=True, stop=True),
                      [("TU", g), ("vb", g)], PKg(g, 0))
                    A("pe", lambda e, g=g: e.matmul(pq(g, 1), lhsT=kbe[g], rhs=TU[g][:, 128:256], start=True, stop=True),
                      [("TU", g), ("kbe", g)], PKg(g, 1))
                for g in G:
                    A("act", lambda e, g=g: e.copy(out=uw[g], in_=pq2(g)), PKg(g), [("uw", g)])

            def _rec(h, hh, j):
                g = hh * 4 + j
                jb = slice(j * 128, (j + 1) * 128)
                A("pe", lambda e: e.matmul(ps[:, hh, 0:128], lhsT=uw[g][:, 128:256], rhs=Sbf[:, h, :], start=True, stop=True),
                  [("uw", g), ("Sb", h)], [("p", hh, 0)])
                A("dve", lambda e: e.tensor_tensor(out=vn[g], in0=uw[g][:, 0:128], in1=ps[:, hh, 0:128], op=ALU.subtract),
                  [("uw", g), ("p", hh, 0)], [("vn", g)])
                A("pe", lambda e: e.matmul(ps[:, 4 + hh, jb], lhsT=Sbf[:, h, :], rhs=qdT[:, h, jb], start=True, stop=False),
                  [("Sb", h), ("qdT", h)], [("p", 4 + hh, j)])
                A("pe", lambda e: e.matmul(ps[:, 4 + hh, jb], lhsT=vn[g], rhs=qkmT[g], start=False, stop=True),
                  [("vn", g), ("qkmT", g)], [("p", 4 + hh, j)])
                A("pe", lambda e: e.matmul(ps[:, hh, 128:256], lhsT=kdc[g], rhs=vn[g], start=True, stop=True),
                  [("kdc", g), ("vn", g)], [("p", hh, 1)])
                A("dve", lambda e, c=sm["dch"][:, j * 8 + h:j * 8 + h + 1]: e.scalar_tensor_tensor(
                    out=Sst[:, h, :], in0=Sst[:, h, :], scalar=c, in1=ps[:, hh, 128:256], op0=ALU.mult, op1=ALU.add),
                    [("S", h), "dch", ("p", hh, 1)], [("S", h)])
                A("dve", lambda e: e.tensor_copy(out=Sbf[:, h, :], in_=Sst[:, h, :]), [("S", h)], [("Sb", h)])

            def _headout(h, hh):
                yq = ysq[0]
                f0 = ft[hh]
                po, pn = 4 + hh, 2 + hh
                A("act", lambda e: e.activation(out=yq, in_=ps[:, po, :], func=AF.Square), PK(po), [("ysq", 0)])
                A("pe", lambda e: e.matmul(ps[:, pn, :], lhsT=ones_b, rhs=yq, start=True, stop=True),
                  ["onesb", ("ysq", 0)], PK(pn))
                A("dve", lambda e: e.tensor_scalar(out=f0, in0=ps[:, pn, :], scalar1=1.0 / 128, scalar2=1e-6,
                                                   op0=ALU.mult, op1=ALU.add), PK(pn), [("ft", hh)])
                A("act", lambda e: e.activation(out=f0, in_=f0, func=AF.Ln), [("ft", hh)], [("ft", hh)])
                A("act", lambda e: e.activation(out=f0, in_=f0, func=AF.Exp, scale=-0.5), [("ft", hh)], [("ft", hh)])
                A("dve", lambda e: e.scalar_tensor_tensor(
                    out=og[:, h, :], in0=ps[:, po, :], scalar=normw[:, 0:1], in1=f0, op0=ALU.mult, op1=ALU.mult),
                    PK(po) + ["normw", ("ft", hh)], [("og", 0)])
            for hA in range(0, 8, 2):
                for hh in range(2):
                    _s1_slice0(hA + hh, (hA + hh) % 2)
                    for i_ in range(3):
                        _s1_qkv(hA + hh, (hA + hh) % 2, i_)
                _chunks_pair(hA)
                for j in range(4):
                    for hh in range(2):
                        _rec(hA + hh, hh, j)
                for hh in range(2):
                    _headout(hA + hh, hh)
            DMA("sp", ogs[ti], og, r=[("og", 0)], w=[("ogs", ti)])

    def mixer_C(l, src, dst):
        ar.reset()
        S.barrier()
        mw = W["mix_w_in"][l]
        wz = ar.alloc([128, 8, 1024], BF16, "wz")
        wga = ar.alloc([128, 8, 1024], BF16, "wga")
        wgb = ar.alloc([128, 8, 1024], BF16, "wgb")
        gwo = ar.alloc([128, 8, 1024], BF16, "gwo")
        wmo = ar.alloc([128, 8, 1024], BF16, "wmo")
        wpre = ar.alloc([128, D], F32, "wpre")
        wpost = ar.alloc([128, D], F32, "wpost")
        xb = [ar.alloc([128, D], F32, "xb") for _ in range(6)]
        hb = [ar.alloc([128, D], BF16, "hb") for _ in range(2)]
        hT = ar.alloc([128, 8, 512], BF16, "hT")
        ogn = [ar.alloc([128, 8, 512], BF16, "ogn") for _ in range(2)]
        m2T = [ar.alloc([128, 8, 512], BF16, "m2T") for _ in range(2)]
        ogT = ar.alloc([128, 8, 512], BF16, "ogT")
        mT = ar.alloc([128, 8, 512], BF16, "mT")
        ft = [ar.alloc([128, 512], F32, "ft") for _ in range(6)]
        tmp = [ar.alloc([128, 512], F32, "tmp") for _ in range(2)]
        ss = ar.alloc([128, 8], F32, "ss")
        rstd = ar.alloc([128, 8], F32, "rstd")
        ss2 = ar.alloc([128, 8], F32, "ss2")
        rstd2 = ar.alloc([128, 4], F32, "rstd2")
        bcast_rows(wpre, W["mix_norm_pre"][l], "wpre")
        bcast_rows(wpost, W["mix_norm_post"][l], "wpost")
        load_weight(wz, mw[:, MW["z"]:MW["z"] + 1024], lambda k, ch: ("wz", k), 1024)
        load_weight(wga, mw[:, MW["ga"]:MW["ga"] + 1024], lambda k, ch: ("wga", k), 1024)
        load_weight(wgb, mw[:, MW["gb"]:MW["gb"] + 1024], lambda k, ch: ("wgb", k), 1024)
        load_weight(gwo, W["gdn_w_o"][l], lambda k, ch: ("gwo", k), 1024)
        load_weight(wmo, W["mix_w_out"][l], lambda k, ch: ("wmo", k), 1024)
        xi = [0]
        for ti in range(NT):
            on = ogn[ti % 2]
            mt = m2T[ti % 2]
            DMA("sp", on, ogs[ti], r=[("ogs", ti)], w=[("ogn", ti % 2)])
            DMA("sp", mt, m2s[ti], r=[("m2s", ti)], w=[("m2T", ti % 2)])
            slots = prenorm_hT(src, ti, xb, xi, 6, wpre, hb, hT, ss, rstd)
            for h in range(8):
                for k in range(8):
                    A("pe", lambda e, h=h, k=k: e.matmul(ps[:, h % 2, :], lhsT=wz[:, k, h * 128:(h + 1) * 128], rhs=hT[:, k, :],
                                                         start=(k == 0), stop=(k == 7)), [("wz", k)] + HT4, PK(h % 2))
                f0, f1 = ft[h % 2], ft[2 + h % 2]
                sigmoid_from(ps[:, h % 2, :], PK(h % 2), f0, ("ft", h % 2), f1, ("ft", 2 + h % 2))
                A("dve", lambda e, h=h, f1=f1: e.tensor_tensor(out=f1, in0=ps[:, h % 2, :], in1=f1, op=ALU.mult),
                  PK(h % 2) + [("ft", 2 + h % 2)], [("ft", 2 + h % 2)])
                A("dve", lambda e, h=h, f1=f1, on=on: e.tensor_tensor(out=ogT[:, h, :], in0=on[:, h, :], in1=f1, op=ALU.mult),
                  [("ogn", ti % 2), ("ft", 2 + h % 2)], [("ogT", h)])
            for fo in range(8):
                pya, pga = (2, 3) if fo % 2 == 0 else (0, 1)
                for h in range(8):
                    A("pe", lambda e, fo=fo, h=h, pya=pya: e.matmul(ps[:, pya, :], lhsT=gwo[:, h, fo * 128:(fo + 1) * 128], rhs=ogT[:, h, :],
                                                           start=(h == 0), stop=(h == 7)), [("gwo", h), ("ogT", h)], PK(pya))
                for k in range(8):
                    A("pe", lambda e, fo=fo, k=k, pga=pga: e.matmul(ps[:, pga, :], lhsT=wga[:, k, fo * 128:(fo + 1) * 128], rhs=hT[:, k, :],
                                                           start=(k == 0), stop=(k == 7)), [("wga", k)] + HT4, PK(pga))
                f0, f1, f2 = ft[fo % 2], ft[2 + fo % 2], ft[4 + fo % 2]
                sigmoid_from(ps[:, pga, :], PK(pga), f0, ("ft", fo % 2), f1, ("ft", 2 + fo % 2))
                A("dve", lambda e, f1=f1, f2=f2, pya=pya: e.tensor_tensor(out=f2, in0=ps[:, pya, :], in1=f1, op=ALU.mult),
                  PK(pya) + [("ft", 2 + fo % 2)], [("ft", 4 + fo % 2)])
                for k in range(8):
                    A("pe", lambda e, fo=fo, k=k, pga=pga: e.matmul(ps[:, pga, :], lhsT=wgb[:, k, fo * 128:(fo + 1) * 128], rhs=hT[:, k, :],
                                                           start=(k == 0), stop=(k == 7)), [("wgb", k)] + HT4, PK(pga))
                sigmoid_from(ps[:, pga, :], PK(pga), f0, ("ft", fo % 2), f1, ("ft", 2 + fo % 2))
                A("dve", lambda e, fo=fo, f1=f1, mt=mt: e.tensor_tensor(out=f1, in0=f1, in1=mt[:, fo, :], op=ALU.mult),
                  [("ft", 2 + fo % 2), ("m2T", ti % 2)], [("ft", 2 + fo % 2)])
                A("dve", lambda e, fo=fo, f1=f1, f2=f2: e.tensor_tensor(out=mT[:, fo, :], in0=f2, in1=f1, op=ALU.add),
                  [("ft", 4 + fo % 2), ("ft", 2 + fo % 2)], [("mT", fo)])
            for b in range(4):
                pbase = 4 if b % 2 == 0 else 0
                for half in range(2):
                    po = pbase + half
                    for fo in range(8):
                        A("pe", lambda e, fo=fo, b=b, half=half, po=po: e.matmul(
                            ps[:, po, :], lhsT=mT[:, fo, b * 128:(b + 1) * 128], rhs=wmo[:, fo, half * 512:(half + 1) * 512],
                            start=(fo == 0), stop=(fo == 7)), [("mT", fo), ("wmo", fo)], PK(po))
                postnorm_residual(b, slots[b], xb, wpost, tmp, ss2, rstd2, 1.0, dst, ti, pbase)

    cur = x_in
    for l in range(depth):
        if phases is None or "ffn1" in phases:
            ffn_phase(l, "ffn1", cur, y_out)
            cur = y_out
        if phases is None or "mix" in phases or "mixA" in phases:
            mixer_A(l, cur)
        if phases is None or "mix" in phases or "mixB" in phases:
            mixer_B(l, cur)
        if phases is None or "mix" in phases or "mixC" in phases:
            mixer_C(l, cur, y_out)
            cur = y_out
        if phases is None or "ffn2" in phases:
            ffn_phase(l, "ffn2", cur, y_out)
            cur = y_out

    S.barrier()
    S.add("sp", lambda e: e.nop())
    stack = ExitStack()
    S.emit_all(stack)
    stack.close()
    return nc


WEIGHT_SHAPES = {
    "ffn1_norm_pre": (D,), "ffn1_norm_post": (D,), "ffn1_w_in": (D, 2 * DFF), "ffn1_w_out": (DFF, D),
    "mix_norm_pre": (D,), "mix_norm_post": (D,), "mix_w_in": (D, PIN),
    "gdn_conv_w": (4, 3072), "gdn_a_log": (8,), "gdn_dt_bias": (8,), "gdn_norm_w": (128,), "gdn_w_o": (D, D),
    "cnv_pw1_b": (2048,), "cnv_dw_w": (31, D), "cnv_dw_b": (D,), "cnv_ln_g": (D,), "cnv_ln_b": (D,),
    "cnv_w_o": (D, D), "cnv_b_o": (D,), "mix_w_out": (D, D),
    "ffn2_norm_pre": (D,), "ffn2_norm_post": (D,), "ffn2_w_in": (D, 2 * DFF), "ffn2_w_out": (DFF, D),
}
CONST_SHAPES = {"c_ident": [128, 128], "c_ltri": [128, 128], "c_ones": [128, 128],
                "c_masks": [128, 512], "c_maski": [128, 512], "c_mpb": [128, 1792], "c_id2": [128, 256]}


def consts():
    i = np.arange(128)
    ltri = (i[:, None] <= i[None, :]).astype(np.float32)
    ms = (i[None, :] < i[:, None]).astype(np.float32)
    mi = (i[None, :] <= i[:, None]).astype(np.float32)
    mp = np.zeros((128, 7, 128), np.float32)
    for lv in range(7):
        blk = i // (1 << lv)
        mp[:, lv, :] = ((blk[:, None] // 2 == blk[None, :] // 2) & (blk[:, None] % 2 == 1)
                        & (blk[None, :] % 2 == 0)).astype(np.float32)
    mb = np.ascontiguousarray(mp.transpose(2, 1, 0))
    mpb = np.concatenate([mp, mb], axis=2)
    return {"c_mpb": np.ascontiguousarray(mpb).reshape(128, 1792),
            "c_id2": np.tile(np.eye(128, dtype=np.float32), (1, 2)),
            "c_ident": np.eye(128, dtype=np.float32), "c_ltri": ltri,
            "c_ones": np.ones((128, 128), np.float32),
            "c_masks": np.tile(ms, (1, 4)), "c_maski": np.tile(mi, (1, 4))}


def kernel(**inputs):
    x = np.ascontiguousarray(inputs["x"], dtype=np.float32)
    B, T, _ = x.shape
    nc = build_program(T, DEPTH)
    shared = {k: np.ascontiguousarray(inputs[k], dtype=np.float32) for k in WEIGHT_SHAPES}
    shared.update(consts())
    in_maps = []
    for b in range(B):
        m = dict(shared)
        m["x"] = x[b]
        in_maps.append(m)
    res = run_bass_kernel_spmd(nc, in_maps, core_ids=list(range(B)))
    return np.stack([np.asarray(r["y"]).reshape(T, D) for r in res.results], axis=0).astype(np.float32)
```
